# Optimizing a Trainium2 kernel written in Bass

```python
import jax, jax.numpy as jnp
from jax import lax
import numpy as np

D_MODEL = 1024
BATCH = 2
SEQ = 8192
DEPTH = 1
DEC_BATCH = 32
DEC_SEQ = 64
PAST_LEN = 4096

CHUNK = 64
CONV_WIDTH = 31
CONV_BUF = CONV_WIDTH - 1
E_CONV = D_MODEL
N_HEADS = 4
HEAD_DIM = 256
E_MLSTM = N_HEADS * HEAD_DIM
D_FF = 4 * D_MODEL
D_IN = 2 * E_CONV + 4 * E_MLSTM + 2 * N_HEADS + 2 * D_MODEL
ALPHA = (2.0 * DEPTH) ** 0.25
BETA = (8.0 * DEPTH) ** -0.25
LN_EPS = 1e-5

kernel_name = "hybrid_conformer_mlstm_stream_step"


def layer_norm(x, g, b):
    xf = x.astype(jnp.float32)
    mu = jnp.mean(xf, axis=-1, keepdims=True)
    var = jnp.mean(jnp.square(xf - mu), axis=-1, keepdims=True)
    y = (xf - mu) * lax.rsqrt(var + LN_EPS)
    return (y * g.astype(jnp.float32) + b.astype(jnp.float32)).astype(x.dtype)


def mlstm_chunk(carry, blk):
    C0, n0, m0 = carry
    q, k, v, ig, lf = blk
    L = q.shape[2]
    b = jnp.cumsum(lf, axis=-1)
    a = b + m0[..., None]
    causal = jnp.tril(jnp.ones((L, L), dtype=bool))
    d = jnp.where(causal, b[..., :, None] - b[..., None, :] + ig[..., None, :], -jnp.inf)
    m = jnp.maximum(a, jnp.max(d, axis=-1))
    w_intra = jnp.exp(d - m[..., None])
    w_inter = jnp.exp(a - m)
    s = jnp.einsum('bhtk,bhsk->bhts', q, k) * w_intra
    num = jnp.einsum('bhts,bhsv->bhtv', s, v) + w_inter[..., None] * jnp.einsum('bhtk,bhkv->bhtv', q, C0)
    den = jnp.sum(s, axis=-1) + w_inter * jnp.einsum('bhtk,bhk->bht', q, n0)
    h = num / jnp.maximum(jnp.abs(den), jnp.exp(-m))[..., None]
    m_end = m[..., -1]
    w_end = jnp.exp(b[..., -1:] - b + ig - m_end[..., None])
    decay = jnp.exp(a[..., -1] - m_end)
    C1 = decay[..., None, None] * C0 + jnp.einsum('bhs,bhsk,bhsv->bhkv', w_end, k, v)
    n1 = decay[..., None] * n0 + jnp.einsum('bhs,bhsk->bhk', w_end, k)
    return (C1, n1, m_end), h


def mlstm(q, k, v, ig, lf, C0, n0, m0):
    Bsz, S = q.shape[0], q.shape[1]
    L = min(S, CHUNK)
    nc = S // L
    def blocks4(t):
        return t.reshape(Bsz, nc, L, N_HEADS, HEAD_DIM).transpose(1, 0, 3, 2, 4)
    def blocks3(t):
        return t.reshape(Bsz, nc, L, N_HEADS).transpose(1, 0, 3, 2)
    (C1, n1, m1), h = lax.scan(mlstm_chunk, (C0, n0, m0),
                               (blocks4(q), blocks4(k), blocks4(v), blocks3(ig), blocks3(lf)))
    h = h.transpose(1, 0, 3, 2, 4).reshape(Bsz, S, N_HEADS, HEAD_DIM)
    return h, C1, n1, m1


def encoder_layer(x, conv_buf, C0, n0, m0, w_in, b_gate, w_dw, b_dw, ln_a_g, ln_a_b, w_a_out,
                  hn_g, w_b_out, w_out, ln1_g, ln1_b, w_ff1, w_ff2, ln2_g, ln2_b):
    Bsz, S, _ = x.shape
    z = x @ w_in
    o1 = 2 * E_CONV
    o2 = o1 + E_MLSTM
    o3 = o2 + E_MLSTM
    o4 = o3 + E_MLSTM
    o5 = o4 + E_MLSTM
    o6 = o5 + N_HEADS
    o7 = o6 + N_HEADS
    o8 = o7 + D_MODEL
    z_conv, z_q, z_k, z_v, z_o = z[..., :o1], z[..., o1:o2], z[..., o2:o3], z[..., o3:o4], z[..., o4:o5]
    z_i, z_f, g_a, g_b = z[..., o5:o6], z[..., o6:o7], z[..., o7:o8], z[..., o8:]

    glu = z_conv[..., :E_CONV] * jax.nn.sigmoid(z_conv[..., E_CONV:])
    conv_in = jnp.concatenate([conv_buf.astype(glu.dtype), glu], axis=1)
    dw = lax.conv_general_dilated(conv_in, w_dw[:, None, :].astype(conv_in.dtype), window_strides=(1,),
                                  padding='VALID', dimension_numbers=('NWC', 'WIO', 'NWC'),
                                  feature_group_count=E_CONV) + b_dw
    y_a = jax.nn.silu(layer_norm(dw, ln_a_g, ln_a_b)) @ w_a_out
    new_buf = conv_in[:, -CONV_BUF:]

    def heads(t):
        return t.reshape(Bsz, S, N_HEADS, HEAD_DIM).astype(jnp.float32)
    ig = (z_i + b_gate[:N_HEADS]).astype(jnp.float32)
    lf = jax.nn.log_sigmoid((z_f + b_gate[N_HEADS:]).astype(jnp.float32))
    h, C1, n1, m1 = mlstm(heads(z_q), heads(z_k) * (HEAD_DIM ** -0.5), heads(z_v), ig, lf, C0, n0, m0)
    mu = jnp.mean(h, axis=-1, keepdims=True)
    var = jnp.mean(jnp.square(h - mu), axis=-1, keepdims=True)
    hn = (h - mu) * lax.rsqrt(var + LN_EPS) * hn_g.reshape(N_HEADS, HEAD_DIM).astype(jnp.float32)
    y_b = (jax.nn.sigmoid(z_o) * hn.reshape(Bsz, S, E_MLSTM).astype(x.dtype)) @ w_b_out

    merged = jax.nn.sigmoid(g_a) * y_a + jax.nn.sigmoid(g_b) * y_b
    x1 = layer_norm(ALPHA * x + merged @ w_out, ln1_g, ln1_b)
    ff = jnp.square(jax.nn.relu(x1 @ w_ff1)) @ w_ff2
    x2 = layer_norm(ALPHA * x1 + ff, ln2_g, ln2_b)
    return x2, new_buf, C1, n1, m1


def setup_inputs(seed: int = 0) -> dict:
    key = jax.random.key(seed)
    ks = jax.random.split(key, 24)
    nrm = jax.random.normal
    f32 = jnp.float32
    x_prompt = nrm(ks[0], (BATCH, SEQ, D_MODEL), f32)
    x_sample = nrm(ks[1], (DEC_BATCH, DEC_SEQ, D_MODEL), f32)
    cache_conv = 0.5 * nrm(ks[2], (DEPTH, DEC_BATCH, CONV_BUF, E_CONV), f32)
    state_C = 0.05 * nrm(ks[3], (DEPTH, DEC_BATCH, N_HEADS, HEAD_DIM, HEAD_DIM), f32)
    state_n = 0.05 * nrm(ks[4], (DEPTH, DEC_BATCH, N_HEADS, HEAD_DIM), f32)
    state_m = 0.5 * nrm(ks[5], (DEPTH, DEC_BATCH, N_HEADS), f32)
    w_in = nrm(ks[6], (DEPTH, D_MODEL, D_IN), f32) * D_MODEL ** -0.5
    b_gate = jnp.concatenate([0.1 * nrm(ks[7], (DEPTH, N_HEADS), f32),
                              3.0 + 0.5 * nrm(ks[8], (DEPTH, N_HEADS), f32)], axis=-1)
    w_dw = nrm(ks[9], (DEPTH, CONV_WIDTH, E_CONV), f32) * CONV_WIDTH ** -0.5
    b_dw = 0.02 * nrm(ks[10], (DEPTH, E_CONV), f32)
    ln_a_g = 1.0 + 0.02 * nrm(ks[11], (DEPTH, E_CONV), f32)
    ln_a_b = 0.02 * nrm(ks[12], (DEPTH, E_CONV), f32)
    w_a_out = nrm(ks[13], (DEPTH, E_CONV, D_MODEL), f32) * (E_CONV ** -0.5) * BETA
    hn_g = 1.0 + 0.02 * nrm(ks[14], (DEPTH, E_MLSTM), f32)
    w_b_out = nrm(ks[15], (DEPTH, E_MLSTM, D_MODEL), f32) * (E_MLSTM ** -0.5) * BETA
    w_out = nrm(ks[16], (DEPTH, D_MODEL, D_MODEL), f32) * (D_MODEL ** -0.5) * BETA
    ln1_g = 1.0 + 0.02 * nrm(ks[17], (DEPTH, D_MODEL), f32)
    ln1_b = 0.02 * nrm(ks[18], (DEPTH, D_MODEL), f32)
    w_ff1 = nrm(ks[19], (DEPTH, D_MODEL, D_FF), f32) * D_MODEL ** -0.5
    w_ff2 = nrm(ks[20], (DEPTH, D_FF, D_MODEL), f32) * (D_FF ** -0.5) * BETA
    ln2_g = 1.0 + 0.02 * nrm(ks[21], (DEPTH, D_MODEL), f32)
    ln2_b = 0.02 * nrm(ks[22], (DEPTH, D_MODEL), f32)
    return {"x_prompt": x_prompt, "x_sample": x_sample, "cache_conv": cache_conv,
            "state_C": state_C, "state_n": state_n, "state_m": state_m,
            "w_in": w_in, "b_gate": b_gate, "w_dw": w_dw, "b_dw": b_dw,
            "ln_a_g": ln_a_g, "ln_a_b": ln_a_b, "w_a_out": w_a_out, "hn_g": hn_g,
            "w_b_out": w_b_out, "w_out": w_out, "ln1_g": ln1_g, "ln1_b": ln1_b,
            "w_ff1": w_ff1, "w_ff2": w_ff2, "ln2_g": ln2_g, "ln2_b": ln2_b}


def reference(x_prompt, x_sample, cache_conv, state_C, state_n, state_m, w_in, b_gate, w_dw, b_dw,
              ln_a_g, ln_a_b, w_a_out, hn_g, w_b_out, w_out, ln1_g, ln1_b, w_ff1, w_ff2, ln2_g, ln2_b):
    yp, ys = x_prompt, x_sample
    bp = x_prompt.shape[0]
    pc, pC, pn, pm = [], [], [], []
    sc, sC, sn, sm = [], [], [], []
    for l in range(DEPTH):
        params = (w_in[l], b_gate[l], w_dw[l], b_dw[l], ln_a_g[l], ln_a_b[l], w_a_out[l], hn_g[l],
                  w_b_out[l], w_out[l], ln1_g[l], ln1_b[l], w_ff1[l], w_ff2[l], ln2_g[l], ln2_b[l])
        buf0 = jnp.zeros((bp, CONV_BUF, E_CONV), x_prompt.dtype)
        C0 = jnp.zeros((bp, N_HEADS, HEAD_DIM, HEAD_DIM), jnp.float32)
        n0 = jnp.zeros((bp, N_HEADS, HEAD_DIM), jnp.float32)
        m0 = jnp.zeros((bp, N_HEADS), jnp.float32)
        yp, b1, C1, n1, m1 = encoder_layer(yp, buf0, C0, n0, m0, *params)
        pc.append(b1); pC.append(C1); pn.append(n1); pm.append(m1)
        ys, b2, C2, n2, m2 = encoder_layer(ys, cache_conv[l], state_C[l].astype(jnp.float32),
                                           state_n[l].astype(jnp.float32),
                                           state_m[l].astype(jnp.float32), *params)
        sc.append(b2); sC.append(C2); sn.append(n2); sm.append(m2)
    return (yp, ys, jnp.stack(pc), jnp.stack(pC), jnp.stack(pn), jnp.stack(pm),
            jnp.stack(sc), jnp.stack(sC), jnp.stack(sn), jnp.stack(sm))
```

```python
import numpy as np
import concourse.bass as bass
import concourse.mybir as mybir
from concourse.bass_utils import run_bass_kernel_spmd

F32, BF16 = mybir.dt.float32, mybir.dt.bfloat16
AF = mybir.ActivationFunctionType
ALU = mybir.AluOpType
AX = mybir.AxisListType

NSEQ, NPRE, NMAIN, NSMP = 8192, 6144, 2048, 256
T = 256
ALPHA = 2.0 ** 0.25
EPS = 1e-5
NPV = 32 + 8 * 31
NW = 4
G_CONV, G_Q, G_K, G_V, G_O, G_GA, G_GB = 0, 4, 6, 8, 10, 12, 14
G_A, G_B, G_OUT, G_F1, G_F2 = 16, 18, 20, 22, 30
NG = 38
WIN_COL = [0, 512, 1024, 1536, 2048, 2560, 3072, 3584, 4096, 4608, 5120, 5632, 6152, 6664, 7176, 7688]


class Res:
    __slots__ = ("name", "w", "r", "excl")

    def __init__(self, name, excl=False):
        self.name = name
        self.w = None
        self.r = {}
        self.excl = excl


class Stream:
    def __init__(self, nc, name):
        self.name = name
        self.sem = nc.alloc_semaphore("ds_" + name)
        self.count = 0


class KB:
    def __init__(self, nc):
        self.nc = nc
        self.eng = {"pe": nc.tensor, "act": nc.scalar, "dve": nc.vector, "pool": nc.gpsimd, "sp": nc.sync}
        self.sem = {e: nc.alloc_semaphore("sem_" + e) for e in ("pe", "act", "dve", "pool")}
        self.seq = {e: 0 for e in self.sem}
        self.waited = {}
        self.nbuf = 0

    def sb(self, shape, dt=F32, name=None):
        self.nbuf += 1
        nm = "s_" + (name or ("b%d" % self.nbuf))
        t = self.nc.alloc_sbuf_tensor(nm, list(shape), dt).ap()
        return t, Res(nm)

    def _wait(self, eng, sem, val):
        key = (eng, sem.num)
        if self.waited.get(key, 0) >= val:
            return
        self.waited[key] = val
        self.eng[eng].wait_ge(sem, val)

    def _deps(self, eng, reads, writes):
        toks = []
        for r in reads:
            if r.w is not None:
                toks.append((r.w, "raw"))
            if r.excl:
                for t in r.r.values():
                    toks.append((t, "war"))
        for w in writes:
            if w.w is not None:
                toks.append((w.w, "waw"))
            for t in w.r.values():
                toks.append((t, "war"))
        for (tok, kind) in toks:
            sem, val, teng = tok
            if teng == eng:
                if eng == "pe":
                    continue
            self._wait(eng, sem, val)

    def op(self, eng, fn, reads=(), writes=()):
        self._deps(eng, reads, writes)
        ins = fn(self.eng[eng])
        self.seq[eng] += 1
        ins.then_inc(self.sem[eng], 1)
        tok = (self.sem[eng], self.seq[eng], eng)
        for r in reads:
            r.r[eng] = tok
        for w in writes:
            w.w = tok
            w.r = {}
        return tok

    def dma(self, q, stream, out, in_, reads=(), writes=(), **kw):
        self._deps(q, reads, writes)
        ins = self.eng[q].dma_start(out=out, in_=in_, **kw)
        stream.count += 16
        ins.then_inc(stream.sem, 16)
        tok = (stream.sem, stream.count, "dma:" + stream.name)
        for r in reads:
            r.r["dma:" + stream.name] = tok
        for w in writes:
            w.w = tok
            w.r = {}
        return tok


def build():
    nc = bass.Bass("TRN2", target_bir_lowering=False)
    kb = KB(nc)

    def din(name, shape):
        return nc.dram_tensor(name, list(shape), F32, kind="ExternalInput").ap()

    def dout(name, shape):
        return nc.dram_tensor(name, list(shape), F32, kind="ExternalOutput").ap()

    xT_seq = din("xT_seq", [1024, NSEQ])
    xT_smp = din("xT_smp", [1024, NSMP])
    x_tok = din("x_tok", [NMAIN + NSMP, 1024])
    valid_d = din("valid", [4, NSEQ])
    nbig_d = din("nbig", [4, NSEQ])
    cache_c = din("cache_c", [4, 30, 1024])
    sC = din("sC", [4, 4, 256, 256])
    sn = din("sn", [4, 4, 256])
    sm = din("sm", [4, 4])
    w_in = din("w_in", [1024, 8200])
    w_a = din("w_a", [1024, 1024])
    w_b = din("w_b", [1024, 1024])
    w_o = din("w_o", [1024, 1024])
    w_f1 = din("w_f1", [1024, 4096])
    w_f2 = din("w_f2", [4096, 1024])
    pv_d = din("pv", [128, NPV])
    gb_d = din("gbias", [4, 2])
    lnrows = din("lnrows", [4, 1024])
    ident_d = din("ident", [128, 128])
    cmask_d = din("cmask2", [128, 64])
    reset_d = din("resetm", [4, 512])
    oh_d = din("oh", [4, 4, 8])
    wsc = nc.dram_tensor("wsc", [NG, 128, 4096], BF16, kind="Internal").ap()
    y_d = dout("y", [NMAIN + NSMP, 1024])
    conv_p = dout("conv_p", [30, 1024])
    Cn_p = dout("Cn_p", [4, 2, 128, 257])
    m_p = dout("m_p", [4, 1])
    conv_s = dout("conv_s", [4, 30, 1024])
    Cn_s = dout("Cn_s", [4, 4, 2, 128, 257])
    m_s = dout("m_s", [4, 4])

    R_in = Res("dram_in")
    R_wsc = [Res("wsc%d" % g) for g in range(NG)]
    R_out = Res("dram_out")
    s_const = Stream(nc, "const")
    s_conv = [Stream(nc, "wcv%d" % g) for g in range(NG)]
    s_out = Stream(nc, "out")

    pb, pbr = [], []
    for i in range(7):
        pb.append(nc.alloc_psum_tensor("pb%d" % i, [128, 512], F32).ap())
        pbr.append(Res("pb%d" % i, True))
    pbh = nc.alloc_psum_tensor("pbh", [128, 1024], BF16).ap()
    pbh_r = Res("pbh", True)

    ident, ident_r = kb.sb([128, 128], F32, "ident")
    identb, identb_r = kb.sb([128, 128], BF16, "identb")
    onesb, onesb_r = kb.sb([128, 128], BF16, "onesb")
    ones4, ones4_r = kb.sb([4, 128], F32, "ones4")
    cmask, cmask_r = kb.sb([128, 64], F32, "cmask")
    resetm, resetm_r = kb.sb([4, 512], F32, "resetm")
    oh, oh_r = kb.sb([4, 4, 8], F32, "oh")
    pv, pv_r = kb.sb([128, NPV], F32, "pv")
    gbt, gbt_r = kb.sb([4, 2], F32, "gbt")
    nbf, nbf_r = kb.sb([4, 1], F32, "nbf")
    lnbc, lnbc_r = kb.sb([128, 4, 1024], F32, "lnbc")
    for (dst, src, rr) in ((ident, ident_d, ident_r), (cmask, cmask_d, cmask_r), (resetm, reset_d, resetm_r),
                           (oh, oh_d, oh_r), (pv, pv_d, pv_r), (gbt, gb_d, gbt_r)):
        kb.dma("sp", s_const, dst, src, reads=[R_in], writes=[rr])
    for i in range(4):
        kb.dma("sp", s_const, lnbc[:, i, :], lnrows[i:i + 1, :].partition_broadcast(128), reads=[R_in], writes=[lnbc_r])
    tok_all = (s_const.sem, s_const.count, "dma:const")
    for rr in (ident_r, cmask_r, resetm_r, oh_r, pv_r, gbt_r, lnbc_r):
        rr.w = tok_all
    kb.op("dve", lambda e: e.tensor_copy(out=identb, in_=ident), reads=[ident_r], writes=[identb_r])
    kb.op("dve", lambda e: e.memset(onesb, 1.0 / 1024.0), writes=[onesb_r])
    kb.op("dve", lambda e: e.memset(ones4, 1.0), writes=[ones4_r])
    kb.op("dve", lambda e: e.tensor_scalar(out=nbf, in0=gbt[:, 1:2], scalar1=-1.0, scalar2=None, op0=ALU.mult),
          reads=[gbt_r], writes=[nbf_r])

    tokE, tokE_r = kb.sb([128, 66, 2, 4], F32, "tokE")
    floorC, floorC_r = kb.sb([64, 132, 4], F32, "floorC")
    decb, decb_r = kb.sb([128, 17, 4, 8], F32, "decb")
    m_all, m_all_r = kb.sb([4, 129], F32, "m_all")
    msin, msin_r = kb.sb([4, 4], F32, "msin")
    mend_s, mend_s_r = kb.sb([4, 4], F32, "mend_s")
    Cst, Cst_r, Cb, Cb_r, Cst_rd = [], [], [], [], []
    for h in range(4):
        a, r = kb.sb([128, 2, 257], F32, "Cst%d" % h)
        Cst.append(a); Cst_r.append(r); Cst_rd.append([Res("Cst%d_0" % h), Res("Cst%d_1" % h)])
        a, r = kb.sb([128, 2, 257], BF16, "Cb%d" % h)
        Cb.append(a); Cb_r.append(r)
        kb.op("dve", lambda e, a=Cst[h]: e.memset(a, 0.0), writes=Cst_rd[h])
    kb.op("dve", lambda e: e.memset(m_all, 0.0), writes=[m_all_r])
    s_msin = Stream(nc, "msin")
    kb.dma("sp", s_msin, msin, sm, reads=[R_in], writes=[msin_r])
    xTh, xTh_r = kb.sb([128, 8, 128], BF16, "xTh")

    with nc.sbuf_tensor("p_wkv", [128, 8, 2048], BF16) as wkv_t, \
            nc.sbuf_tensor("p_wg", [128, 8, 8], BF16) as wg_t, \
            nc.sbuf_tensor("xTb0", [128, 8, 512], BF16) as xTb0_t, \
            nc.sbuf_tensor("xTb1", [128, 8, 512], BF16) as xTb1_t, \
            nc.sbuf_tensor("xTb2", [128, 8, 512], BF16) as xTb2_t, \
            nc.sbuf_tensor("xTb3", [128, 8, 512], BF16) as xTb3_t, \
            nc.sbuf_tensor("gt", [4, 12, 512], F32) as gt_t, \
            nc.sbuf_tensor("E3", [68, 512], F32) as E3_t, \
            nc.sbuf_tensor("gsm", [4, 8, 8], F32) as gsm_t, \
            nc.sbuf_tensor("drhs", [4, 4, 8], F32) as drhs_t, \
            nc.sbuf_tensor("ktok", [128, 2, 1024], BF16) as ktok_t, \
            nc.sbuf_tensor("wvp", [128, 2, 4, 257], BF16) as wvp_t:
        wkv, wg = wkv_t.ap(), wg_t.ap()
        xTb = [xTb0_t.ap(), xTb1_t.ap(), xTb2_t.ap(), xTb3_t.ap()]
        gt, E3, gsm, drhs, ktokp, wvp = gt_t.ap(), E3_t.ap(), gsm_t.ap(), drhs_t.ap(), ktok_t.ap(), wvp_t.ap()
        wkv_r, wg_r = Res("wkv"), Res("wg")
        xTb_r = [Res("xTb%d" % i) for i in range(4)]
        gt_r = [Res("gt%d" % i) for i in range(12)]
        E3_r, drhs_r = Res("E3"), Res("drhs")
        gsm_r = [Res("gsm%d" % i) for i in range(8)]
        ktokp_r = [Res("ktokp0"), Res("ktokp1")]
        wvp_r = [Res("wvp0"), Res("wvp1")]
        s_pw = Stream(nc, "pw")
        s_x = [Stream(nc, "xTb%d" % i) for i in range(4)]
        s_vv = [Stream(nc, "vblk0"), Stream(nc, "vblk1")]
        s_vn = [Stream(nc, "nblk0"), Stream(nc, "nblk1")]

        w_in_k = w_in.rearrange("(k p) n -> p k n", p=128)
        kb.dma("pool", s_pw, wg, w_in_k[:, :, 6144:6152], reads=[R_in], writes=[wg_r])
        kb.dma("pool", s_pw, wkv[:, :, 0:1024], w_in_k[:, :, 3072:4096], reads=[R_in], writes=[wkv_r])
        kb.dma("pool", s_pw, wkv[:, :, 1024:2048], w_in_k[:, :, 4096:5120], reads=[R_in], writes=[wkv_r])
        tok_pw = (s_pw.sem, s_pw.count, "dma:pw")
        wg_r.w = tok_pw
        wkv_r.w = tok_pw
        kb.op("dve", lambda e: e.memset(E3, 0.0), writes=[E3_r])

        xTs = xT_seq.rearrange("(k p) t -> p k t", p=128)
        xTm = xT_smp.rearrange("(k p) t -> p k t", p=128)

        xsrcs = [(xTs[:, :, b_ * 512:(b_ + 1) * 512], 512) for b_ in range(16)] + [(xTm[:, :, 0:NSMP], NSMP)]
        xissued = [0]

        def xget(i):
            while xissued[0] < len(xsrcs) and xissued[0] <= i + 1:
                j = xissued[0]
                src_, n_ = xsrcs[j]
                kb.dma("pool", s_x[j % 4], xTb[j % 4][:, :, 0:n_], src_, reads=[R_in], writes=[xTb_r[j % 4]])
                xissued[0] += 1
            return xTb[i % 4], xTb_r[i % 4]

        def conv_dma(g, src):
            kb.dma("pool", s_conv[g], wsc[g].rearrange("p (k n) -> p k n", k=src.shape[1]), src, reads=[R_in], writes=[R_wsc[g]])

        conv_jobs = []
        for gi, c0 in enumerate(WIN_COL):
            conv_jobs.append((gi, w_in_k[:, :, c0:c0 + 512]))
        for (g0, wd) in ((G_A, w_a), (G_B, w_b), (G_OUT, w_o)):
            wk_ = wd.rearrange("(k p) n -> p k n", p=128)
            for i in range(2):
                conv_jobs.append((g0 + i, wk_[:, :, i * 512:(i + 1) * 512]))
        wk_ = w_f1.rearrange("(k p) n -> p k n", p=128)
        for i in range(8):
            conv_jobs.append((G_F1 + i, wk_[:, :, i * 512:(i + 1) * 512]))
        wk_ = w_f2.rearrange("(k p) n -> p k n", p=128)
        for i in range(8):
            conv_jobs.append((G_F2 + i, wk_[:, 4 * i:4 * i + 4, :]))

        use_order = [2, 3, 0, 1, G_Q, G_Q + 1, G_K, G_K + 1, G_V, G_V + 1, G_O, G_O + 1, G_GB, G_GB + 1,
                     G_GA, G_GA + 1, G_A, G_A + 1, G_B, G_B + 1, G_OUT, G_OUT + 1] + [G_F1 + i for i in range(4)] + \
                    [G_F2 + i for i in range(4)] + [G_F1 + 4 + i for i in range(4)] + [G_F2 + 4 + i for i in range(4)]
        conv_jobs.sort(key=lambda j_: use_order.index(j_[0]))
        def gate_block(blk, n, chunk0, tile0, sample, grouped):
            nch = n // 64
            xt, xt_r = xget(blk)
            vs, ns_ = (5, 6) if blk % 2 == 0 else (7, 8)
            for _ in range(1):
                if conv_jobs:
                    conv_dma(*conv_jobs.pop(0))
            def load_mask(b_):
                v_, n_2 = (5, 6) if b_ % 2 == 0 else (7, 8)
                kb.dma("sp", s_vv[b_ % 2], gt[:, v_, 0:512], valid_d[:, b_ * 512:b_ * 512 + 512], reads=[R_in], writes=[gt_r[v_]])
                kb.dma("sp", s_vn[b_ % 2], gt[:, n_2, 0:512], nbig_d[:, b_ * 512:b_ * 512 + 512], reads=[R_in], writes=[gt_r[n_2]])
            if blk == 0:
                load_mask(0)
            if blk + 1 < 16:
                load_mask(blk + 1)
            zi, zf = pb[4], pb[5]
            for k in range(8):
                kb.op("pe", lambda e, k=k: e.matmul(zi[0:4, 0:n], lhsT=wg[:, k, 0:4], rhs=xt[:, k, 0:n], start=(k == 0), stop=(k == 7)),
                      reads=[wg_r, xt_r], writes=[pbr[4]])
            for k in range(8):
                kb.op("pe", lambda e, k=k: e.matmul(zf[0:4, 0:n], lhsT=wg[:, k, 4:8], rhs=xt[:, k, 0:n], start=(k == 0), stop=(k == 7)),
                      reads=[wg_r, xt_r], writes=[pbr[5]])
            s1_, s2_ = (1, 2) if blk % 2 == 0 else (9, 10)
            te, nlfm, igm, bneg, g = gt[:, 0, 0:n], gt[:, s1_, 0:n], gt[:, s2_, 0:n], gt[:, 3, 0:n], gt[:, 4, 0:n]
            kb.op("act", lambda e: e.activation(out=te, in_=zf[0:4, 0:n], func=AF.Exp, scale=-1.0, bias=nbf[:, 0:1]),
                  reads=[pbr[5], nbf_r], writes=[gt_r[0]])
            kb.op("act", lambda e: e.activation(out=te, in_=te, func=AF.Ln, bias=1.0), reads=[gt_r[0]], writes=[gt_r[0]])
            if not sample:
                kb.op("dve", lambda e: e.tensor_tensor(out=nlfm, in0=te, in1=gt[:, vs, 0:n], op=ALU.mult),
                      reads=[gt_r[0], gt_r[vs]], writes=[gt_r[s1_]])
                kb.op("dve", lambda e: e.scalar_tensor_tensor(out=igm, in0=zi[0:4, 0:n], scalar=gbt[:, 0:1], in1=gt[:, vs, 0:n],
                                                              op0=ALU.add, op1=ALU.mult),
                      reads=[pbr[4], gbt_r, gt_r[vs]], writes=[gt_r[s2_]])
                kb.op("dve", lambda e: e.tensor_tensor(out=igm, in0=igm, in1=gt[:, ns_, 0:n], op=ALU.add),
                      reads=[gt_r[s2_], gt_r[ns_]], writes=[gt_r[s2_]])
            else:
                kb.op("dve", lambda e: e.tensor_copy(out=nlfm, in_=te), reads=[gt_r[0]], writes=[gt_r[s1_]])
                kb.op("dve", lambda e: e.tensor_scalar(out=igm, in0=zi[0:4, 0:n], scalar1=gbt[:, 0:1], scalar2=None, op0=ALU.add),
                      reads=[pbr[4], gbt_r], writes=[gt_r[s2_]])
            yield
            kb.op("dve", lambda e: e.tensor_tensor_scan(out=bneg, data0=resetm[:, 0:n], data1=nlfm, initial=0.0,
                                                        op0=ALU.mult, op1=ALU.add),
                  reads=[resetm_r, gt_r[s1_]], writes=[gt_r[3]])
            yield
            kb.op("dve", lambda e: e.tensor_tensor(out=g, in0=igm, in1=bneg, op=ALU.add),
                  reads=[gt_r[s2_], gt_r[3]], writes=[gt_r[4]])
            yield
            g3 = g.rearrange("p (c l) -> p c l", l=64)
            b3 = bneg.rearrange("p (c l) -> p c l", l=64)
            gmax, Mc, nd, dec = gsm[:, 0, 0:nch], gsm[:, 1, 0:nch], gsm[:, 2, 0:nch], gsm[:, 3, 0:nch]
            nbtot = b3[:, :, 63]
            kb.op("dve", lambda e: e.tensor_reduce(out=gmax, in_=g3, axis=AX.X, op=ALU.max), reads=[gt_r[4]], writes=[gsm_r[0]])
            yield
            if not sample:
                kb.op("dve", lambda e: e.tensor_tensor_scan(out=m_all[:, chunk0 + 1:chunk0 + 1 + nch], data0=gmax, data1=nbtot,
                                                            initial=m_all[:, chunk0:chunk0 + 1], op0=ALU.max, op1=ALU.subtract),
                      reads=[gsm_r[0], gt_r[3], m_all_r], writes=[m_all_r])
                yield
                m0 = m_all[:, chunk0:chunk0 + nch]
                m0_r = m_all_r
            else:
                m0 = msin[:, 0:nch]
                m0_r = msin_r
            kb.op("dve", lambda e: e.tensor_tensor(out=Mc, in0=gmax, in1=m0, op=ALU.max), reads=[gsm_r[0], m0_r], writes=[gsm_r[1]])
            yield
            if sample:
                kb.op("dve", lambda e: e.tensor_tensor(out=mend_s[:, 0:nch], in0=Mc, in1=nbtot, op=ALU.subtract),
                      reads=[gsm_r[1], gt_r[3]], writes=[mend_s_r])
                yield
            kb.op("dve", lambda e: e.tensor_tensor(out=nd, in0=m0, in1=Mc, op=ALU.subtract), reads=[gsm_r[1], m0_r], writes=[gsm_r[2]])
            yield
            if grouped:
                Mc2, ndg = gsm[:, 4, 0:nch], gsm[:, 5, 0:nch // 2]
                Mcv = Mc.rearrange("p (j t) -> p j t", t=2)
                Mc2v = Mc2.rearrange("p (j t) -> p j t", t=2)
                ndv = nd.rearrange("p (j t) -> p j t", t=2)
                kb.op("dve", lambda e: e.tensor_copy(out=Mc2, in_=Mc), reads=[gsm_r[1]], writes=[gsm_r[4]])
                yield
                kb.op("dve", lambda e: e.tensor_tensor(out=Mc2v[:, :, 0], in0=Mcv[:, :, 0], in1=ndv[:, :, 1], op=ALU.subtract),
                      reads=[gsm_r[1], gsm_r[2], gsm_r[4]], writes=[gsm_r[4]])
                yield
                kb.op("dve", lambda e: e.tensor_tensor(out=ndg, in0=ndv[:, :, 0], in1=ndv[:, :, 1], op=ALU.add), reads=[gsm_r[2]], writes=[gsm_r[5]])
                yield
                Mw, Mw_r, ndx, ndx_r, ndec = Mc2, gsm_r[4], ndg, gsm_r[5], nch // 2
            else:
                Mw, Mw_r, ndx, ndx_r, ndec = Mc, gsm_r[1], nd, gsm_r[2], nch
            e3a = E3[0:4, 0:n].rearrange("p (c l) -> p c l", l=64)
            e3b = E3[32:36, 0:n].rearrange("p (c l) -> p c l", l=64)
            e3c = E3[64:68, 0:n].rearrange("p (c l) -> p c l", l=64)
            m0b = m0.unsqueeze(2).to_broadcast([4, nch, 64])
            Mcb = Mw.unsqueeze(2).to_broadcast([4, nch, 64])
            kb.op("dve", lambda e: e.tensor_tensor(out=e3a, in0=g3, in1=m0b, op=ALU.subtract), reads=[gt_r[4], m0_r], writes=[E3_r])
            yield
            kb.op("dve", lambda e: e.tensor_tensor(out=e3b, in0=g3, in1=Mcb, op=ALU.subtract), reads=[gt_r[4], Mw_r], writes=[E3_r])
            yield
            kb.op("dve", lambda e: e.tensor_tensor(out=e3c, in0=b3, in1=m0b, op=ALU.subtract), reads=[gt_r[3], m0_r], writes=[E3_r])
            yield
            for r0 in (0, 32, 64):
                kb.op("act", lambda e, r0=r0: e.activation(out=E3[r0:r0 + 4, 0:n], in_=E3[r0:r0 + 4, 0:n], func=AF.Exp), reads=[E3_r], writes=[E3_r])
                yield
            decx = gsm[:, 3, 0:ndec]
            kb.op("act", lambda e: e.activation(out=decx, in_=ndx, func=AF.Exp), reads=[ndx_r], writes=[gsm_r[3]])
            yield
            kb.op("dve", lambda e: e.tensor_tensor(out=drhs[:, :, 0:ndec], in0=oh[:, :, 0:ndec],
                                                   in1=decx.unsqueeze(1).to_broadcast([4, 4, ndec]), op=ALU.mult),
                  reads=[oh_r, gsm_r[3]], writes=[drhs_r])
            yield
            pd = pb[6][:, 0:4 * ndec].rearrange("p (h c) -> p h c", h=4)
            kb.op("pe", lambda e: e.matmul(pd, lhsT=ones4[0:4, :], rhs=drhs[:, :, 0:ndec], start=True, stop=True),
                  reads=[ones4_r, drhs_r], writes=[pbr[6]])
            yield
            kb.op("act", lambda e: e.activation(out=decb[:, blk, :, 0:ndec], in_=pd, func=AF.Identity), reads=[pbr[6]], writes=[decb_r])
            yield
            ntt = n // 128
            ptE = pb[6][:, 64:64 + 4 * 68].rearrange("p (t x) -> p t x", x=68)
            for tt in range(ntt):
                kb.op("pe", lambda e, tt=tt: e.transpose(out=ptE[:, tt, :], in_=E3[0:68, tt * 128:(tt + 1) * 128], identity=ident[0:68, 0:68]),
                      reads=[E3_r, ident_r], writes=[pbr[6]])
                yield
            src2 = pb[6][:, 64:64 + 4 * 68].rearrange("p (t x) -> p t x", x=68)[:, 0:ntt, 0:64].rearrange("p t (q x) -> p t q x", x=32)[:, :, :, 0:4]
            kb.op("act", lambda e: e.activation(out=tokE[:, tile0:tile0 + ntt, :, :], in_=src2, func=AF.Identity), reads=[pbr[6]], writes=[tokE_r])
            yield
            pf = pb[6][0:64, 400:400 + 4 * nch].rearrange("p (c x) -> p c x", x=4)
            for cc in range(nch):
                kb.op("pe", lambda e, cc=cc: e.transpose(out=pf[:, cc, :], in_=E3[64:68, cc * 64:(cc + 1) * 64], identity=ident[64:68, 64:68]),
                      reads=[E3_r, ident_r], writes=[pbr[6]])
                yield
            kb.op("dve", lambda e: e.tensor_copy(out=floorC[:, chunk0:chunk0 + nch, :], in_=pf), reads=[pbr[6]], writes=[floorC_r])
            yield

        ubank = [0]

        ubank = [0]

        def state_update_tile(ktok_ap, ktok_res, wv_ap, wv_res, tile, banks):
            out = []
            for h in range(4):
                for dkc in range(2):
                    def f(h=h, dkc=dkc):
                        bi = banks[ubank[0] % len(banks)]
                        ubank[0] += 1
                        kb.op("pe", lambda e: e.matmul(
                            pb[bi][:, 0:257], lhsT=ktok_ap[:, h * 256 + dkc * 128:h * 256 + dkc * 128 + 128],
                            rhs=wv_ap[:, h, 0:257], start=True, stop=True),
                            reads=[ktok_res, wv_res], writes=[pbr[bi]])
                        kb.op("dve", lambda e: e.scalar_tensor_tensor(
                            out=Cst[h][:, dkc, :], in0=Cst[h][:, dkc, :], scalar=decb[:, tile // 4, h, (tile % 4):(tile % 4) + 1],
                            in1=pb[bi][:, 0:257], op0=ALU.mult, op1=ALU.add),
                            reads=[Cst_rd[h][dkc], decb_r, pbr[bi]], writes=[Cst_rd[h][dkc]])
                    out.append(f)
            return out

        deferred = []

        gstep = [None]

        def pump(n=1):
            for _ in range(n):
                if deferred:
                    deferred.pop(0)()
            if gstep[0] is not None and n == 1:
                for _ in range(2):
                    try:
                        next(gstep[0])
                    except StopIteration:
                        gstep[0] = None
                        break

        evt = [0]

        def kv_tile(xt, xt_r, tsl, wk_ap, wk_r, wv_w_ap, wv_w_r, ktok_ap, ktok_res, wv_ap, wv_res, tileg, vext_ap=None, vext_res=None):
            for hf in range(2):
                bi = hf
                for k in range(8):
                    kb.op("pe", lambda e, k=k, hf=hf, bi=bi: e.matmul(pb[bi][:, 0:512], lhsT=xt[:, k, tsl], rhs=wk_ap(hf)[:, k, :],
                                                                     start=(k == 0), stop=(k == 7)),
                          reads=[xt_r, wk_r(hf)], writes=[pbr[bi]])
                    if k % 4 == 3:
                        pump()
                kb.op("act", lambda e, hf=hf, bi=bi: e.activation(out=ktok_ap[:, hf * 512:(hf + 1) * 512], in_=pb[bi][:, 0:512],
                                                                  func=AF.Identity, scale=0.0625),
                      reads=[pbr[bi]], writes=[ktok_res])
            for hf in range(2):
                bi = 2 + hf
                for k in range(8):
                    kb.op("pe", lambda e, k=k, hf=hf, bi=bi: e.matmul(pb[bi][:, 0:512], lhsT=xt[:, k, tsl], rhs=wv_w_ap(hf)[:, k, :],
                                                                     start=(k == 0), stop=(k == 7)),
                          reads=[xt_r, wv_w_r(hf)], writes=[pbr[bi]])
                    if k % 4 == 3:
                        pump()
                if vext_ap is not None:
                    kb.op("act", lambda e, hf=hf, bi=bi: e.activation(out=vext_ap[:, 2 * hf:2 * hf + 2, 0:256],
                                                                      in_=pb[bi][:, 0:512].rearrange("p (h v) -> p h v", h=2), func=AF.Identity),
                          reads=[pbr[bi]], writes=[vext_res])
                for h2 in range(2):
                    hh_ = 2 * hf + h2
                    kb.op("act", lambda e, hf=hf, bi=bi, h2=h2, hh_=hh_: e.activation(
                        out=wv_ap[:, hh_, 0:256], in_=pb[bi][:, h2 * 256:(h2 + 1) * 256], func=AF.Identity,
                        scale=tokE[:, tileg, 1, hh_:hh_ + 1]),
                        reads=[pbr[bi], tokE_r], writes=[wv_res])
            kb.op("pool", lambda e: e.tensor_copy(out=wv_ap[:, :, 256], in_=tokE[:, tileg, 1, :]), reads=[tokE_r], writes=[wv_res])

        def p2_block(tb):
            xt, xt_r = xTb[tb % 4], xTb_r[tb % 4]
            for _ in range(2):
                if conv_jobs:
                    conv_dma(*conv_jobs.pop(0))
            for t4 in range(4):
                tile = tb * 4 + t4
                i2 = tile % 2
                kv_tile(xt, xt_r, slice(t4 * 128, (t4 + 1) * 128),
                        lambda hf: wkv[:, :, hf * 512:(hf + 1) * 512], lambda hf: wkv_r,
                        lambda hf: wkv[:, :, 1024 + hf * 512:1024 + (hf + 1) * 512], lambda hf: wkv_r,
                        ktokp[:, i2, :], ktokp_r[i2], wvp[:, i2, :, :], wvp_r[i2], tile)
                deferred.extend(state_update_tile(ktokp[:, i2, :], ktokp_r[i2], wvp[:, i2, :, :], wvp_r[i2], tile, [4, 5]))

        gens = [gate_block(blk, 512, blk * 8, blk * 4, False, blk < NPRE // 512) for blk in range(16)]
        gens.append(gate_block(16, NSMP, 128, 64, True, False))
        next(gens[0])
        for bi_ in range(17):
            if bi_ + 1 < 17:
                next(gens[bi_ + 1])
            if 0 <= bi_ - 1 < NPRE // 512:
                gstep[0] = gens[bi_]
                p2_block(bi_ - 1)
                gstep[0] = None
            for _ in gens[bi_]:
                pass
        kb.dma("sp", s_out, m_p, m_all[:, 128:129], reads=[m_all_r], writes=[R_out])
        kb.dma("sp", s_out, m_s, mend_s, reads=[mend_s_r], writes=[R_out])

        pump(1000)
        while conv_jobs:
            conv_dma(*conv_jobs.pop(0))
        s_h = Stream(nc, "xTh")
        kb.dma("pool", s_h, xTh, xTs[:, :, NPRE - 128:NPRE], reads=[R_in], writes=[xTh_r])
        bar_res = [wkv_r, wg_r, E3_r, drhs_r] + xTb_r + gt_r + gsm_r + ktokp_r + wvp_r
        for eng in ("pe", "act", "dve", "pool", "sp"):
            kb._deps(eng, [], bar_res)
            kb._deps(eng, bar_res, [])

    ring, ring_r, s_ring = [], [], []
    for i in range(NW):
        a, r = kb.sb([128, 8, 512], BF16, "ring%d" % i)
        ring.append(a); ring_r.append(r); s_ring.append(Stream(nc, "ring%d" % i))
    xTm_b, xTm_r, s_xm = [], [], []
    for i in range(2):
        a, r = kb.sb([128, 8, T], BF16, "xTm%d" % i)
        xTm_b.append(a); xTm_r.append(r); s_xm.append(Stream(nc, "xTm%d" % i))
    xtok, xtok_r = kb.sb([128, 2, 1024], F32, "xtok")
    s_xt = Stream(nc, "xtok")
    glu, glu_r = kb.sb([128, 8, 376], BF16, "glu")
    glu32, glu32_r = kb.sb([128, 8, 4, 30], F32, "glu32")
    hist, hist_r = kb.sb([128, 8, 30], BF16, "hist")
    sigzh, sigzh_r = kb.sb([128, 8, 128], BF16, "sigzh")
    sigz, sigz_r = kb.sb([128, 8, T], BF16, "sigz")
    diags = [kb.sb([128, 31, 128], BF16, "diag%d" % i) for i in range(2)]
    dw, dw_r = kb.sb([128, 8, T], F32, "dw")
    dwb, sq, t1, t2 = [], [], [], []
    for i in range(2):
        dwb.append(kb.sb([128, T], BF16, "dwb%d" % i))
        sq.append(kb.sb([128, T], BF16, "sq%d" % i))
        t1.append(kb.sb([128, T], F32, "t1_%d" % i))
        t2.append(kb.sb([128, T], F32, "t2_%d" % i))
    mean_sb, mean_r = kb.sb([128, T], F32, "mean_sb")
    rstd, rstd_r = kb.sb([128, T], F32, "rstd")
    aT, aT_r = kb.sb([128, 8, T], BF16, "aT")
    sga, sga_r = kb.sb([128, 8, T], BF16, "sga")
    mrg, mrg_r = dw, dw_r
    mrgb, mrgb_r = kb.sb([128, 8, T], BF16, "mrgb")
    qT, qT_r = kb.sb([128, 8, T], BF16, "qT")
    kT, kT_r = kb.sb([128, 8, T], BF16, "kT")
    ktok, ktok_r = kb.sb([128, 2, 1024], BF16, "ktokm")
    vext, vext_r = kb.sb([128, 2, 4, 257], BF16, "vext")
    wvm, wvm_r = kb.sb([128, 2, 4, 257], BF16, "wvm")
    sog, sog_r = kb.sb([128, 8, T], BF16, "sog")
    sgb, sgb_r = sigz, sigz_r
    pT, pT_r0 = kb.sb([128, 4, 64], BF16, "pT")
    pT_rh = [Res("pT%d" % h) for h in range(4)]
    hbuf, hbuf_r = kb.sb([64, 1024], BF16, "hbuf")
    hgT, hgT_r = kb.sb([128, 8, T], BF16, "hgT")
    sm1, sm1_r = kb.sb([64, 64], F32, "sm1")
    asb, asb_r = [], []
    for i in range(3):
        a_, r_ = kb.sb([64, 4, 257], BF16, "asb%d" % i)
        asb.append(a_); asb_r.append(r_)
    sm1s = [kb.sb([64, 32], F32, "sm1s%d" % i) for i in range(3)]
    bsts = [kb.sb([64, 4, 6], F32, "bsts%d" % i) for i in range(3)]
    hbufs = [(hbuf, hbuf_r), kb.sb([64, 1024], BF16, "hbuf2")]
    bst, bst_r = kb.sb([64, 4, 6], F32, "bst")
    x1T, x1T_r = qT, qT_r
    hff, hff_r = kb.sb([128, 16, T], BF16, "hff")
    rl = t1
    lsm, lsm_r = kb.sb([128, 16], F32, "lsm")
    lst, lst_r = kb.sb([128, 2, 2, 6], F32, "lst")
    xtok_rt = [Res("xtok_t0"), Res("xtok_t1")]
    lsm_rt = [Res("lsm0"), Res("lsm1")]
    lst_rt = [Res("lst0"), Res("lst1")]
    cch, cch_r = kb.sb([30, 1024], F32, "cch")
    cout, cout_r = kb.sb([30, 1024], F32, "cout")
    s_cch = Stream(nc, "cch")
    s_st = [Stream(nc, "state%d" % h) for h in range(4)]
    s_y = Stream(nc, "ystore")
    kb.op("dve", lambda e: e.memset(vext, 1.0), writes=[vext_r])
    kb.op("dve", lambda e: e.memset(glu, 0.0), writes=[glu_r])

    blk_groups = [2, 3, 0, 1, G_Q, G_Q + 1, G_K, G_K + 1, G_V, G_V + 1, G_O, G_O + 1, G_GB, G_GB + 1,
                  G_GA, G_GA + 1, G_A, G_A + 1, G_B, G_B + 1, G_OUT, G_OUT + 1] + [G_F1 + i for i in range(4)] + [G_F2 + i for i in range(4)] + [G_F1 + 4 + i for i in range(4)] + [G_F2 + 4 + i for i in range(4)]
    NBLK = NMAIN // T + 1
    sched = blk_groups * NBLK
    nxt = [0, 0]

    def ring_use():
        i = nxt[1]
        while nxt[0] < len(sched) and nxt[0] <= i + NW - 1:
            j = nxt[0]
            s = j % NW
            kb.dma("sp", s_ring[s], ring[s], wsc[sched[j]].rearrange("p (k n) -> p k n", k=8), reads=[R_wsc[sched[j]]], writes=[ring_r[s]])
            nxt[0] += 1
        nxt[1] += 1
        return ring[i % NW], ring_r[i % NW]

    bankrr = [0]

    def nbank():
        b = bankrr[0] % 4
        bankrr[0] += 1
        return b

    def proj_fm(wt, wt_r, act, act_r, n, evac, banks=None):
        for ocl in range(4):
            b = nbank() if banks is None else banks[ocl % len(banks)]
            for k in range(8):
                kb.op("pe", lambda e, k=k, b=b, ocl=ocl: e.matmul(pb[b][:, 0:n], lhsT=wt[:, k, ocl * 128:(ocl + 1) * 128], rhs=act[:, k, 0:n],
                                                                  start=(k == 0), stop=(k == 7)),
                      reads=[wt_r, act_r], writes=[pbr[b]])
            evac(ocl, pb[b][:, 0:n], pbr[b])

    def proj_tm(wt, wt_r, act, act_r, ntile, evac):
        for tt in range(ntile):
            b = nbank()
            for k in range(8):
                kb.op("pe", lambda e, k=k, b=b, tt=tt: e.matmul(pb[b][:, 0:512], lhsT=act[:, k, tt * 128:(tt + 1) * 128], rhs=wt[:, k, :],
                                                                start=(k == 0), stop=(k == 7)),
                      reads=[wt_r, act_r], writes=[pbr[b]])
            evac(tt, pb[b][:, 0:512], pbr[b])

    xw_loaded = [0]

    def load_xm(blk):
        i = blk % 2
        if blk < NMAIN // T:
            src = xTs[:, :, NPRE + blk * T:NPRE + (blk + 1) * T]
        else:
            src = xTm[:, :, 0:T]
        kb.dma("pool", s_xm[i], xTm_b[i], src, reads=[R_in], writes=[xTm_r[i]])

    load_xm(0)
    for h in range(4):
        kb.op("act", lambda e, h=h: e.activation(out=Cb[h], in_=Cst[h], func=AF.Identity), reads=Cst_rd[h], writes=[Cb_r[h]])

    for blk in range(NBLK):
        sample = blk == NBLK - 1
        nseg, L = (4, 64) if sample else (1, T)
        xt, xt_r = xTm_b[blk % 2], xTm_r[blk % 2]
        gl = glu[:, :, 0:nseg * (30 + L)].rearrange("p c (s l) -> p c s l", s=nseg)
        tok_lo = blk * T
        last_p = blk == NBLK - 2
        kb.dma("sp", s_xt, xtok, x_tok[tok_lo:tok_lo + T, :].rearrange("(t p) f -> p t f", p=128), reads=[R_in], writes=xtok_rt)

        for gi in (2, 3, 0, 1):
            wt, wt_r = ring_use()
            if gi >= 2:
                def ev(ocl, ps, ps_r, gi=gi):
                    c = (gi - 2) * 4 + ocl
                    kb.op("act", lambda e: e.activation(out=sigz[:, c, :], in_=ps, func=AF.Sigmoid), reads=[ps_r], writes=[sigz_r])
                proj_fm(wt, wt_r, xt, xt_r, T, ev)
                if blk == 0:
                    def evh(ocl, ps, ps_r, gi=gi):
                        c = (gi - 2) * 4 + ocl
                        kb.op("act", lambda e: e.activation(out=sigzh[:, c, :], in_=ps, func=AF.Sigmoid), reads=[ps_r], writes=[sigzh_r])
                    proj_fm(wt, wt_r, xTh, xTh_r, 128, evh)
            else:
                def ev(ocl, ps, ps_r, gi=gi):
                    c = gi * 4 + ocl
                    kb.op("dve", lambda e: e.tensor_tensor(out=gl[:, c, :, 30:30 + L], in0=ps.rearrange("p (s l) -> p s l", s=nseg),
                                                           in1=sigz[:, c, :].rearrange("p (s l) -> p s l", s=nseg), op=ALU.mult),
                          reads=[ps_r, sigz_r], writes=[glu_r])
                    if sample or last_p:
                        kb.op("dve", lambda e: e.tensor_tensor(out=glu32[:, c, 0:nseg, :],
                                                               in0=ps.rearrange("p (s l) -> p s l", s=nseg)[:, :, L - 30:L],
                                                               in1=sigz[:, c, :].rearrange("p (s l) -> p s l", s=nseg)[:, :, L - 30:L], op=ALU.mult),
                              reads=[ps_r, sigz_r], writes=[glu32_r])
                if blk == 0:
                    def evh(ocl, ps, ps_r, gi=gi):
                        c = gi * 4 + ocl
                        kb.op("dve", lambda e: e.tensor_tensor(out=gl[:, c, 0, 0:30], in0=ps[:, 98:128], in1=sigzh[:, c, 98:128], op=ALU.mult),
                              reads=[ps_r, sigzh_r], writes=[glu_r])
                    proj_fm(wt, wt_r, xTh, xTh_r, 128, evh)
                proj_fm(wt, wt_r, xt, xt_r, T, ev)
        if sample:
            for s in range(4):
                pc = pb[6][:, 0:240].rearrange("p (c r) -> p c r", r=30)
                kb.dma("sp", s_cch, cch, cache_c[s], reads=[R_in], writes=[cch_r])
                for c in range(8):
                    kb.op("pe", lambda e, s=s, c=c: e.transpose(out=pc[:, c, :], in_=cch[0:30, c * 128:(c + 1) * 128], identity=ident[0:30, 0:30]),
                          reads=[cch_r, ident_r], writes=[pbr[6]])
                kb.op("act", lambda e, s=s: e.activation(out=gl[:, :, s, 0:30], in_=pc, func=AF.Identity), reads=[pbr[6]], writes=[glu_r])
        elif blk > 0:
            kb.op("pool", lambda e: e.tensor_copy(out=gl[:, :, 0, 0:30], in_=hist), reads=[hist_r], writes=[glu_r])
        if not sample:
            kb.op("pool", lambda e: e.tensor_copy(out=hist, in_=gl[:, :, 0, L:L + 30]), reads=[glu_r], writes=[hist_r])
        if blk + 1 < NBLK:
            load_xm(blk + 1)
        if sample or last_p:
            for s in range(nseg):
                pc = pb[6][0:30, 0:512]
                pc2 = pb[5][0:30, 0:512]
                for c in range(8):
                    dst = (pc if c < 4 else pc2)[:, (c % 4) * 128:(c % 4 + 1) * 128]
                    kb.op("pe", lambda e, s=s, c=c, dst=dst: e.transpose(out=dst, in_=glu32[:, c, s, :], identity=ident),
                          reads=[glu32_r, ident_r], writes=[pbr[6], pbr[5]])
                kb.op("act", lambda e: e.activation(out=cout[:, 0:512], in_=pc, func=AF.Identity), reads=[pbr[6]], writes=[cout_r])
                kb.op("act", lambda e: e.activation(out=cout[:, 512:1024], in_=pc2, func=AF.Identity), reads=[pbr[5]], writes=[cout_r])
                kb.dma("sp", s_out, conv_s[s] if sample else conv_p, cout, reads=[cout_r], writes=[R_out])

        tile_g0 = (NPRE + blk * T) // 128 if not sample else 64
        for gi in range(2):
            wt, wt_r = ring_use()
            def ev(ocl, ps, ps_r, gi=gi):
                kb.op("act", lambda e: e.activation(out=qT[:, gi * 4 + ocl, :], in_=ps, func=AF.Identity), reads=[ps_r], writes=[qT_r])
            proj_fm(wt, wt_r, xt, xt_r, T, ev)
        for gi in range(2):
            wt, wt_r = ring_use()
            def ev(ocl, ps, ps_r, gi=gi):
                kb.op("act", lambda e: e.activation(out=kT[:, gi * 4 + ocl, :], in_=ps, func=AF.Identity, scale=0.0625), reads=[ps_r], writes=[kT_r])
            proj_fm(wt, wt_r, xt, xt_r, T, ev)
            def evt_(tt, ps, ps_r, gi=gi):
                kb.op("dve", lambda e: e.tensor_scalar(out=ktok[:, tt, gi * 512:(gi + 1) * 512], in0=ps, scalar1=0.0625, scalar2=None, op0=ALU.mult),
                      reads=[ps_r], writes=[ktok_r])
            proj_tm(wt, wt_r, xt, xt_r, T // 128, evt_)
        for gi in range(2):
            wt, wt_r = ring_use()
            def evt_(tt, ps, ps_r, gi=gi):
                p3 = ps.rearrange("p (h v) -> p h v", h=2)
                kb.op("act", lambda e: e.activation(out=vext[:, tt, 2 * gi:2 * gi + 2, 0:256], in_=p3, func=AF.Identity), reads=[ps_r], writes=[vext_r])
                kb.op("dve", lambda e: e.tensor_tensor(out=wvm[:, tt, 2 * gi:2 * gi + 2, 0:256], in0=p3,
                                                       in1=tokE[:, tile_g0 + tt, 1, 2 * gi:2 * gi + 2].unsqueeze(2).to_broadcast([128, 2, 256]), op=ALU.mult),
                      reads=[ps_r, tokE_r], writes=[wvm_r])
            proj_tm(wt, wt_r, xt, xt_r, T // 128, evt_)
        for tt in range(T // 128):
            kb.op("dve", lambda e, tt=tt: e.tensor_copy(out=wvm[:, tt, :, 256], in_=tokE[:, tile_g0 + tt, 1, :]), reads=[tokE_r], writes=[wvm_r])
        for gi in range(2):
            wt, wt_r = ring_use()
            def ev(ocl, ps, ps_r, gi=gi):
                oc = gi * 4 + ocl
                kb.op("act", lambda e: e.activation(out=sog[:, oc, :], in_=ps, func=AF.Sigmoid), reads=[ps_r], writes=[sog_r])
                kb.op("dve", lambda e: e.tensor_scalar(out=sog[:, oc, :], in0=sog[:, oc, :], scalar1=pv[:, 24 + oc:25 + oc], scalar2=None, op0=ALU.mult),
                      reads=[sog_r, pv_r], writes=[sog_r])
            proj_fm(wt, wt_r, xt, xt_r, T, ev)
        for gi in range(2):
            wt, wt_r = ring_use()
            def ev(ocl, ps, ps_r, gi=gi):
                kb.op("act", lambda e: e.activation(out=sgb[:, gi * 4 + ocl, :], in_=ps, func=AF.Sigmoid), reads=[ps_r], writes=[sgb_r])
            proj_fm(wt, wt_r, xt, xt_r, T, ev)

        pend = None
        for c in range(8):
            def build_diag(c2):
                dg, dg_r = diags[c2 % 2]
                kb.op("dve" if c2 % 2 == 0 else "pool", lambda e: e.tensor_tensor(out=dg, in0=identb.unsqueeze(1).to_broadcast([128, 31, 128]),
                                                       in1=pv[:, 32 + c2 * 31:32 + (c2 + 1) * 31].unsqueeze(2).to_broadcast([128, 31, 128]),
                                                       op=ALU.mult),
                      reads=[identb_r, pv_r], writes=[dg_r])
            diag, diag_r = diags[c % 2]
            if c == 0:
                build_diag(0)
            b = nbank()
            po = pb[b][:, 0:T].rearrange("p (s l) -> p s l", s=nseg)
            for j in range(31):
                kb.op("pe", lambda e, j=j, c=c, po=po, diag=diag: e.matmul(po, lhsT=diag[:, j, :], rhs=gl[:, c, :, j:j + L], start=(j == 0), stop=(j == 30)),
                      reads=[diag_r, glu_r], writes=[pbr[b]])
            if c + 1 < 8:
                build_diag(c + 1)
            i2 = c % 2
            kb.op("dve", lambda e, c=c, b=b: e.tensor_scalar(out=dw[:, c, :], in0=pb[b][:, 0:T], scalar1=pv[:, c:c + 1], scalar2=None, op0=ALU.add),
                  reads=[pbr[b], pv_r], writes=[dw_r])
            kb.op("act", lambda e, c=c, b=b, i2=i2: e.activation(out=sq[i2][0], in_=pb[b][:, 0:T], func=AF.Square, bias=pv[:, c:c + 1]),
                  reads=[pbr[b], pv_r], writes=[sq[i2][1]])
            kb.op("act", lambda e, c=c, b=b, i2=i2: e.activation(out=dwb[i2][0], in_=pb[b][:, 0:T], func=AF.Identity, bias=pv[:, c:c + 1]),
                  reads=[pbr[b], pv_r], writes=[dwb[i2][1]])

            def stats(c=c, i2=i2):
                kb.op("pe", lambda e: e.matmul(pb[4][:, 0:T], lhsT=onesb, rhs=dwb[i2][0], start=(c == 0), stop=(c == 7)),
                      reads=[onesb_r, dwb[i2][1]], writes=[pbr[4]])
                kb.op("pe", lambda e: e.matmul(pb[5][:, 0:T], lhsT=onesb, rhs=sq[i2][0], start=(c == 0), stop=(c == 7)),
                      reads=[onesb_r, sq[i2][1]], writes=[pbr[5]])
            if pend is not None:
                pend()
            pend = stats
        pend()
        def ln_finalize_a():
            kb.op("dve", lambda e: e.tensor_copy(out=mean_sb, in_=pb[4][:, 0:T]), reads=[pbr[4]], writes=[mean_r])
            kb.op("pool", lambda e: e.tensor_tensor(out=t1[0][0], in0=mean_sb, in1=mean_sb, op=ALU.mult), reads=[mean_r], writes=[t1[0][1]])
            kb.op("dve", lambda e: e.tensor_tensor(out=rstd, in0=pb[5][:, 0:T], in1=t1[0][0], op=ALU.subtract), reads=[pbr[5], t1[0][1]], writes=[rstd_r])
        def ln_finalize_b():
            kb.op("dve", lambda e: e.tensor_scalar(out=rstd, in0=rstd, scalar1=0.0, scalar2=EPS, op0=ALU.max, op1=ALU.add), reads=[rstd_r], writes=[rstd_r])
            kb.op("act", lambda e: e.activation(out=rstd, in_=rstd, func=AF.Sqrt), reads=[rstd_r], writes=[rstd_r])
            kb.op("dve", lambda e: e.reciprocal(out=rstd, in_=rstd), reads=[rstd_r], writes=[rstd_r])
            for c in range(8):
                i2 = c % 2
                kb.op("pool", lambda e, c=c, i2=i2: e.tensor_tensor(out=t1[i2][0], in0=dw[:, c, :], in1=mean_sb, op=ALU.subtract),
                      reads=[dw_r, mean_r], writes=[t1[i2][1]])
                kb.op("dve", lambda e, i2=i2: e.tensor_tensor(out=t2[i2][0], in0=t1[i2][0], in1=rstd, op=ALU.mult),
                      reads=[t1[i2][1], rstd_r], writes=[t2[i2][1]])
                kb.op("act", lambda e, c=c, i2=i2: e.activation(out=aT[:, c, :], in_=t2[i2][0], func=AF.Silu, scale=pv[:, 8 + c:9 + c], bias=pv[:, 16 + c:17 + c]),
                      reads=[t2[i2][1], pv_r], writes=[aT_r])
        def chunk_ids(cc):
            tt, half = cc // 2, cc % 2
            cg = (NPRE + blk * T) // 64 + cc if not sample else 128 + cc
            return tt, half, half * 64, cc * 64, cg, tile_g0 + tt

        def mlstm_crit(cc):
            tt, half, p0, tok0, cg, tg = chunk_ids(cc)
            ai = asb[cc % 3]
            ai_r = asb_r[cc % 3]
            if sample:
                for h in range(4):
                    kb.dma("sp", s_st[h], Cst[h][:, :, 0:256], sC[cc, h].rearrange("(c p) v -> p c v", p=128), reads=[R_in], writes=Cst_rd[h])
                    kb.dma("sp", s_st[h], Cst[h][:, :, 256], sn[cc, h].rearrange("(c p) -> p c", p=128), reads=[R_in], writes=Cst_rd[h],
                           allow_slow_non_contiguous=True)
                    kb.op("act", lambda e, h=h: e.activation(out=Cb[h], in_=Cst[h], func=AF.Identity), reads=Cst_rd[h], writes=[Cb_r[h]])
            if half == 0:
                for h in range(4):
                    for dkc in range(2):
                        kb.op("pe", lambda e, h=h, dkc=dkc: e.matmul(pb[4][:, h * 128:(h + 1) * 128], lhsT=kT[:, 2 * h + dkc, tt * 128:(tt + 1) * 128],
                                                                     rhs=qT[:, 2 * h + dkc, tt * 128:(tt + 1) * 128], start=(dkc == 0), stop=(dkc == 1)),
                              reads=[kT_r, qT_r], writes=[pbr[4]])
            for h in range(4):
                kb.op("dve", lambda e, h=h: e.scalar_tensor_tensor(out=pT[p0:p0 + 64, h, :], in0=pb[4][p0:p0 + 64, h * 128 + p0:h * 128 + p0 + 64],
                                                                   scalar=tokE[p0:p0 + 64, tg, 0, h:h + 1], in1=cmask[p0:p0 + 64, :],
                                                                   op0=ALU.mult, op1=ALU.mult),
                      reads=[pbr[4], tokE_r, cmask_r], writes=[pT_rh[h]])
            def upd(h):
                for dkc in range(2):
                    bi = dkc
                    kb.op("pe", lambda e, dkc=dkc, bi=bi: e.matmul(
                        pb[bi][:, 0:257], lhsT=ktok[p0:p0 + 64, tt, h * 256 + dkc * 128:h * 256 + dkc * 128 + 128],
                        rhs=wvm[p0:p0 + 64, tt, h, 0:257], start=True, stop=True),
                        reads=[ktok_r, wvm_r], writes=[pbr[bi]])
                    kb.op("dve", lambda e, dkc=dkc, bi=bi: e.scalar_tensor_tensor(
                        out=Cst[h][:, dkc, :], in0=Cst[h][:, dkc, :], scalar=decb[:, cg // 8, h, (cg % 8):(cg % 8) + 1],
                        in1=pb[bi][:, 0:257], op0=ALU.mult, op1=ALU.add),
                        reads=[Cst_rd[h][dkc], decb_r, pbr[bi]], writes=[Cst_rd[h][dkc]])

            for h in range(4):
                upd(h)
                nb = (5, 6, 3, 2)[h]
                on = pb[nb][0:64, 256:512] if h == 3 else pb[nb][0:64, 0:256]
                od = pb[2][0:64, h:h + 1]
                kb.op("pe", lambda e, h=h, on=on: e.matmul(on, lhsT=pT[p0:p0 + 64, h, :], rhs=vext[p0:p0 + 64, tt, h, 0:256], start=True, stop=False),
                      reads=[pT_rh[h], vext_r], writes=[pbr[nb]])
                for dkc in range(2):
                    kb.op("pe", lambda e, h=h, dkc=dkc, on=on: e.matmul(on, lhsT=qT[:, 2 * h + dkc, tok0:tok0 + 64], rhs=Cb[h][:, dkc, 0:256],
                                                                        start=False, stop=(dkc == 1)),
                          reads=[qT_r, Cb_r[h]], writes=[pbr[nb]])
                kb.op("pe", lambda e, h=h, od=od: e.matmul(od, lhsT=pT[p0:p0 + 64, h, :], rhs=vext[p0:p0 + 64, tt, h, 256:257], start=True, stop=False),
                      reads=[pT_rh[h], vext_r], writes=[pbr[2]])
                for dkc in range(2):
                    kb.op("pe", lambda e, h=h, dkc=dkc, od=od: e.matmul(od, lhsT=qT[:, 2 * h + dkc, tok0:tok0 + 64], rhs=Cb[h][:, dkc, 256:257],
                                                                        start=False, stop=(dkc == 1)),
                          reads=[qT_r, Cb_r[h]], writes=[pbr[2]])
                kb.op("act", lambda e, h=h, on=on: e.activation(out=ai[:, h, 0:256], in_=on, func=AF.Identity), reads=[pbr[nb]], writes=[ai_r])
                if sample:
                    kb.dma("sp", s_out, Cn_s[cc, h].rearrange("c p v -> p c v"), Cst[h], reads=Cst_rd[h], writes=[R_out])
                elif last_p and cc == T // 64 - 1:
                    kb.dma("sp", s_out, Cn_p[h].rearrange("c p v -> p c v"), Cst[h], reads=Cst_rd[h], writes=[R_out])
                if not (sample or (last_p and cc == T // 64 - 1)):
                    kb.op("act", lambda e, h=h: e.activation(out=Cb[h], in_=Cst[h], func=AF.Identity), reads=Cst_rd[h], writes=[Cb_r[h]])
            kb.op("act", lambda e: e.activation(out=ai[:, :, 256], in_=pb[2][0:64, 0:4], func=AF.Identity), reads=[pbr[2]], writes=[ai_r])

        def norm_a(cc):
            tt, half, p0, tok0, cg, tg = chunk_ids(cc)
            ai, ai_r = asb[cc % 3], asb_r[cc % 3]
            sm, sm_r, bs, bs_r = sm1s[cc % 3][0], sm1s[cc % 3][1], bsts[cc % 3][0], bsts[cc % 3][1]
            d_, d2_, vv_ = sm[:, 0:4], sm[:, 4:8], sm[:, 8:12]
            mv = sm[:, 24:32].rearrange("p (h x) -> p h x", x=2)
            for h in range(4):
                kb.op("dve", lambda e, h=h: e.bn_stats(out=bs[:, h, :], in_=ai[:, h, 0:256]), reads=[ai_r], writes=[bs_r])
                kb.op("dve", lambda e, h=h: e.bn_aggr(out=mv[:, h, :], in_=bs[:, h, :]), reads=[bs_r], writes=[sm_r])
            kb.op("dve", lambda e: e.tensor_tensor(out=d2_, in0=ai[:, :, 256], in1=ai[:, :, 256], op=ALU.mult), reads=[ai_r], writes=[sm_r])
            kb.op("dve", lambda e: e.tensor_tensor(out=d_, in0=floorC[:, cg, :], in1=floorC[:, cg, :], op=ALU.mult), reads=[sm_r, floorC_r], writes=[sm_r])
            kb.op("dve", lambda e: e.tensor_tensor(out=d2_, in0=d2_, in1=d_, op=ALU.max), reads=[sm_r], writes=[sm_r])
            kb.op("dve", lambda e: e.scalar_tensor_tensor(out=vv_, in0=d2_, scalar=EPS, in1=mv[:, :, 1], op0=ALU.mult, op1=ALU.add), reads=[sm_r], writes=[sm_r])
            kb.op("act", lambda e: e.activation(out=vv_, in_=vv_, func=AF.Sqrt), reads=[sm_r], writes=[sm_r])

        def norm_b(cc):
            ai, ai_r = asb[cc % 3], asb_r[cc % 3]
            sm, sm_r = sm1s[cc % 3][0], sm1s[cc % 3][1]
            hb, hb_r = hbufs[cc % 2]
            vv_, rs_, nm_ = sm[:, 8:12], sm[:, 12:16], sm[:, 16:20]
            mv = sm[:, 24:32].rearrange("p (h x) -> p h x", x=2)
            kb.op("dve", lambda e: e.reciprocal(out=rs_, in_=vv_), reads=[sm_r], writes=[sm_r])
            kb.op("dve", lambda e: e.scalar_tensor_tensor(out=nm_, in0=mv[:, :, 0], scalar=-1.0, in1=rs_, op0=ALU.mult, op1=ALU.mult), reads=[sm_r], writes=[sm_r])
            for h in range(4):
                kb.op("act", lambda e, h=h: e.activation(out=hb[:, h * 256:(h + 1) * 256], in_=ai[:, h, 0:256],
                                                         func=AF.Identity, scale=rs_[:, h:h + 1], bias=nm_[:, h:h + 1]),
                      reads=[ai_r, sm_r], writes=[hb_r])

        def norm_c(cc):
            tt, half, p0, tok0, cg, tg = chunk_ids(cc)
            hb, hb_r = hbufs[cc % 2]
            for fc in range(8):
                kb.op("pe", lambda e, fc=fc: e.transpose(out=pbh[:, fc * 64:(fc + 1) * 64], in_=hb[:, fc * 128:(fc + 1) * 128], identity=identb[0:64, 0:64]),
                      reads=[hb_r, identb_r], writes=[pbh_r])
            kb.op("dve", lambda e: e.tensor_tensor(out=hgT[:, :, tok0:tok0 + 64], in0=pbh[:, 0:512].rearrange("p (c t) -> p c t", t=64),
                                                   in1=sog[:, :, tok0:tok0 + 64], op=ALU.mult),
                  reads=[pbh_r, sog_r], writes=[hgT_r])

        ln_finalize_a()
        NCH = T // 64
        for it in range(NCH + 3):
            if it < NCH:
                mlstm_crit(it)
            if 0 <= it - 1 < NCH:
                norm_a(it - 1)
            if 0 <= it - 2 < NCH:
                norm_b(it - 2)
            if 0 <= it - 3 < NCH:
                norm_c(it - 3)
        ln_finalize_b()
        for gi in range(2):
            wt, wt_r = ring_use()
            def ev(ocl, ps, ps_r, gi=gi):
                kb.op("act", lambda e: e.activation(out=sga[:, gi * 4 + ocl, :], in_=ps, func=AF.Sigmoid), reads=[ps_r], writes=[sga_r])
            proj_fm(wt, wt_r, xt, xt_r, T, ev)
        for gi in range(2):
            wt, wt_r = ring_use()
            def ev(ocl, ps, ps_r, gi=gi):
                oc = gi * 4 + ocl
                kb.op("dve", lambda e: e.tensor_tensor(out=mrg[:, oc, :], in0=ps, in1=sga[:, oc, :], op=ALU.mult), reads=[ps_r, sga_r], writes=[mrg_r])
            proj_fm(wt, wt_r, aT, aT_r, T, ev)

        for gi in range(2):
            wt, wt_r = ring_use()
            def ev(ocl, ps, ps_r, gi=gi):
                oc = gi * 4 + ocl
                i2 = oc % 2
                kb.op("dve", lambda e: e.tensor_tensor(out=t2[i2][0], in0=ps, in1=sgb[:, oc, :], op=ALU.mult), reads=[ps_r, sgb_r], writes=[t2[i2][1]])
                kb.op("pool", lambda e: e.tensor_tensor(out=mrgb[:, oc, :], in0=mrg[:, oc, :], in1=t2[i2][0], op=ALU.add),
                      reads=[mrg_r, t2[i2][1]], writes=[mrgb_r])
            proj_fm(wt, wt_r, hgT, hgT_r, T, ev)

        def layernorm_tok(tt, gi, bi_):
            xs = xtok[:, tt, :]
            ls = lsm[:, tt * 8:tt * 8 + 8]
            for j in range(2):
                kb.op("dve", lambda e, j=j: e.bn_stats(out=lst[:, tt, j, :], in_=xs[:, j * 512:(j + 1) * 512]), reads=[xtok_rt[tt]], writes=[lst_rt[tt]])
            kb.op("dve", lambda e: e.bn_aggr(out=ls[:, 0:2], in_=lst[:, tt, :, :].rearrange("p a b -> p (a b)")), reads=[lst_rt[tt]], writes=[lsm_rt[tt]])
            kb.op("dve", lambda e: e.tensor_scalar(out=ls[:, 2:3], in0=ls[:, 1:2], scalar1=EPS, scalar2=None, op0=ALU.add), reads=[lsm_rt[tt]], writes=[lsm_rt[tt]])
            kb.op("act", lambda e: e.activation(out=ls[:, 2:3], in_=ls[:, 2:3], func=AF.Sqrt), reads=[lsm_rt[tt]], writes=[lsm_rt[tt]])
            kb.op("dve", lambda e: e.reciprocal(out=ls[:, 3:4], in_=ls[:, 2:3]), reads=[lsm_rt[tt]], writes=[lsm_rt[tt]])
            kb.op("dve", lambda e: e.scalar_tensor_tensor(out=ls[:, 4:5], in0=ls[:, 0:1], scalar=-1.0, in1=ls[:, 3:4], op0=ALU.mult, op1=ALU.mult),
                  reads=[lsm_rt[tt]], writes=[lsm_rt[tt]])
            kb.op("act", lambda e: e.activation(out=xs, in_=xs, func=AF.Identity, scale=ls[:, 3:4], bias=ls[:, 4:5]),
                  reads=[xtok_rt[tt], lsm_rt[tt]], writes=[xtok_rt[tt]])
            kb.op("dve", lambda e: e.tensor_tensor(out=xs, in0=xs, in1=lnbc[:, gi, :], op=ALU.mult), reads=[xtok_rt[tt], lnbc_r], writes=[xtok_rt[tt]])
            kb.op("dve", lambda e: e.tensor_tensor(out=xs, in0=xs, in1=lnbc[:, bi_, :], op=ALU.add), reads=[xtok_rt[tt], lnbc_r], writes=[xtok_rt[tt]])

        for gi in range(2):
            wt, wt_r = ring_use()
            def evt_(tt, ps, ps_r, gi=gi):
                kb.op("dve", lambda e: e.scalar_tensor_tensor(out=xtok[:, tt, gi * 512:(gi + 1) * 512], in0=xtok[:, tt, gi * 512:(gi + 1) * 512],
                                                              scalar=ALPHA, in1=ps, op0=ALU.mult, op1=ALU.add),
                      reads=[ps_r, xtok_rt[tt]], writes=[xtok_rt[tt]])
            proj_tm(wt, wt_r, mrgb, mrgb_r, T // 128, evt_)
        for tt in range(T // 128):
            layernorm_tok(tt, 0, 1)
            for hb in range(2):
                bnk = 4 + hb
                for f4 in range(4):
                    fc = hb * 4 + f4
                    kb.op("pe", lambda e, fc=fc, f4=f4, bnk=bnk: e.transpose(out=pb[bnk][:, f4 * 128:(f4 + 1) * 128], in_=xtok[:, tt, fc * 128:(fc + 1) * 128],
                                                                             identity=ident),
                          reads=[xtok_rt[tt], ident_r], writes=[pbr[bnk]])
                kb.op("act", lambda e, hb=hb, bnk=bnk: e.activation(out=x1T[:, hb * 4:hb * 4 + 4, tt * 128:(tt + 1) * 128],
                                                                    in_=pb[bnk][:, 0:512].rearrange("p (c t) -> p c t", t=128), func=AF.Identity),
                      reads=[pbr[bnk]], writes=[x1T_r])
        for hh in range(2):
            for gi in range(4):
                wt, wt_r = ring_use()
                def ev(ocl, ps, ps_r, gi=gi):
                    oc = gi * 4 + ocl
                    i2 = oc % 2
                    kb.op("act", lambda e: e.activation(out=rl[i2][0], in_=ps, func=AF.Relu), reads=[ps_r], writes=[rl[i2][1]])
                    kb.op("pool", lambda e: e.tensor_tensor(out=hff[:, oc, :], in0=rl[i2][0], in1=rl[i2][0], op=ALU.mult), reads=[rl[i2][1]], writes=[hff_r])
                proj_fm(wt, wt_r, x1T, x1T_r, T, ev, banks=[4, 5, 6])
            for gi in range(4):
                wt, wt_r = ring_use()
                w4 = wt.rearrange("p a n -> p (a n)").rearrange("p (k n) -> p k n", k=4)
                for tt in range(T // 128):
                    for hc in range(2):
                        b = tt * 2 + hc
                        for k4 in range(4):
                            kl = gi * 4 + k4
                            kk = hh * 16 + kl
                            kb.op("pe", lambda e, k4=k4, kl=kl, kk=kk, b=b, tt=tt, hc=hc, w4=w4: e.matmul(
                                pb[b][:, 0:512], lhsT=hff[:, kl, tt * 128:(tt + 1) * 128], rhs=w4[:, k4, hc * 512:(hc + 1) * 512],
                                start=(kk == 0), stop=(kk == 31)),
                                reads=[hff_r, wt_r], writes=[pbr[b]])
        for tt in range(T // 128):
            for hc in range(2):
                b = tt * 2 + hc
                kb.op("dve", lambda e, b=b, tt=tt, hc=hc: e.scalar_tensor_tensor(
                    out=xtok[:, tt, hc * 512:(hc + 1) * 512], in0=xtok[:, tt, hc * 512:(hc + 1) * 512], scalar=ALPHA, in1=pb[b][:, 0:512],
                    op0=ALU.mult, op1=ALU.add), reads=[pbr[b], xtok_rt[tt]], writes=[xtok_rt[tt]])
        for tt in range(T // 128):
            layernorm_tok(tt, 2, 3)
        kb.dma("sp", s_y, y_d[tok_lo:tok_lo + T, :].rearrange("(t p) f -> p t f", p=128), xtok, reads=xtok_rt, writes=[R_out])

    nc.sync.wait_ge(s_out.sem, s_out.count)
    nc.sync.wait_ge(s_y.sem, s_y.count)
    return nc


_CACHE = {}


def kernel(**inp):
    f = lambda k: np.ascontiguousarray(np.asarray(inp[k], dtype=np.float32))
    xp, xs = f("x_prompt"), f("x_sample")
    cc_, sC_, sn_, sm_ = f("cache_conv")[0], f("state_C")[0], f("state_n")[0], f("state_m")[0]
    w_in, bg, w_dw, b_dw = f("w_in")[0], f("b_gate")[0], f("w_dw")[0], f("b_dw")[0]
    pv = np.zeros((128, NPV), np.float32)
    for i, v in enumerate((b_dw, f("ln_a_g")[0], f("ln_a_b")[0], f("hn_g")[0])):
        pv[:, 8 * i:8 * i + 8] = v.reshape(8, 128).T
    pv[:, 32:] = w_dw.reshape(31, 8, 128).transpose(2, 1, 0).reshape(128, 248)
    gbias = np.stack([bg[0:4], bg[4:8]], axis=1).astype(np.float32)
    lnrows = np.stack([f("ln1_g")[0], f("ln1_b")[0], f("ln2_g")[0], f("ln2_b")[0]]).astype(np.float32)
    ident = np.eye(128, dtype=np.float32)
    cm = (np.arange(64)[:, None] <= np.arange(64)[None, :]).astype(np.float32)
    cmask2 = np.concatenate([cm, cm], axis=0)
    resetm = np.ones((4, 512), np.float32)
    resetm[:, ::64] = 0
    oh = np.zeros((4, 4, 8), np.float32)
    for h in range(4):
        oh[h, h, :] = 1
    shared = {"w_in": w_in, "w_a": f("w_a_out")[0], "w_b": f("w_b_out")[0], "w_o": f("w_out")[0],
              "w_f1": f("w_ff1")[0], "w_f2": f("w_ff2")[0], "pv": pv, "gbias": gbias, "lnrows": lnrows,
              "ident": ident, "cmask2": cmask2, "resetm": resetm, "oh": oh}
    in_maps = []
    for c in range(8):
        b, q = c // 4, c % 4
        npad = NPRE - NMAIN * q
        xT_seq = np.zeros((1024, NSEQ), np.float32)
        xT_seq[:, npad:] = xp[b, :NMAIN * (q + 1)].T
        valid = np.zeros((4, NSEQ), np.float32)
        valid[:, npad:] = 1.0
        nbig = np.where(valid > 0, 0.0, -30000.0).astype(np.float32)
        xsm = xs[4 * c:4 * c + 4].reshape(NSMP, 1024)
        m = dict(shared)
        m.update({"xT_seq": xT_seq, "xT_smp": np.ascontiguousarray(xsm.T),
                  "x_tok": np.ascontiguousarray(np.concatenate([xp[b, NMAIN * q:NMAIN * (q + 1)], xsm], axis=0)),
                  "valid": valid, "nbig": nbig, "cache_c": np.ascontiguousarray(cc_[4 * c:4 * c + 4]),
                  "sC": np.ascontiguousarray(sC_[4 * c:4 * c + 4]), "sn": np.ascontiguousarray(sn_[4 * c:4 * c + 4]),
                  "sm": np.ascontiguousarray(sm_[4 * c:4 * c + 4].T)})
        in_maps.append(m)
    if inp.get("_only_maps"):
        return in_maps
    if "nc" not in _CACHE:
        _CACHE["nc"] = build()
    res = run_bass_kernel_spmd(_CACHE["nc"], in_maps, core_ids=list(range(8)))
    R = res.results
    yp = np.zeros((2, 8192, 1024), np.float32)
    ys = np.zeros((32, 64, 1024), np.float32)
    conv_p = np.zeros((1, 2, 30, 1024), np.float32)
    C_p = np.zeros((1, 2, 4, 256, 256), np.float32)
    n_p = np.zeros((1, 2, 4, 256), np.float32)
    m_p = np.zeros((1, 2, 4), np.float32)
    conv_s = np.zeros((1, 32, 30, 1024), np.float32)
    C_s = np.zeros((1, 32, 4, 256, 256), np.float32)
    n_s = np.zeros((1, 32, 4, 256), np.float32)
    m_s = np.zeros((1, 32, 4), np.float32)
    for c in range(8):
        b, q = c // 4, c % 4
        r = R[c]
        yp[b, NMAIN * q:NMAIN * (q + 1)] = r["y"][:NMAIN]
        ys[4 * c:4 * c + 4] = r["y"][NMAIN:].reshape(4, 64, 1024)
        if q == 3:
            conv_p[0, b] = r["conv_p"]
            cn = r["Cn_p"].reshape(4, 256, 257)
            C_p[0, b] = cn[:, :, :256]
            n_p[0, b] = cn[:, :, 256]
            m_p[0, b] = r["m_p"][:, 0]
        conv_s[0, 4 * c:4 * c + 4] = r["conv_s"]
        cn = r["Cn_s"].reshape(4, 4, 256, 257)
        C_s[0, 4 * c:4 * c + 4] = cn[:, :, :, :256]
        n_s[0, 4 * c:4 * c + 4] = cn[:, :, :, 256]
        m_s[0, 4 * c:4 * c + 4] = r["m_s"].T
    return (yp, ys, conv_p, C_p, n_p, m_p, conv_s, C_s, n_s, m_s)
```

```python
import numpy as np
import concourse.bass as bass
import concourse.mybir as mybir
from concourse.bass_utils import run_bass_kernel_spmd

F32, BF16 = mybir.dt.float32, mybir.dt.bfloat16
AF = mybir.ActivationFunctionType
ALU = mybir.AluOpType
AX = mybir.AxisListType

NSEQ, NPRE, NMAIN, NSMP = 8192, 6144, 2048, 256
T = 256
ALPHA = 2.0 ** 0.25
EPS = 1e-5
NPV = 32 + 8 * 31
NW = 4
G_CONV, G_Q, G_K, G_V, G_O, G_GA, G_GB = 0, 4, 6, 8, 10, 12, 14
G_A, G_B, G_OUT, G_F1, G_F2 = 16, 18, 20, 22, 30
NG = 38
WIN_COL = [0, 512, 1024, 1536, 2048, 2560, 3072, 3584, 4096, 4608, 5120, 5632, 6152, 6664, 7176, 7688]


class Res:
    __slots__ = ("name", "w", "r", "excl", "dj")

    def __init__(self, name, excl=False):
        self.name = name
        self.w = None
        self.r = {}
        self.excl = excl
        self.dj = False


class Stream:
    def __init__(self, nc, name):
        self.name = name
        self.sem = nc.alloc_semaphore("ds_" + name)
        self.count = 0


class KB:
    def __init__(self, nc):
        self.nc = nc
        self.eng = {"pe": nc.tensor, "act": nc.scalar, "dve": nc.vector, "pool": nc.gpsimd, "sp": nc.sync}
        self.sem = {e: nc.alloc_semaphore("sem_" + e) for e in ("pe", "act", "dve", "pool")}
        self.seq = {e: 0 for e in self.sem}
        self.waited = {}
        self.nbuf = 0

    def sb(self, shape, dt=F32, name=None):
        self.nbuf += 1
        nm = "s_" + (name or ("b%d" % self.nbuf))
        t = self.nc.alloc_sbuf_tensor(nm, list(shape), dt).ap()
        return t, Res(nm)

    def _wait(self, eng, sem, val):
        key = (eng, sem.num)
        if self.waited.get(key, 0) >= val:
            return
        self.waited[key] = val
        self.eng[eng].wait_ge(sem, val)

    def _deps(self, eng, reads, writes):
        toks = []
        for r in reads:
            if r.w is not None:
                toks.append((r.w, "raw", False))
            if r.excl:
                for t in r.r.values():
                    toks.append((t, "war", False))
        for w in writes:
            if w.w is not None:
                toks.append((w.w, "waw", w.dj))
            for t in w.r.values():
                toks.append((t, "war", w.dj))
        for (tok, kind, dj) in toks:
            sem, val, teng = tok
            if teng == eng:
                if eng == "pe":
                    continue
                if dj and kind != "raw":
                    continue
            self._wait(eng, sem, val)

    def op(self, eng, fn, reads=(), writes=()):
        self._deps(eng, reads, writes)
        ins = fn(self.eng[eng])
        self.seq[eng] += 1
        ins.then_inc(self.sem[eng], 1)
        tok = (self.sem[eng], self.seq[eng], eng)
        for r in reads:
            r.r[eng] = tok
        for w in writes:
            w.w = tok
            w.r = {}
        return tok

    def dma(self, q, stream, out, in_, reads=(), writes=(), **kw):
        self._deps(q, reads, writes)
        ins = self.eng[q].dma_start(out=out, in_=in_, **kw)
        stream.count += 16
        ins.then_inc(stream.sem, 16)
        tok = (stream.sem, stream.count, "dma:" + stream.name)
        for r in reads:
            r.r["dma:" + stream.name] = tok
        for w in writes:
            w.w = tok
            w.r = {}
        return tok


def build():
    nc = bass.Bass("TRN2", target_bir_lowering=False)
    kb = KB(nc)

    def din(name, shape):
        return nc.dram_tensor(name, list(shape), F32, kind="ExternalInput").ap()

    def dout(name, shape):
        return nc.dram_tensor(name, list(shape), F32, kind="ExternalOutput").ap()

    xT_seq = din("xT_seq", [1024, NSEQ])
    xT_smp = din("xT_smp", [1024, NSMP])
    x_tok = din("x_tok", [NMAIN + NSMP, 1024])
    valid_d = din("valid", [4, NSEQ])
    nbig_d = din("nbig", [4, NSEQ])
    cache_c = din("cache_c", [4, 30, 1024])
    sC = din("sC", [4, 4, 256, 256])
    sn = din("sn", [4, 4, 256])
    sm = din("sm", [4, 4])
    w_in = din("w_in", [1024, 8200])
    w_a = din("w_a", [1024, 1024])
    w_b = din("w_b", [1024, 1024])
    w_o = din("w_o", [1024, 1024])
    w_f1 = din("w_f1", [1024, 4096])
    w_f2 = din("w_f2", [4096, 1024])
    pv_d = din("pv", [128, NPV])
    gb_d = din("gbias", [4, 2])
    lnrows = din("lnrows", [4, 1024])
    ident_d = din("ident", [128, 128])
    cmask_d = din("cmask2", [128, 64])
    reset_d = din("resetm", [4, 512])
    oh_d = din("oh", [4, 4, 8])
    wsc = nc.dram_tensor("wsc", [NG, 128, 4096], BF16, kind="Internal").ap()
    y_d = dout("y", [NMAIN + NSMP, 1024])
    conv_p = dout("conv_p", [30, 1024])
    Cn_p = dout("Cn_p", [4, 2, 128, 257])
    m_p = dout("m_p", [4, 1])
    conv_s = dout("conv_s", [4, 30, 1024])
    Cn_s = dout("Cn_s", [4, 4, 2, 128, 257])
    m_s = dout("m_s", [4, 4])

    R_in = Res("dram_in")
    R_wsc = [Res("wsc%d" % g) for g in range(NG)]
    R_out = Res("dram_out")
    s_const = Stream(nc, "const")
    s_conv = [Stream(nc, "wcv%d" % g) for g in range(NG)]
    s_out = Stream(nc, "out")

    pb, pbr = [], []
    for i in range(7):
        pb.append(nc.alloc_psum_tensor("pb%d" % i, [128, 512], F32).ap())
        pbr.append(Res("pb%d" % i, True))
    pbh = nc.alloc_psum_tensor("pbh", [128, 1024], BF16).ap()
    pbh_r = Res("pbh", True)

    ident, ident_r = kb.sb([128, 128], F32, "ident")
    identb, identb_r = kb.sb([128, 128], BF16, "identb")
    onesb, onesb_r = kb.sb([128, 128], BF16, "onesb")
    ones4, ones4_r = kb.sb([4, 128], F32, "ones4")
    cmask, cmask_r = kb.sb([128, 64], F32, "cmask")
    resetm, resetm_r = kb.sb([4, 512], F32, "resetm")
    oh, oh_r = kb.sb([4, 4, 8], F32, "oh")
    pv, pv_r = kb.sb([128, NPV], F32, "pv")
    gbt, gbt_r = kb.sb([4, 2], F32, "gbt")
    nbf, nbf_r = kb.sb([4, 1], F32, "nbf")
    lnbc, lnbc_r = kb.sb([128, 4, 1024], F32, "lnbc")
    for (dst, src, rr) in ((ident, ident_d, ident_r), (cmask, cmask_d, cmask_r), (resetm, reset_d, resetm_r),
                           (oh, oh_d, oh_r), (pv, pv_d, pv_r), (gbt, gb_d, gbt_r)):
        kb.dma("sp", s_const, dst, src, reads=[R_in], writes=[rr])
    for i in range(4):
        kb.dma("sp", s_const, lnbc[:, i, :], lnrows[i:i + 1, :].partition_broadcast(128), reads=[R_in], writes=[lnbc_r])
    tok_all = (s_const.sem, s_const.count, "dma:const")
    for rr in (ident_r, cmask_r, resetm_r, oh_r, pv_r, gbt_r, lnbc_r):
        rr.w = tok_all
    kb.op("dve", lambda e: e.tensor_copy(out=identb, in_=ident), reads=[ident_r], writes=[identb_r])
    kb.op("dve", lambda e: e.memset(onesb, 1.0 / 1024.0), writes=[onesb_r])
    kb.op("dve", lambda e: e.memset(ones4, 1.0), writes=[ones4_r])
    kb.op("dve", lambda e: e.tensor_scalar(out=nbf, in0=gbt[:, 1:2], scalar1=-1.0, scalar2=None, op0=ALU.mult),
          reads=[gbt_r], writes=[nbf_r])

    tokE, tokE_r = kb.sb([128, 66, 2, 4], F32, "tokE")
    floorC, floorC_r = kb.sb([64, 132, 4], F32, "floorC")
    decb, decb_r = kb.sb([128, 17, 4, 8], F32, "decb")
    m_all, m_all_r = kb.sb([4, 129], F32, "m_all")
    msin, msin_r = kb.sb([4, 4], F32, "msin")
    mend_s, mend_s_r = kb.sb([4, 4], F32, "mend_s")
    Cst, Cst_r, Cb, Cb_r, Cst_rd = [], [], [], [], []
    for h in range(4):
        a, r = kb.sb([128, 2, 257], F32, "Cst%d" % h)
        Cst.append(a); Cst_r.append(r); Cst_rd.append([Res("Cst%d_0" % h), Res("Cst%d_1" % h)])
        a, r = kb.sb([128, 2, 257], BF16, "Cb%d" % h)
        Cb.append(a); Cb_r.append(r)
        kb.op("dve", lambda e, a=Cst[h]: e.memset(a, 0.0), writes=Cst_rd[h])
    kb.op("dve", lambda e: e.memset(m_all, 0.0), writes=[m_all_r])
    s_msin = Stream(nc, "msin")
    kb.dma("sp", s_msin, msin, sm, reads=[R_in], writes=[msin_r])
    xTh, xTh_r = kb.sb([128, 8, 128], BF16, "xTh")

    with nc.sbuf_tensor("p_wkv", [128, 8, 2048], BF16) as wkv_t, \
            nc.sbuf_tensor("p_wg", [128, 8, 8], BF16) as wg_t, \
            nc.sbuf_tensor("xTb0", [128, 8, 512], BF16) as xTb0_t, \
            nc.sbuf_tensor("xTb1", [128, 8, 512], BF16) as xTb1_t, \
            nc.sbuf_tensor("xTb2", [128, 8, 512], BF16) as xTb2_t, \
            nc.sbuf_tensor("xTb3", [128, 8, 512], BF16) as xTb3_t, \
            nc.sbuf_tensor("gt", [4, 12, 512], F32) as gt_t, \
            nc.sbuf_tensor("E3", [68, 512], F32) as E3_t, \
            nc.sbuf_tensor("gsm", [4, 8, 8], F32) as gsm_t, \
            nc.sbuf_tensor("drhs", [4, 4, 8], F32) as drhs_t, \
            nc.sbuf_tensor("ktok", [128, 2, 1024], BF16) as ktok_t, \
            nc.sbuf_tensor("wvp", [128, 2, 4, 257], BF16) as wvp_t:
        wkv, wg = wkv_t.ap(), wg_t.ap()
        xTb = [xTb0_t.ap(), xTb1_t.ap(), xTb2_t.ap(), xTb3_t.ap()]
        gt, E3, gsm, drhs, ktokp, wvp = gt_t.ap(), E3_t.ap(), gsm_t.ap(), drhs_t.ap(), ktok_t.ap(), wvp_t.ap()
        wkv_r, wg_r = Res("wkv"), Res("wg")
        xTb_r = [Res("xTb%d" % i) for i in range(4)]
        gt_r = [Res("gt%d" % i) for i in range(12)]
        E3_r, drhs_r = Res("E3"), Res("drhs")
        gsm_r = [Res("gsm%d" % i) for i in range(8)]
        ktokp_r = [Res("ktokp0"), Res("ktokp1")]
        wvp_r = [Res("wvp0"), Res("wvp1")]
        s_pw = Stream(nc, "pw")
        s_x = [Stream(nc, "xTb%d" % i) for i in range(4)]
        s_vv = [Stream(nc, "vblk0"), Stream(nc, "vblk1")]
        s_vn = [Stream(nc, "nblk0"), Stream(nc, "nblk1")]

        w_in_k = w_in.rearrange("(k p) n -> p k n", p=128)
        kb.dma("pool", s_pw, wg, w_in_k[:, :, 6144:6152], reads=[R_in], writes=[wg_r])
        kb.dma("pool", s_pw, wkv[:, :, 0:1024], w_in_k[:, :, 3072:4096], reads=[R_in], writes=[wkv_r])
        kb.dma("pool", s_pw, wkv[:, :, 1024:2048], w_in_k[:, :, 4096:5120], reads=[R_in], writes=[wkv_r])
        tok_pw = (s_pw.sem, s_pw.count, "dma:pw")
        wg_r.w = tok_pw
        wkv_r.w = tok_pw
        kb.op("dve", lambda e: e.memset(E3, 0.0), writes=[E3_r])

        xTs = xT_seq.rearrange("(k p) t -> p k t", p=128)
        xTm = xT_smp.rearrange("(k p) t -> p k t", p=128)

        xsrcs = [(xTs[:, :, b_ * 512:(b_ + 1) * 512], 512) for b_ in range(16)] + [(xTm[:, :, 0:NSMP], NSMP)]
        xissued = [0]

        def xget(i):
            while xissued[0] < len(xsrcs) and xissued[0] <= i + 1:
                j = xissued[0]
                src_, n_ = xsrcs[j]
                kb.dma("pool", s_x[j % 4], xTb[j % 4][:, :, 0:n_], src_, reads=[R_in], writes=[xTb_r[j % 4]])
                xissued[0] += 1
            return xTb[i % 4], xTb_r[i % 4]

        def conv_dma(g, src):
            kb.dma("pool", s_conv[g], wsc[g].rearrange("p (k n) -> p k n", k=src.shape[1]), src, reads=[R_in], writes=[R_wsc[g]])

        conv_jobs = []
        for gi, c0 in enumerate(WIN_COL):
            conv_jobs.append((gi, w_in_k[:, :, c0:c0 + 512]))
        for (g0, wd) in ((G_A, w_a), (G_B, w_b), (G_OUT, w_o)):
            wk_ = wd.rearrange("(k p) n -> p k n", p=128)
            for i in range(2):
                conv_jobs.append((g0 + i, wk_[:, :, i * 512:(i + 1) * 512]))
        wk_ = w_f1.rearrange("(k p) n -> p k n", p=128)
        for i in range(8):
            conv_jobs.append((G_F1 + i, wk_[:, :, i * 512:(i + 1) * 512]))
        wk_ = w_f2.rearrange("(k p) n -> p k n", p=128)
        for i in range(8):
            conv_jobs.append((G_F2 + i, wk_[:, 4 * i:4 * i + 4, :]))

        use_order = [2, 3, 0, 1, G_Q, G_Q + 1, G_K, G_K + 1, G_V, G_V + 1, G_O, G_O + 1, G_GB, G_GB + 1,
                     G_GA, G_GA + 1, G_A, G_A + 1, G_B, G_B + 1, G_OUT, G_OUT + 1] + [G_F1 + i for i in range(4)] + \
                    [G_F2 + i for i in range(4)] + [G_F1 + 4 + i for i in range(4)] + [G_F2 + 4 + i for i in range(4)]
        conv_jobs.sort(key=lambda j_: use_order.index(j_[0]))
        def gate_block(blk, n, chunk0, tile0, sample, grouped):
            nch = n // 64
            xt, xt_r = xget(blk)
            vs, ns_ = (5, 6) if blk % 2 == 0 else (7, 8)
            for _ in range(1):
                if conv_jobs:
                    conv_dma(*conv_jobs.pop(0))
            def load_mask(b_):
                v_, n_2 = (5, 6) if b_ % 2 == 0 else (7, 8)
                kb.dma("sp", s_vv[b_ % 2], gt[:, v_, 0:512], valid_d[:, b_ * 512:b_ * 512 + 512], reads=[R_in], writes=[gt_r[v_]])
                kb.dma("sp", s_vn[b_ % 2], gt[:, n_2, 0:512], nbig_d[:, b_ * 512:b_ * 512 + 512], reads=[R_in], writes=[gt_r[n_2]])
            if blk == 0:
                load_mask(0)
            if blk + 1 < 16:
                load_mask(blk + 1)
            zi, zf = pb[4], pb[5]
            for k in range(8):
                kb.op("pe", lambda e, k=k: e.matmul(zi[0:4, 0:n], lhsT=wg[:, k, 0:4], rhs=xt[:, k, 0:n], start=(k == 0), stop=(k == 7)),
                      reads=[wg_r, xt_r], writes=[pbr[4]])
            for k in range(8):
                kb.op("pe", lambda e, k=k: e.matmul(zf[0:4, 0:n], lhsT=wg[:, k, 4:8], rhs=xt[:, k, 0:n], start=(k == 0), stop=(k == 7)),
                      reads=[wg_r, xt_r], writes=[pbr[5]])
            s1_, s2_ = (1, 2) if blk % 2 == 0 else (9, 10)
            te, nlfm, igm, bneg, g = gt[:, 0, 0:n], gt[:, s1_, 0:n], gt[:, s2_, 0:n], gt[:, 3, 0:n], gt[:, 4, 0:n]
            kb.op("act", lambda e: e.activation(out=te, in_=zf[0:4, 0:n], func=AF.Exp, scale=-1.0, bias=nbf[:, 0:1]),
                  reads=[pbr[5], nbf_r], writes=[gt_r[0]])
            kb.op("act", lambda e: e.activation(out=te, in_=te, func=AF.Ln, bias=1.0), reads=[gt_r[0]], writes=[gt_r[0]])
            if not sample:
                kb.op("dve", lambda e: e.tensor_tensor(out=nlfm, in0=te, in1=gt[:, vs, 0:n], op=ALU.mult),
                      reads=[gt_r[0], gt_r[vs]], writes=[gt_r[s1_]])
                kb.op("dve", lambda e: e.scalar_tensor_tensor(out=igm, in0=zi[0:4, 0:n], scalar=gbt[:, 0:1], in1=gt[:, vs, 0:n],
                                                              op0=ALU.add, op1=ALU.mult),
                      reads=[pbr[4], gbt_r, gt_r[vs]], writes=[gt_r[s2_]])
                kb.op("dve", lambda e: e.tensor_tensor(out=igm, in0=igm, in1=gt[:, ns_, 0:n], op=ALU.add),
                      reads=[gt_r[s2_], gt_r[ns_]], writes=[gt_r[s2_]])
            else:
                kb.op("dve", lambda e: e.tensor_copy(out=nlfm, in_=te), reads=[gt_r[0]], writes=[gt_r[s1_]])
                kb.op("dve", lambda e: e.tensor_scalar(out=igm, in0=zi[0:4, 0:n], scalar1=gbt[:, 0:1], scalar2=None, op0=ALU.add),
                      reads=[pbr[4], gbt_r], writes=[gt_r[s2_]])
            yield
            kb.op("dve", lambda e: e.tensor_tensor_scan(out=bneg, data0=resetm[:, 0:n], data1=nlfm, initial=0.0,
                                                        op0=ALU.mult, op1=ALU.add),
                  reads=[resetm_r, gt_r[s1_]], writes=[gt_r[3]])
            yield
            kb.op("dve", lambda e: e.tensor_tensor(out=g, in0=igm, in1=bneg, op=ALU.add),
                  reads=[gt_r[s2_], gt_r[3]], writes=[gt_r[4]])
            yield
            g3 = g.rearrange("p (c l) -> p c l", l=64)
            b3 = bneg.rearrange("p (c l) -> p c l", l=64)
            gmax, Mc, nd, dec = gsm[:, 0, 0:nch], gsm[:, 1, 0:nch], gsm[:, 2, 0:nch], gsm[:, 3, 0:nch]
            nbtot = b3[:, :, 63]
            kb.op("dve", lambda e: e.tensor_reduce(out=gmax, in_=g3, axis=AX.X, op=ALU.max), reads=[gt_r[4]], writes=[gsm_r[0]])
            yield
            if not sample:
                kb.op("dve", lambda e: e.tensor_tensor_scan(out=m_all[:, chunk0 + 1:chunk0 + 1 + nch], data0=gmax, data1=nbtot,
                                                            initial=m_all[:, chunk0:chunk0 + 1], op0=ALU.max, op1=ALU.subtract),
                      reads=[gsm_r[0], gt_r[3], m_all_r], writes=[m_all_r])
                yield
                m0 = m_all[:, chunk0:chunk0 + nch]
                m0_r = m_all_r
            else:
                m0 = msin[:, 0:nch]
                m0_r = msin_r
            kb.op("dve", lambda e: e.tensor_tensor(out=Mc, in0=gmax, in1=m0, op=ALU.max), reads=[gsm_r[0], m0_r], writes=[gsm_r[1]])
            yield
            if sample:
                kb.op("dve", lambda e: e.tensor_tensor(out=mend_s[:, 0:nch], in0=Mc, in1=nbtot, op=ALU.subtract),
                      reads=[gsm_r[1], gt_r[3]], writes=[mend_s_r])
                yield
            kb.op("dve", lambda e: e.tensor_tensor(out=nd, in0=m0, in1=Mc, op=ALU.subtract), reads=[gsm_r[1], m0_r], writes=[gsm_r[2]])
            yield
            if grouped:
                Mc2, ndg = gsm[:, 4, 0:nch], gsm[:, 5, 0:nch // 2]
                Mcv = Mc.rearrange("p (j t) -> p j t", t=2)
                Mc2v = Mc2.rearrange("p (j t) -> p j t", t=2)
                ndv = nd.rearrange("p (j t) -> p j t", t=2)
                kb.op("dve", lambda e: e.tensor_copy(out=Mc2, in_=Mc), reads=[gsm_r[1]], writes=[gsm_r[4]])
                yield
                kb.op("dve", lambda e: e.tensor_tensor(out=Mc2v[:, :, 0], in0=Mcv[:, :, 0], in1=ndv[:, :, 1], op=ALU.subtract),
                      reads=[gsm_r[1], gsm_r[2], gsm_r[4]], writes=[gsm_r[4]])
                yield
                kb.op("dve", lambda e: e.tensor_tensor(out=ndg, in0=ndv[:, :, 0], in1=ndv[:, :, 1], op=ALU.add), reads=[gsm_r[2]], writes=[gsm_r[5]])
                yield
                Mw, Mw_r, ndx, ndx_r, ndec = Mc2, gsm_r[4], ndg, gsm_r[5], nch // 2
            else:
                Mw, Mw_r, ndx, ndx_r, ndec = Mc, gsm_r[1], nd, gsm_r[2], nch
            e3a = E3[0:4, 0:n].rearrange("p (c l) -> p c l", l=64)
            e3b = E3[32:36, 0:n].rearrange("p (c l) -> p c l", l=64)
            e3c = E3[64:68, 0:n].rearrange("p (c l) -> p c l", l=64)
            m0b = m0.unsqueeze(2).to_broadcast([4, nch, 64])
            Mcb = Mw.unsqueeze(2).to_broadcast([4, nch, 64])
            kb.op("dve", lambda e: e.tensor_tensor(out=e3a, in0=g3, in1=m0b, op=ALU.subtract), reads=[gt_r[4], m0_r], writes=[E3_r])
            yield
            kb.op("dve", lambda e: e.tensor_tensor(out=e3b, in0=g3, in1=Mcb, op=ALU.subtract), reads=[gt_r[4], Mw_r], writes=[E3_r])
            yield
            kb.op("dve", lambda e: e.tensor_tensor(out=e3c, in0=b3, in1=m0b, op=ALU.subtract), reads=[gt_r[3], m0_r], writes=[E3_r])
            yield
            for r0 in (0, 32, 64):
                kb.op("act", lambda e, r0=r0: e.activation(out=E3[r0:r0 + 4, 0:n], in_=E3[r0:r0 + 4, 0:n], func=AF.Exp), reads=[E3_r], writes=[E3_r])
                yield
            decx = gsm[:, 3, 0:ndec]
            kb.op("act", lambda e: e.activation(out=decx, in_=ndx, func=AF.Exp), reads=[ndx_r], writes=[gsm_r[3]])
            yield
            kb.op("dve", lambda e: e.tensor_tensor(out=drhs[:, :, 0:ndec], in0=oh[:, :, 0:ndec],
                                                   in1=decx.unsqueeze(1).to_broadcast([4, 4, ndec]), op=ALU.mult),
                  reads=[oh_r, gsm_r[3]], writes=[drhs_r])
            yield
            pd = pb[6][:, 0:4 * ndec].rearrange("p (h c) -> p h c", h=4)
            kb.op("pe", lambda e: e.matmul(pd, lhsT=ones4[0:4, :], rhs=drhs[:, :, 0:ndec], start=True, stop=True),
                  reads=[ones4_r, drhs_r], writes=[pbr[6]])
            yield
            kb.op("act", lambda e: e.activation(out=decb[:, blk, :, 0:ndec], in_=pd, func=AF.Identity), reads=[pbr[6]], writes=[decb_r])
            yield
            ntt = n // 128
            ptE = pb[6][:, 64:64 + 4 * 68].rearrange("p (t x) -> p t x", x=68)
            for tt in range(ntt):
                kb.op("pe", lambda e, tt=tt: e.transpose(out=ptE[:, tt, :], in_=E3[0:68, tt * 128:(tt + 1) * 128], identity=ident[0:68, 0:68]),
                      reads=[E3_r, ident_r], writes=[pbr[6]])
                yield
            src2 = pb[6][:, 64:64 + 4 * 68].rearrange("p (t x) -> p t x", x=68)[:, 0:ntt, 0:64].rearrange("p t (q x) -> p t q x", x=32)[:, :, :, 0:4]
            kb.op("act", lambda e: e.activation(out=tokE[:, tile0:tile0 + ntt, :, :], in_=src2, func=AF.Identity), reads=[pbr[6]], writes=[tokE_r])
            yield
            pf = pb[6][0:64, 400:400 + 4 * nch].rearrange("p (c x) -> p c x", x=4)
            for cc in range(nch):
                kb.op("pe", lambda e, cc=cc: e.transpose(out=pf[:, cc, :], in_=E3[64:68, cc * 64:(cc + 1) * 64], identity=ident[64:68, 64:68]),
                      reads=[E3_r, ident_r], writes=[pbr[6]])
                yield
            kb.op("dve", lambda e: e.tensor_copy(out=floorC[:, chunk0:chunk0 + nch, :], in_=pf), reads=[pbr[6]], writes=[floorC_r])
            yield

        ubank = [0]

        ubank = [0]

        def state_update_tile(ktok_ap, ktok_res, wv_ap, wv_res, tile, banks):
            out = []
            for h in range(4):
                for dkc in range(2):
                    def f(h=h, dkc=dkc):
                        bi = banks[ubank[0] % len(banks)]
                        ubank[0] += 1
                        kb.op("pe", lambda e: e.matmul(
                            pb[bi][:, 0:257], lhsT=ktok_ap[:, h * 256 + dkc * 128:h * 256 + dkc * 128 + 128],
                            rhs=wv_ap[:, h, 0:257], start=True, stop=True),
                            reads=[ktok_res, wv_res], writes=[pbr[bi]])
                        kb.op("dve", lambda e: e.scalar_tensor_tensor(
                            out=Cst[h][:, dkc, :], in0=Cst[h][:, dkc, :], scalar=decb[:, tile // 4, h, (tile % 4):(tile % 4) + 1],
                            in1=pb[bi][:, 0:257], op0=ALU.mult, op1=ALU.add),
                            reads=[Cst_rd[h][dkc], decb_r, pbr[bi]], writes=[Cst_rd[h][dkc]])
                    out.append(f)
            return out

        deferred = []

        gstep = [None]

        def pump(n=1):
            for _ in range(n):
                if deferred:
                    deferred.pop(0)()
            if gstep[0] is not None and n == 1:
                for _ in range(2):
                    try:
                        next(gstep[0])
                    except StopIteration:
                        gstep[0] = None
                        break

        evt = [0]

        def kv_tile(xt, xt_r, tsl, wk_ap, wk_r, wv_w_ap, wv_w_r, ktok_ap, ktok_res, wv_ap, wv_res, tileg, vext_ap=None, vext_res=None):
            for hf in range(2):
                bi = hf
                for k in range(8):
                    kb.op("pe", lambda e, k=k, hf=hf, bi=bi: e.matmul(pb[bi][:, 0:512], lhsT=xt[:, k, tsl], rhs=wk_ap(hf)[:, k, :],
                                                                     start=(k == 0), stop=(k == 7)),
                          reads=[xt_r, wk_r(hf)], writes=[pbr[bi]])
                    if k % 4 == 3:
                        pump()
                kb.op("act", lambda e, hf=hf, bi=bi: e.activation(out=ktok_ap[:, hf * 512:(hf + 1) * 512], in_=pb[bi][:, 0:512],
                                                                  func=AF.Identity, scale=0.0625),
                      reads=[pbr[bi]], writes=[ktok_res])
            for hf in range(2):
                bi = 2 + hf
                for k in range(8):
                    kb.op("pe", lambda e, k=k, hf=hf, bi=bi: e.matmul(pb[bi][:, 0:512], lhsT=xt[:, k, tsl], rhs=wv_w_ap(hf)[:, k, :],
                                                                     start=(k == 0), stop=(k == 7)),
                          reads=[xt_r, wv_w_r(hf)], writes=[pbr[bi]])
                    if k % 4 == 3:
                        pump()
                if vext_ap is not None:
                    kb.op("act", lambda e, hf=hf, bi=bi: e.activation(out=vext_ap[:, 2 * hf:2 * hf + 2, 0:256],
                                                                      in_=pb[bi][:, 0:512].rearrange("p (h v) -> p h v", h=2), func=AF.Identity),
                          reads=[pbr[bi]], writes=[vext_res])
                for h2 in range(2):
                    hh_ = 2 * hf + h2
                    kb.op("act", lambda e, hf=hf, bi=bi, h2=h2, hh_=hh_: e.activation(
                        out=wv_ap[:, hh_, 0:256], in_=pb[bi][:, h2 * 256:(h2 + 1) * 256], func=AF.Identity,
                        scale=tokE[:, tileg, 1, hh_:hh_ + 1]),
                        reads=[pbr[bi], tokE_r], writes=[wv_res])
            kb.op("pool", lambda e: e.tensor_copy(out=wv_ap[:, :, 256], in_=tokE[:, tileg, 1, :]), reads=[tokE_r], writes=[wv_res])

        def p2_block(tb):
            xt, xt_r = xTb[tb % 4], xTb_r[tb % 4]
            for _ in range(2):
                if conv_jobs:
                    conv_dma(*conv_jobs.pop(0))
            for t4 in range(4):
                tile = tb * 4 + t4
                i2 = tile % 2
                kv_tile(xt, xt_r, slice(t4 * 128, (t4 + 1) * 128),
                        lambda hf: wkv[:, :, hf * 512:(hf + 1) * 512], lambda hf: wkv_r,
                        lambda hf: wkv[:, :, 1024 + hf * 512:1024 + (hf + 1) * 512], lambda hf: wkv_r,
                        ktokp[:, i2, :], ktokp_r[i2], wvp[:, i2, :, :], wvp_r[i2], tile)
                deferred.extend(state_update_tile(ktokp[:, i2, :], ktokp_r[i2], wvp[:, i2, :, :], wvp_r[i2], tile, [4, 5]))

        gens = [gate_block(blk, 512, blk * 8, blk * 4, False, blk < NPRE // 512) for blk in range(16)]
        gens.append(gate_block(16, NSMP, 128, 64, True, False))
        next(gens[0])
        for bi_ in range(17):
            if bi_ + 1 < 17:
                next(gens[bi_ + 1])
            if 0 <= bi_ - 1 < NPRE // 512:
                gstep[0] = gens[bi_]
                p2_block(bi_ - 1)
                gstep[0] = None
            for _ in gens[bi_]:
                pass
        kb.dma("sp", s_out, m_p, m_all[:, 128:129], reads=[m_all_r], writes=[R_out])
        kb.dma("sp", s_out, m_s, mend_s, reads=[mend_s_r], writes=[R_out])

        pump(1000)
        while conv_jobs:
            conv_dma(*conv_jobs.pop(0))
        s_h = Stream(nc, "xTh")
        kb.dma("pool", s_h, xTh, xTs[:, :, NPRE - 128:NPRE], reads=[R_in], writes=[xTh_r])
        bar_res = [wkv_r, wg_r, E3_r, drhs_r] + xTb_r + gt_r + gsm_r + ktokp_r + wvp_r
        for eng in ("pe", "act", "dve", "pool", "sp"):
            kb._deps(eng, [], bar_res)
            kb._deps(eng, bar_res, [])

    ring, ring_r, s_ring = [], [], []
    for i in range(NW):
        a, r = kb.sb([128, 8, 512], BF16, "ring%d" % i)
        ring.append(a); ring_r.append(r); s_ring.append(Stream(nc, "ring%d" % i))
    xTm_b, xTm_r, s_xm = [], [], []
    for i in range(2):
        a, r = kb.sb([128, 8, T], BF16, "xTm%d" % i)
        xTm_b.append(a); xTm_r.append(r); s_xm.append(Stream(nc, "xTm%d" % i))
    xtok, xtok_r = kb.sb([128, 2, 1024], F32, "xtok")
    s_xt = Stream(nc, "xtok")
    glu, glu_r = kb.sb([128, 8, 376], BF16, "glu")
    glu32, glu32_r = kb.sb([128, 8, 4, 30], F32, "glu32")
    hist, hist_r = kb.sb([128, 8, 30], BF16, "hist")
    sigzh, sigzh_r = kb.sb([128, 8, 128], BF16, "sigzh")
    sigz, sigz_r = kb.sb([128, 8, T], BF16, "sigz")
    diags = [kb.sb([128, 31, 128], BF16, "diag%d" % i) for i in range(2)]
    dw, dw_r = kb.sb([128, 8, T], F32, "dw")
    dwb, sq, t1, t2 = [], [], [], []
    for i in range(2):
        dwb.append(kb.sb([128, T], BF16, "dwb%d" % i))
        sq.append(kb.sb([128, T], BF16, "sq%d" % i))
        t1.append(kb.sb([128, T], F32, "t1_%d" % i))
        t2.append(kb.sb([128, T], F32, "t2_%d" % i))
    mean_sb, mean_r = kb.sb([128, T], F32, "mean_sb")
    rstd, rstd_r = kb.sb([128, T], F32, "rstd")
    aT, aT_r = kb.sb([128, 8, T], BF16, "aT")
    sga, sga_r = kb.sb([128, 8, T], BF16, "sga")
    mrg, mrg_r = dw, dw_r
    mrgb, mrgb_r = kb.sb([128, 8, T], BF16, "mrgb")
    qT, qT_r = kb.sb([128, 8, T], BF16, "qT")
    kT, kT_r = kb.sb([128, 8, T], BF16, "kT")
    ktok, ktok_r = kb.sb([128, 2, 1024], BF16, "ktokm")
    vext, vext_r = kb.sb([128, 2, 4, 257], BF16, "vext")
    wvm, wvm_r = kb.sb([128, 2, 4, 257], BF16, "wvm")
    sog, sog_r = kb.sb([128, 8, T], BF16, "sog")
    sgb, sgb_r = sigz, sigz_r
    pT, pT_r0 = kb.sb([128, 4, 64], BF16, "pT")
    pT_rh = [Res("pT%d" % h) for h in range(4)]
    hbuf, hbuf_r = kb.sb([64, 1024], BF16, "hbuf")
    hgT, hgT_r = kb.sb([128, 8, T], BF16, "hgT")
    sm1, sm1_r = kb.sb([64, 64], F32, "sm1")
    asb, asb_r = [], []
    for i in range(3):
        a_, r_ = kb.sb([64, 4, 257], BF16, "asb%d" % i)
        asb.append(a_); asb_r.append(r_)
    sm1s = [kb.sb([64, 32], F32, "sm1s%d" % i) for i in range(3)]
    bsts = [kb.sb([64, 4, 6], F32, "bsts%d" % i) for i in range(3)]
    hbufs = [(hbuf, hbuf_r), kb.sb([64, 1024], BF16, "hbuf2")]
    bst, bst_r = kb.sb([64, 4, 6], F32, "bst")
    x1T, x1T_r = qT, qT_r
    hff, hff_r = kb.sb([128, 16, T], BF16, "hff")
    rl = t1
    lsm, lsm_r = kb.sb([128, 16], F32, "lsm")
    lst, lst_r = kb.sb([128, 2, 2, 6], F32, "lst")
    xtok_rt = [Res("xtok_t0"), Res("xtok_t1")]
    lsm_rt = [Res("lsm0"), Res("lsm1")]
    lst_rt = [Res("lst0"), Res("lst1")]
    cch, cch_r = kb.sb([30, 1024], F32, "cch")
    cout, cout_r = kb.sb([30, 1024], F32, "cout")
    s_cch = Stream(nc, "cch")
    s_st = [Stream(nc, "state%d" % h) for h in range(4)]
    s_y = Stream(nc, "ystore")
    for r_ in (sigz_r, sga_r, qT_r, kT_r, aT_r, hff_r, dw_r, mrgb_r, ktok_r, wvm_r, hgT_r, sog_r):
        r_.dj = True
    kb.op("dve", lambda e: e.memset(vext, 1.0), writes=[vext_r])
    kb.op("dve", lambda e: e.memset(glu, 0.0), writes=[glu_r])

    blk_groups = [2, 3, 0, 1, G_Q, G_Q + 1, G_K, G_K + 1, G_V, G_V + 1, G_O, G_O + 1, G_GB, G_GB + 1,
                  G_GA, G_GA + 1, G_A, G_A + 1, G_B, G_B + 1, G_OUT, G_OUT + 1] + [G_F1 + i for i in range(4)] + [G_F2 + i for i in range(4)] + [G_F1 + 4 + i for i in range(4)] + [G_F2 + 4 + i for i in range(4)]
    NBLK = NMAIN // T + 1
    sched = blk_groups * NBLK
    nxt = [0, 0]

    def ring_use():
        i = nxt[1]
        while nxt[0] < len(sched) and nxt[0] <= i + NW - 1:
            j = nxt[0]
            s = j % NW
            kb.dma("sp", s_ring[s], ring[s], wsc[sched[j]].rearrange("p (k n) -> p k n", k=8), reads=[R_wsc[sched[j]]], writes=[ring_r[s]])
            nxt[0] += 1
        nxt[1] += 1
        return ring[i % NW], ring_r[i % NW]

    bankrr = [0]

    def nbank():
        b = bankrr[0] % 4
        bankrr[0] += 1
        return b

    def proj_fm(wt, wt_r, act, act_r, n, evac, banks=None):
        for ocl in range(4):
            b = nbank() if banks is None else banks[ocl % len(banks)]
            for k in range(8):
                kb.op("pe", lambda e, k=k, b=b, ocl=ocl: e.matmul(pb[b][:, 0:n], lhsT=wt[:, k, ocl * 128:(ocl + 1) * 128], rhs=act[:, k, 0:n],
                                                                  start=(k == 0), stop=(k == 7)),
                      reads=[wt_r, act_r], writes=[pbr[b]])
            evac(ocl, pb[b][:, 0:n], pbr[b])

    def proj_tm(wt, wt_r, act, act_r, ntile, evac):
        for tt in range(ntile):
            b = nbank()
            for k in range(8):
                kb.op("pe", lambda e, k=k, b=b, tt=tt: e.matmul(pb[b][:, 0:512], lhsT=act[:, k, tt * 128:(tt + 1) * 128], rhs=wt[:, k, :],
                                                                start=(k == 0), stop=(k == 7)),
                      reads=[wt_r, act_r], writes=[pbr[b]])
            evac(tt, pb[b][:, 0:512], pbr[b])

    xw_loaded = [0]

    def load_xm(blk):
        i = blk % 2
        if blk < NMAIN // T:
            src = xTs[:, :, NPRE + blk * T:NPRE + (blk + 1) * T]
        else:
            src = xTm[:, :, 0:T]
        kb.dma("pool", s_xm[i], xTm_b[i], src, reads=[R_in], writes=[xTm_r[i]])

    load_xm(0)
    for h in range(4):
        kb.op("act", lambda e, h=h: e.activation(out=Cb[h], in_=Cst[h], func=AF.Identity), reads=Cst_rd[h], writes=[Cb_r[h]])

    for blk in range(NBLK):
        sample = blk == NBLK - 1
        nseg, L = (4, 64) if sample else (1, T)
        xt, xt_r = xTm_b[blk % 2], xTm_r[blk % 2]
        gl = glu[:, :, 0:nseg * (30 + L)].rearrange("p c (s l) -> p c s l", s=nseg)
        tok_lo = blk * T
        last_p = blk == NBLK - 2
        kb.dma("sp", s_xt, xtok, x_tok[tok_lo:tok_lo + T, :].rearrange("(t p) f -> p t f", p=128), reads=[R_in], writes=xtok_rt)

        for gi in (2, 3, 0, 1):
            wt, wt_r = ring_use()
            if gi >= 2:
                def ev(ocl, ps, ps_r, gi=gi):
                    c = (gi - 2) * 4 + ocl
                    kb.op("act", lambda e: e.activation(out=sigz[:, c, :], in_=ps, func=AF.Sigmoid), reads=[ps_r], writes=[sigz_r])
                proj_fm(wt, wt_r, xt, xt_r, T, ev)
                if blk == 0:
                    def evh(ocl, ps, ps_r, gi=gi):
                        c = (gi - 2) * 4 + ocl
                        kb.op("act", lambda e: e.activation(out=sigzh[:, c, :], in_=ps, func=AF.Sigmoid), reads=[ps_r], writes=[sigzh_r])
                    proj_fm(wt, wt_r, xTh, xTh_r, 128, evh)
            else:
                def ev(ocl, ps, ps_r, gi=gi):
                    c = gi * 4 + ocl
                    kb.op("dve", lambda e: e.tensor_tensor(out=gl[:, c, :, 30:30 + L], in0=ps.rearrange("p (s l) -> p s l", s=nseg),
                                                           in1=sigz[:, c, :].rearrange("p (s l) -> p s l", s=nseg), op=ALU.mult),
                          reads=[ps_r, sigz_r], writes=[glu_r])
                    if sample or last_p:
                        kb.op("dve", lambda e: e.tensor_tensor(out=glu32[:, c, 0:nseg, :],
                                                               in0=ps.rearrange("p (s l) -> p s l", s=nseg)[:, :, L - 30:L],
                                                               in1=sigz[:, c, :].rearrange("p (s l) -> p s l", s=nseg)[:, :, L - 30:L], op=ALU.mult),
                              reads=[ps_r, sigz_r], writes=[glu32_r])
                if blk == 0:
                    def evh(ocl, ps, ps_r, gi=gi):
                        c = gi * 4 + ocl
                        kb.op("dve", lambda e: e.tensor_tensor(out=gl[:, c, 0, 0:30], in0=ps[:, 98:128], in1=sigzh[:, c, 98:128], op=ALU.mult),
                              reads=[ps_r, sigzh_r], writes=[glu_r])
                    proj_fm(wt, wt_r, xTh, xTh_r, 128, evh)
                proj_fm(wt, wt_r, xt, xt_r, T, ev)
        if sample:
            for s in range(4):
                pc = pb[6][:, 0:240].rearrange("p (c r) -> p c r", r=30)
                kb.dma("sp", s_cch, cch, cache_c[s], reads=[R_in], writes=[cch_r])
                for c in range(8):
                    kb.op("pe", lambda e, s=s, c=c: e.transpose(out=pc[:, c, :], in_=cch[0:30, c * 128:(c + 1) * 128], identity=ident[0:30, 0:30]),
                          reads=[cch_r, ident_r], writes=[pbr[6]])
                kb.op("act", lambda e, s=s: e.activation(out=gl[:, :, s, 0:30], in_=pc, func=AF.Identity), reads=[pbr[6]], writes=[glu_r])
        elif blk > 0:
            kb.op("pool", lambda e: e.tensor_copy(out=gl[:, :, 0, 0:30], in_=hist), reads=[hist_r], writes=[glu_r])
        if not sample:
            kb.op("pool", lambda e: e.tensor_copy(out=hist, in_=gl[:, :, 0, L:L + 30]), reads=[glu_r], writes=[hist_r])
        if blk + 1 < NBLK:
            load_xm(blk + 1)
        if sample or last_p:
            for s in range(nseg):
                pc = pb[6][0:30, 0:512]
                pc2 = pb[5][0:30, 0:512]
                for c in range(8):
                    dst = (pc if c < 4 else pc2)[:, (c % 4) * 128:(c % 4 + 1) * 128]
                    kb.op("pe", lambda e, s=s, c=c, dst=dst: e.transpose(out=dst, in_=glu32[:, c, s, :], identity=ident),
                          reads=[glu32_r, ident_r], writes=[pbr[6], pbr[5]])
                kb.op("act", lambda e: e.activation(out=cout[:, 0:512], in_=pc, func=AF.Identity), reads=[pbr[6]], writes=[cout_r])
                kb.op("act", lambda e: e.activation(out=cout[:, 512:1024], in_=pc2, func=AF.Identity), reads=[pbr[5]], writes=[cout_r])
                kb.dma("sp", s_out, conv_s[s] if sample else conv_p, cout, reads=[cout_r], writes=[R_out])

        tile_g0 = (NPRE + blk * T) // 128 if not sample else 64
        for gi in range(2):
            wt, wt_r = ring_use()
            def ev(ocl, ps, ps_r, gi=gi):
                kb.op("act", lambda e: e.activation(out=qT[:, gi * 4 + ocl, :], in_=ps, func=AF.Identity), reads=[ps_r], writes=[qT_r])
            proj_fm(wt, wt_r, xt, xt_r, T, ev)
        for gi in range(2):
            wt, wt_r = ring_use()
            def ev(ocl, ps, ps_r, gi=gi):
                kb.op("act", lambda e: e.activation(out=kT[:, gi * 4 + ocl, :], in_=ps, func=AF.Identity, scale=0.0625), reads=[ps_r], writes=[kT_r])
            proj_fm(wt, wt_r, xt, xt_r, T, ev)
            def evt_(tt, ps, ps_r, gi=gi):
                kb.op("dve", lambda e: e.tensor_scalar(out=ktok[:, tt, gi * 512:(gi + 1) * 512], in0=ps, scalar1=0.0625, scalar2=None, op0=ALU.mult),
                      reads=[ps_r], writes=[ktok_r])
            proj_tm(wt, wt_r, xt, xt_r, T // 128, evt_)
        for gi in range(2):
            wt, wt_r = ring_use()
            def evt_(tt, ps, ps_r, gi=gi):
                p3 = ps.rearrange("p (h v) -> p h v", h=2)
                kb.op("act", lambda e: e.activation(out=vext[:, tt, 2 * gi:2 * gi + 2, 0:256], in_=p3, func=AF.Identity), reads=[ps_r], writes=[vext_r])
                kb.op("dve", lambda e: e.tensor_tensor(out=wvm[:, tt, 2 * gi:2 * gi + 2, 0:256], in0=p3,
                                                       in1=tokE[:, tile_g0 + tt, 1, 2 * gi:2 * gi + 2].unsqueeze(2).to_broadcast([128, 2, 256]), op=ALU.mult),
                      reads=[ps_r, tokE_r], writes=[wvm_r])
            proj_tm(wt, wt_r, xt, xt_r, T // 128, evt_)
        for tt in range(T // 128):
            kb.op("dve", lambda e, tt=tt: e.tensor_copy(out=wvm[:, tt, :, 256], in_=tokE[:, tile_g0 + tt, 1, :]), reads=[tokE_r], writes=[wvm_r])
        for gi in range(2):
            wt, wt_r = ring_use()
            def ev(ocl, ps, ps_r, gi=gi):
                oc = gi * 4 + ocl
                kb.op("act", lambda e: e.activation(out=sog[:, oc, :], in_=ps, func=AF.Sigmoid), reads=[ps_r], writes=[sog_r])
                kb.op("dve", lambda e: e.tensor_scalar(out=sog[:, oc, :], in0=sog[:, oc, :], scalar1=pv[:, 24 + oc:25 + oc], scalar2=None, op0=ALU.mult),
                      reads=[sog_r, pv_r], writes=[sog_r])
            proj_fm(wt, wt_r, xt, xt_r, T, ev)
        for gi in range(2):
            wt, wt_r = ring_use()
            def ev(ocl, ps, ps_r, gi=gi):
                kb.op("act", lambda e: e.activation(out=sgb[:, gi * 4 + ocl, :], in_=ps, func=AF.Sigmoid), reads=[ps_r], writes=[sgb_r])
            proj_fm(wt, wt_r, xt, xt_r, T, ev)

        pend = None
        for c in range(8):
            def build_diag(c2):
                dg, dg_r = diags[c2 % 2]
                kb.op("dve" if c2 % 2 == 0 else "pool", lambda e: e.tensor_tensor(out=dg, in0=identb.unsqueeze(1).to_broadcast([128, 31, 128]),
                                                       in1=pv[:, 32 + c2 * 31:32 + (c2 + 1) * 31].unsqueeze(2).to_broadcast([128, 31, 128]),
                                                       op=ALU.mult),
                      reads=[identb_r, pv_r], writes=[dg_r])
            diag, diag_r = diags[c % 2]
            if c == 0:
                build_diag(0)
            b = nbank()
            po = pb[b][:, 0:T].rearrange("p (s l) -> p s l", s=nseg)
            for j in range(31):
                kb.op("pe", lambda e, j=j, c=c, po=po, diag=diag: e.matmul(po, lhsT=diag[:, j, :], rhs=gl[:, c, :, j:j + L], start=(j == 0), stop=(j == 30)),
                      reads=[diag_r, glu_r], writes=[pbr[b]])
            if c + 1 < 8:
                build_diag(c + 1)
            i2 = c % 2
            kb.op("dve", lambda e, c=c, b=b: e.tensor_scalar(out=dw[:, c, :], in0=pb[b][:, 0:T], scalar1=pv[:, c:c + 1], scalar2=None, op0=ALU.add),
                  reads=[pbr[b], pv_r], writes=[dw_r])
            kb.op("act", lambda e, c=c, b=b, i2=i2: e.activation(out=sq[i2][0], in_=pb[b][:, 0:T], func=AF.Square, bias=pv[:, c:c + 1]),
                  reads=[pbr[b], pv_r], writes=[sq[i2][1]])
            kb.op("act", lambda e, c=c, b=b, i2=i2: e.activation(out=dwb[i2][0], in_=pb[b][:, 0:T], func=AF.Identity, bias=pv[:, c:c + 1]),
                  reads=[pbr[b], pv_r], writes=[dwb[i2][1]])

            def stats(c=c, i2=i2):
                kb.op("pe", lambda e: e.matmul(pb[4][:, 0:T], lhsT=onesb, rhs=dwb[i2][0], start=(c == 0), stop=(c == 7)),
                      reads=[onesb_r, dwb[i2][1]], writes=[pbr[4]])
                kb.op("pe", lambda e: e.matmul(pb[5][:, 0:T], lhsT=onesb, rhs=sq[i2][0], start=(c == 0), stop=(c == 7)),
                      reads=[onesb_r, sq[i2][1]], writes=[pbr[5]])
            if pend is not None:
                pend()
            pend = stats
        pend()
        def ln_finalize_a():
            kb.op("dve", lambda e: e.tensor_copy(out=mean_sb, in_=pb[4][:, 0:T]), reads=[pbr[4]], writes=[mean_r])
            kb.op("pool", lambda e: e.tensor_tensor(out=t1[0][0], in0=mean_sb, in1=mean_sb, op=ALU.mult), reads=[mean_r], writes=[t1[0][1]])
            kb.op("dve", lambda e: e.tensor_tensor(out=rstd, in0=pb[5][:, 0:T], in1=t1[0][0], op=ALU.subtract), reads=[pbr[5], t1[0][1]], writes=[rstd_r])
        def ln_finalize_b():
            kb.op("dve", lambda e: e.tensor_scalar(out=rstd, in0=rstd, scalar1=0.0, scalar2=EPS, op0=ALU.max, op1=ALU.add), reads=[rstd_r], writes=[rstd_r])
            kb.op("act", lambda e: e.activation(out=rstd, in_=rstd, func=AF.Sqrt), reads=[rstd_r], writes=[rstd_r])
            kb.op("dve", lambda e: e.reciprocal(out=rstd, in_=rstd), reads=[rstd_r], writes=[rstd_r])
            for c in range(8):
                i2 = c % 2
                kb.op("pool", lambda e, c=c, i2=i2: e.tensor_tensor(out=t1[i2][0], in0=dw[:, c, :], in1=mean_sb, op=ALU.subtract),
                      reads=[dw_r, mean_r], writes=[t1[i2][1]])
                kb.op("dve", lambda e, i2=i2: e.tensor_tensor(out=t2[i2][0], in0=t1[i2][0], in1=rstd, op=ALU.mult),
                      reads=[t1[i2][1], rstd_r], writes=[t2[i2][1]])
                kb.op("act", lambda e, c=c, i2=i2: e.activation(out=aT[:, c, :], in_=t2[i2][0], func=AF.Silu, scale=pv[:, 8 + c:9 + c], bias=pv[:, 16 + c:17 + c]),
                      reads=[t2[i2][1], pv_r], writes=[aT_r])
        def chunk_ids(cc):
            tt, half = cc // 2, cc % 2
            cg = (NPRE + blk * T) // 64 + cc if not sample else 128 + cc
            return tt, half, half * 64, cc * 64, cg, tile_g0 + tt

        def mlstm_crit(cc):
            tt, half, p0, tok0, cg, tg = chunk_ids(cc)
            ai = asb[cc % 3]
            ai_r = asb_r[cc % 3]
            if sample:
                for h in range(4):
                    kb.dma("sp", s_st[h], Cst[h][:, :, 0:256], sC[cc, h].rearrange("(c p) v -> p c v", p=128), reads=[R_in], writes=Cst_rd[h])
                    kb.dma("sp", s_st[h], Cst[h][:, :, 256], sn[cc, h].rearrange("(c p) -> p c", p=128), reads=[R_in], writes=Cst_rd[h],
                           allow_slow_non_contiguous=True)
                    kb.op("act", lambda e, h=h: e.activation(out=Cb[h], in_=Cst[h], func=AF.Identity), reads=Cst_rd[h], writes=[Cb_r[h]])
            if half == 0:
                for h in range(4):
                    for dkc in range(2):
                        kb.op("pe", lambda e, h=h, dkc=dkc: e.matmul(pb[4][:, h * 128:(h + 1) * 128], lhsT=kT[:, 2 * h + dkc, tt * 128:(tt + 1) * 128],
                                                                     rhs=qT[:, 2 * h + dkc, tt * 128:(tt + 1) * 128], start=(dkc == 0), stop=(dkc == 1)),
                              reads=[kT_r, qT_r], writes=[pbr[4]])
            for h in range(4):
                kb.op("dve", lambda e, h=h: e.scalar_tensor_tensor(out=pT[p0:p0 + 64, h, :], in0=pb[4][p0:p0 + 64, h * 128 + p0:h * 128 + p0 + 64],
                                                                   scalar=tokE[p0:p0 + 64, tg, 0, h:h + 1], in1=cmask[p0:p0 + 64, :],
                                                                   op0=ALU.mult, op1=ALU.mult),
                      reads=[pbr[4], tokE_r, cmask_r], writes=[pT_rh[h]])
            def upd(h):
                for dkc in range(2):
                    bi = dkc
                    kb.op("pe", lambda e, dkc=dkc, bi=bi: e.matmul(
                        pb[bi][:, 0:257], lhsT=ktok[p0:p0 + 64, tt, h * 256 + dkc * 128:h * 256 + dkc * 128 + 128],
                        rhs=wvm[p0:p0 + 64, tt, h, 0:257], start=True, stop=True),
                        reads=[ktok_r, wvm_r], writes=[pbr[bi]])
                    kb.op("dve", lambda e, dkc=dkc, bi=bi: e.scalar_tensor_tensor(
                        out=Cst[h][:, dkc, :], in0=Cst[h][:, dkc, :], scalar=decb[:, cg // 8, h, (cg % 8):(cg % 8) + 1],
                        in1=pb[bi][:, 0:257], op0=ALU.mult, op1=ALU.add),
                        reads=[Cst_rd[h][dkc], decb_r, pbr[bi]], writes=[Cst_rd[h][dkc]])

            for h in range(4):
                upd(h)
                nb = (5, 6, 3, 2)[h]
                on = pb[nb][0:64, 256:512] if h == 3 else pb[nb][0:64, 0:256]
                od = pb[2][0:64, h:h + 1]
                kb.op("pe", lambda e, h=h, on=on: e.matmul(on, lhsT=pT[p0:p0 + 64, h, :], rhs=vext[p0:p0 + 64, tt, h, 0:256], start=True, stop=False),
                      reads=[pT_rh[h], vext_r], writes=[pbr[nb]])
                for dkc in range(2):
                    kb.op("pe", lambda e, h=h, dkc=dkc, on=on: e.matmul(on, lhsT=qT[:, 2 * h + dkc, tok0:tok0 + 64], rhs=Cb[h][:, dkc, 0:256],
                                                                        start=False, stop=(dkc == 1)),
                          reads=[qT_r, Cb_r[h]], writes=[pbr[nb]])
                kb.op("pe", lambda e, h=h, od=od: e.matmul(od, lhsT=pT[p0:p0 + 64, h, :], rhs=vext[p0:p0 + 64, tt, h, 256:257], start=True, stop=False),
                      reads=[pT_rh[h], vext_r], writes=[pbr[2]])
                for dkc in range(2):
                    kb.op("pe", lambda e, h=h, dkc=dkc, od=od: e.matmul(od, lhsT=qT[:, 2 * h + dkc, tok0:tok0 + 64], rhs=Cb[h][:, dkc, 256:257],
                                                                        start=False, stop=(dkc == 1)),
                          reads=[qT_r, Cb_r[h]], writes=[pbr[2]])
                kb.op("act", lambda e, h=h, on=on: e.activation(out=ai[:, h, 0:256], in_=on, func=AF.Identity), reads=[pbr[nb]], writes=[ai_r])
                if sample:
                    kb.dma("sp", s_out, Cn_s[cc, h].rearrange("c p v -> p c v"), Cst[h], reads=Cst_rd[h], writes=[R_out])
                elif last_p and cc == T // 64 - 1:
                    kb.dma("sp", s_out, Cn_p[h].rearrange("c p v -> p c v"), Cst[h], reads=Cst_rd[h], writes=[R_out])
                if not (sample or (last_p and cc == T // 64 - 1)):
                    kb.op("act", lambda e, h=h: e.activation(out=Cb[h], in_=Cst[h], func=AF.Identity), reads=Cst_rd[h], writes=[Cb_r[h]])
            kb.op("act", lambda e: e.activation(out=ai[:, :, 256], in_=pb[2][0:64, 0:4], func=AF.Identity), reads=[pbr[2]], writes=[ai_r])

        def norm_a(cc):
            tt, half, p0, tok0, cg, tg = chunk_ids(cc)
            ai, ai_r = asb[cc % 3], asb_r[cc % 3]
            sm, sm_r, bs, bs_r = sm1s[cc % 3][0], sm1s[cc % 3][1], bsts[cc % 3][0], bsts[cc % 3][1]
            d_, d2_, vv_ = sm[:, 0:4], sm[:, 4:8], sm[:, 8:12]
            mv = sm[:, 24:32].rearrange("p (h x) -> p h x", x=2)
            for h in range(4):
                kb.op("dve", lambda e, h=h: e.bn_stats(out=bs[:, h, :], in_=ai[:, h, 0:256]), reads=[ai_r], writes=[bs_r])
                kb.op("dve", lambda e, h=h: e.bn_aggr(out=mv[:, h, :], in_=bs[:, h, :]), reads=[bs_r], writes=[sm_r])
            kb.op("dve", lambda e: e.tensor_tensor(out=d2_, in0=ai[:, :, 256], in1=ai[:, :, 256], op=ALU.mult), reads=[ai_r], writes=[sm_r])
            kb.op("dve", lambda e: e.tensor_tensor(out=d_, in0=floorC[:, cg, :], in1=floorC[:, cg, :], op=ALU.mult), reads=[sm_r, floorC_r], writes=[sm_r])
            kb.op("dve", lambda e: e.tensor_tensor(out=d2_, in0=d2_, in1=d_, op=ALU.max), reads=[sm_r], writes=[sm_r])
            kb.op("dve", lambda e: e.scalar_tensor_tensor(out=vv_, in0=d2_, scalar=EPS, in1=mv[:, :, 1], op0=ALU.mult, op1=ALU.add), reads=[sm_r], writes=[sm_r])
            kb.op("act", lambda e: e.activation(out=vv_, in_=vv_, func=AF.Sqrt), reads=[sm_r], writes=[sm_r])

        def norm_b(cc):
            ai, ai_r = asb[cc % 3], asb_r[cc % 3]
            sm, sm_r = sm1s[cc % 3][0], sm1s[cc % 3][1]
            hb, hb_r = hbufs[cc % 2]
            vv_, rs_, nm_ = sm[:, 8:12], sm[:, 12:16], sm[:, 16:20]
            mv = sm[:, 24:32].rearrange("p (h x) -> p h x", x=2)
            kb.op("dve", lambda e: e.reciprocal(out=rs_, in_=vv_), reads=[sm_r], writes=[sm_r])
            kb.op("dve", lambda e: e.scalar_tensor_tensor(out=nm_, in0=mv[:, :, 0], scalar=-1.0, in1=rs_, op0=ALU.mult, op1=ALU.mult), reads=[sm_r], writes=[sm_r])
            for h in range(4):
                kb.op("act", lambda e, h=h: e.activation(out=hb[:, h * 256:(h + 1) * 256], in_=ai[:, h, 0:256],
                                                         func=AF.Identity, scale=rs_[:, h:h + 1], bias=nm_[:, h:h + 1]),
                      reads=[ai_r, sm_r], writes=[hb_r])

        def norm_c(cc):
            tt, half, p0, tok0, cg, tg = chunk_ids(cc)
            hb, hb_r = hbufs[cc % 2]
            for fc in range(8):
                kb.op("pe", lambda e, fc=fc: e.transpose(out=pbh[:, fc * 64:(fc + 1) * 64], in_=hb[:, fc * 128:(fc + 1) * 128], identity=identb[0:64, 0:64]),
                      reads=[hb_r, identb_r], writes=[pbh_r])
            kb.op("dve", lambda e: e.tensor_tensor(out=hgT[:, :, tok0:tok0 + 64], in0=pbh[:, 0:512].rearrange("p (c t) -> p c t", t=64),
                                                   in1=sog[:, :, tok0:tok0 + 64], op=ALU.mult),
                  reads=[pbh_r, sog_r], writes=[hgT_r])

        ln_finalize_a()
        NCH = T // 64
        for it in range(NCH + 3):
            if it < NCH:
                mlstm_crit(it)
            if 0 <= it - 1 < NCH:
                norm_a(it - 1)
            if 0 <= it - 2 < NCH:
                norm_b(it - 2)
            if 0 <= it - 3 < NCH:
                norm_c(it - 3)
        ln_finalize_b()
        for gi in range(2):
            wt, wt_r = ring_use()
            def ev(ocl, ps, ps_r, gi=gi):
                kb.op("act", lambda e: e.activation(out=sga[:, gi * 4 + ocl, :], in_=ps, func=AF.Sigmoid), reads=[ps_r], writes=[sga_r])
            proj_fm(wt, wt_r, xt, xt_r, T, ev)
        for gi in range(2):
            wt, wt_r = ring_use()
            def ev(ocl, ps, ps_r, gi=gi):
                oc = gi * 4 + ocl
                kb.op("dve", lambda e: e.tensor_tensor(out=mrg[:, oc, :], in0=ps, in1=sga[:, oc, :], op=ALU.mult), reads=[ps_r, sga_r], writes=[mrg_r])
            proj_fm(wt, wt_r, aT, aT_r, T, ev)

        for gi in range(2):
            wt, wt_r = ring_use()
            def ev(ocl, ps, ps_r, gi=gi):
                oc = gi * 4 + ocl
                i2 = oc % 2
                kb.op("dve", lambda e: e.tensor_tensor(out=t2[i2][0], in0=ps, in1=sgb[:, oc, :], op=ALU.mult), reads=[ps_r, sgb_r], writes=[t2[i2][1]])
                kb.op("pool", lambda e: e.tensor_tensor(out=mrgb[:, oc, :], in0=mrg[:, oc, :], in1=t2[i2][0], op=ALU.add),
                      reads=[mrg_r, t2[i2][1]], writes=[mrgb_r])
            proj_fm(wt, wt_r, hgT, hgT_r, T, ev)

        def layernorm_tok(tt, gi, bi_):
            xs = xtok[:, tt, :]
            ls = lsm[:, tt * 8:tt * 8 + 8]
            for j in range(2):
                kb.op("dve", lambda e, j=j: e.bn_stats(out=lst[:, tt, j, :], in_=xs[:, j * 512:(j + 1) * 512]), reads=[xtok_rt[tt]], writes=[lst_rt[tt]])
            kb.op("dve", lambda e: e.bn_aggr(out=ls[:, 0:2], in_=lst[:, tt, :, :].rearrange("p a b -> p (a b)")), reads=[lst_rt[tt]], writes=[lsm_rt[tt]])
            kb.op("dve", lambda e: e.tensor_scalar(out=ls[:, 2:3], in0=ls[:, 1:2], scalar1=EPS, scalar2=None, op0=ALU.add), reads=[lsm_rt[tt]], writes=[lsm_rt[tt]])
            kb.op("act", lambda e: e.activation(out=ls[:, 2:3], in_=ls[:, 2:3], func=AF.Sqrt), reads=[lsm_rt[tt]], writes=[lsm_rt[tt]])
            kb.op("dve", lambda e: e.reciprocal(out=ls[:, 3:4], in_=ls[:, 2:3]), reads=[lsm_rt[tt]], writes=[lsm_rt[tt]])
            kb.op("dve", lambda e: e.scalar_tensor_tensor(out=ls[:, 4:5], in0=ls[:, 0:1], scalar=-1.0, in1=ls[:, 3:4], op0=ALU.mult, op1=ALU.mult),
                  reads=[lsm_rt[tt]], writes=[lsm_rt[tt]])
            kb.op("act", lambda e: e.activation(out=xs, in_=xs, func=AF.Identity, scale=ls[:, 3:4], bias=ls[:, 4:5]),
                  reads=[xtok_rt[tt], lsm_rt[tt]], writes=[xtok_rt[tt]])
            kb.op("dve", lambda e: e.tensor_tensor(out=xs, in0=xs, in1=lnbc[:, gi, :], op=ALU.mult), reads=[xtok_rt[tt], lnbc_r], writes=[xtok_rt[tt]])
            kb.op("dve", lambda e: e.tensor_tensor(out=xs, in0=xs, in1=lnbc[:, bi_, :], op=ALU.add), reads=[xtok_rt[tt], lnbc_r], writes=[xtok_rt[tt]])

        for gi in range(2):
            wt, wt_r = ring_use()
            def evt_(tt, ps, ps_r, gi=gi):
                kb.op("dve", lambda e: e.scalar_tensor_tensor(out=xtok[:, tt, gi * 512:(gi + 1) * 512], in0=xtok[:, tt, gi * 512:(gi + 1) * 512],
                                                              scalar=ALPHA, in1=ps, op0=ALU.mult, op1=ALU.add),
                      reads=[ps_r, xtok_rt[tt]], writes=[xtok_rt[tt]])
            proj_tm(wt, wt_r, mrgb, mrgb_r, T // 128, evt_)
        for tt in range(T // 128):
            layernorm_tok(tt, 0, 1)
            for hb in range(2):
                bnk = 4 + hb
                for f4 in range(4):
                    fc = hb * 4 + f4
                    kb.op("pe", lambda e, fc=fc, f4=f4, bnk=bnk: e.transpose(out=pb[bnk][:, f4 * 128:(f4 + 1) * 128], in_=xtok[:, tt, fc * 128:(fc + 1) * 128],
                                                                             identity=ident),
                          reads=[xtok_rt[tt], ident_r], writes=[pbr[bnk]])
                kb.op("act", lambda e, hb=hb, bnk=bnk: e.activation(out=x1T[:, hb * 4:hb * 4 + 4, tt * 128:(tt + 1) * 128],
                                                                    in_=pb[bnk][:, 0:512].rearrange("p (c t) -> p c t", t=128), func=AF.Identity),
                      reads=[pbr[bnk]], writes=[x1T_r])
        for hh in range(2):
            for gi in range(4):
                wt, wt_r = ring_use()
                def ev(ocl, ps, ps_r, gi=gi):
                    oc = gi * 4 + ocl
                    i2 = oc % 2
                    kb.op("act", lambda e: e.activation(out=rl[i2][0], in_=ps, func=AF.Relu), reads=[ps_r], writes=[rl[i2][1]])
                    kb.op("pool", lambda e: e.tensor_tensor(out=hff[:, oc, :], in0=rl[i2][0], in1=rl[i2][0], op=ALU.mult), reads=[rl[i2][1]], writes=[hff_r])
                proj_fm(wt, wt_r, x1T, x1T_r, T, ev, banks=[4, 5, 6])
            for gi in range(4):
                wt, wt_r = ring_use()
                w4 = wt.rearrange("p a n -> p (a n)").rearrange("p (k n) -> p k n", k=4)
                for tt in range(T // 128):
                    for hc in range(2):
                        b = tt * 2 + hc
                        for k4 in range(4):
                            kl = gi * 4 + k4
                            kk = hh * 16 + kl
                            kb.op("pe", lambda e, k4=k4, kl=kl, kk=kk, b=b, tt=tt, hc=hc, w4=w4: e.matmul(
                                pb[b][:, 0:512], lhsT=hff[:, kl, tt * 128:(tt + 1) * 128], rhs=w4[:, k4, hc * 512:(hc + 1) * 512],
                                start=(kk == 0), stop=(kk == 31)),
                                reads=[hff_r, wt_r], writes=[pbr[b]])
        for tt in range(T // 128):
            for hc in range(2):
                b = tt * 2 + hc
                kb.op("dve", lambda e, b=b, tt=tt, hc=hc: e.scalar_tensor_tensor(
                    out=xtok[:, tt, hc * 512:(hc + 1) * 512], in0=xtok[:, tt, hc * 512:(hc + 1) * 512], scalar=ALPHA, in1=pb[b][:, 0:512],
                    op0=ALU.mult, op1=ALU.add), reads=[pbr[b], xtok_rt[tt]], writes=[xtok_rt[tt]])
        for tt in range(T // 128):
            layernorm_tok(tt, 2, 3)
        kb.dma("sp", s_y, y_d[tok_lo:tok_lo + T, :].rearrange("(t p) f -> p t f", p=128), xtok, reads=xtok_rt, writes=[R_out])

    nc.sync.wait_ge(s_out.sem, s_out.count)
    nc.sync.wait_ge(s_y.sem, s_y.count)
    return nc


_CACHE = {}


def kernel(**inp):
    f = lambda k: np.ascontiguousarray(np.asarray(inp[k], dtype=np.float32))
    xp, xs = f("x_prompt"), f("x_sample")
    cc_, sC_, sn_, sm_ = f("cache_conv")[0], f("state_C")[0], f("state_n")[0], f("state_m")[0]
    w_in, bg, w_dw, b_dw = f("w_in")[0], f("b_gate")[0], f("w_dw")[0], f("b_dw")[0]
    pv = np.zeros((128, NPV), np.float32)
    for i, v in enumerate((b_dw, f("ln_a_g")[0], f("ln_a_b")[0], f("hn_g")[0])):
        pv[:, 8 * i:8 * i + 8] = v.reshape(8, 128).T
    pv[:, 32:] = w_dw.reshape(31, 8, 128).transpose(2, 1, 0).reshape(128, 248)
    gbias = np.stack([bg[0:4], bg[4:8]], axis=1).astype(np.float32)
    lnrows = np.stack([f("ln1_g")[0], f("ln1_b")[0], f("ln2_g")[0], f("ln2_b")[0]]).astype(np.float32)
    ident = np.eye(128, dtype=np.float32)
    cm = (np.arange(64)[:, None] <= np.arange(64)[None, :]).astype(np.float32)
    cmask2 = np.concatenate([cm, cm], axis=0)
    resetm = np.ones((4, 512), np.float32)
    resetm[:, ::64] = 0
    oh = np.zeros((4, 4, 8), np.float32)
    for h in range(4):
        oh[h, h, :] = 1
    shared = {"w_in": w_in, "w_a": f("w_a_out")[0], "w_b": f("w_b_out")[0], "w_o": f("w_out")[0],
              "w_f1": f("w_ff1")[0], "w_f2": f("w_ff2")[0], "pv": pv, "gbias": gbias, "lnrows": lnrows,
              "ident": ident, "cmask2": cmask2, "resetm": resetm, "oh": oh}
    in_maps = []
    for c in range(8):
        b, q = c // 4, c % 4
        npad = NPRE - NMAIN * q
        xT_seq = np.zeros((1024, NSEQ), np.float32)
        xT_seq[:, npad:] = xp[b, :NMAIN * (q + 1)].T
        valid = np.zeros((4, NSEQ), np.float32)
        valid[:, npad:] = 1.0
        nbig = np.where(valid > 0, 0.0, -30000.0).astype(np.float32)
        xsm = xs[4 * c:4 * c + 4].reshape(NSMP, 1024)
        m = dict(shared)
        m.update({"xT_seq": xT_seq, "xT_smp": np.ascontiguousarray(xsm.T),
                  "x_tok": np.ascontiguousarray(np.concatenate([xp[b, NMAIN * q:NMAIN * (q + 1)], xsm], axis=0)),
                  "valid": valid, "nbig": nbig, "cache_c": np.ascontiguousarray(cc_[4 * c:4 * c + 4]),
                  "sC": np.ascontiguousarray(sC_[4 * c:4 * c + 4]), "sn": np.ascontiguousarray(sn_[4 * c:4 * c + 4]),
                  "sm": np.ascontiguousarray(sm_[4 * c:4 * c + 4].T)})
        in_maps.append(m)
    if inp.get("_only_maps"):
        return in_maps
    if "nc" not in _CACHE:
        _CACHE["nc"] = build()
    res = run_bass_kernel_spmd(_CACHE["nc"], in_maps, core_ids=list(range(8)))
    R = res.results
    yp = np.zeros((2, 8192, 1024), np.float32)
    ys = np.zeros((32, 64, 1024), np.float32)
    conv_p = np.zeros((1, 2, 30, 1024), np.float32)
    C_p = np.zeros((1, 2, 4, 256, 256), np.float32)
    n_p = np.zeros((1, 2, 4, 256), np.float32)
    m_p = np.zeros((1, 2, 4), np.float32)
    conv_s = np.zeros((1, 32, 30, 1024), np.float32)
    C_s = np.zeros((1, 32, 4, 256, 256), np.float32)
    n_s = np.zeros((1, 32, 4, 256), np.float32)
    m_s = np.zeros((1, 32, 4), np.float32)
    for c in range(8):
        b, q = c // 4, c % 4
        r = R[c]
        yp[b, NMAIN * q:NMAIN * (q + 1)] = r["y"][:NMAIN]
        ys[4 * c:4 * c + 4] = r["y"][NMAIN:].reshape(4, 64, 1024)
        if q == 3:
            conv_p[0, b] = r["conv_p"]
            cn = r["Cn_p"].reshape(4, 256, 257)
            C_p[0, b] = cn[:, :, :256]
            n_p[0, b] = cn[:, :, 256]
            m_p[0, b] = r["m_p"][:, 0]
        conv_s[0, 4 * c:4 * c + 4] = r["conv_s"]
        cn = r["Cn_s"].reshape(4, 4, 256, 257)
        C_s[0, 4 * c:4 * c + 4] = cn[:, :, :, :256]
        n_s[0, 4 * c:4 * c + 4] = cn[:, :, :, 256]
        m_s[0, 4 * c:4 * c + 4] = r["m_s"].T
    return (yp, ys, conv_p, C_p, n_p, m_p, conv_s, C_s, n_s, m_s)
```

```python
import numpy as np
import concourse.bass as bass
import concourse.mybir as mybir
from concourse.bass_utils import run_bass_kernel_spmd

F32, BF16 = mybir.dt.float32, mybir.dt.bfloat16
AF = mybir.ActivationFunctionType
ALU = mybir.AluOpType
AX = mybir.AxisListType

NSEQ, NPRE, NMAIN, NSMP = 8192, 6144, 2048, 256
T = 256
ALPHA = 2.0 ** 0.25
EPS = 1e-5
NPV = 32 + 8 * 31
NW = 4
G_CONV, G_Q, G_K, G_V, G_O, G_GA, G_GB = 0, 4, 6, 8, 10, 12, 14
G_A, G_B, G_OUT, G_F1, G_F2 = 16, 18, 20, 22, 30
NG = 38
WIN_COL = [0, 512, 1024, 1536, 2048, 2560, 3072, 3584, 4096, 4608, 5120, 5632, 6152, 6664, 7176, 7688]


class Res:
    __slots__ = ("name", "w", "r", "excl", "dj")

    def __init__(self, name, excl=False):
        self.name = name
        self.w = None
        self.r = {}
        self.excl = excl
        self.dj = False


class Stream:
    def __init__(self, nc, name):
        self.name = name
        self.sem = nc.alloc_semaphore("ds_" + name)
        self.count = 0


class KB:
    def __init__(self, nc):
        self.nc = nc
        self.eng = {"pe": nc.tensor, "act": nc.scalar, "dve": nc.vector, "pool": nc.gpsimd, "sp": nc.sync}
        self.sem = {e: nc.alloc_semaphore("sem_" + e) for e in ("pe", "act", "dve", "pool")}
        self.seq = {e: 0 for e in self.sem}
        self.waited = {}
        self.nbuf = 0

    def sb(self, shape, dt=F32, name=None):
        self.nbuf += 1
        nm = "s_" + (name or ("b%d" % self.nbuf))
        t = self.nc.alloc_sbuf_tensor(nm, list(shape), dt).ap()
        return t, Res(nm)

    def _wait(self, eng, sem, val):
        key = (eng, sem.num)
        if self.waited.get(key, 0) >= val:
            return
        self.waited[key] = val
        self.eng[eng].wait_ge(sem, val)

    def _deps(self, eng, reads, writes):
        toks = []
        for r in reads:
            if r.w is not None:
                toks.append((r.w, "raw", False))
            if r.excl:
                for t in r.r.values():
                    toks.append((t, "war", False))
        for w in writes:
            if w.w is not None:
                toks.append((w.w, "waw", w.dj))
            for t in w.r.values():
                toks.append((t, "war", w.dj))
        for (tok, kind, dj) in toks:
            sem, val, teng = tok
            if teng == eng:
                if eng == "pe":
                    continue
                if dj and kind != "raw":
                    continue
            self._wait(eng, sem, val)

    def op(self, eng, fn, reads=(), writes=()):
        self._deps(eng, reads, writes)
        ins = fn(self.eng[eng])
        self.seq[eng] += 1
        ins.then_inc(self.sem[eng], 1)
        tok = (self.sem[eng], self.seq[eng], eng)
        for r in reads:
            r.r[eng] = tok
        for w in writes:
            w.w = tok
            w.r = {}
        return tok

    def dma(self, q, stream, out, in_, reads=(), writes=(), **kw):
        self._deps(q, reads, writes)
        ins = self.eng[q].dma_start(out=out, in_=in_, **kw)
        stream.count += 16
        ins.then_inc(stream.sem, 16)
        tok = (stream.sem, stream.count, "dma:" + stream.name)
        for r in reads:
            r.r["dma:" + stream.name] = tok
        for w in writes:
            w.w = tok
            w.r = {}
        return tok


def build():
    nc = bass.Bass("TRN2", target_bir_lowering=False)
    kb = KB(nc)

    def din(name, shape):
        return nc.dram_tensor(name, list(shape), F32, kind="ExternalInput").ap()

    def dout(name, shape):
        return nc.dram_tensor(name, list(shape), F32, kind="ExternalOutput").ap()

    xT_seq = din("xT_seq", [1024, NSEQ])
    xT_smp = din("xT_smp", [1024, NSMP])
    x_tok = din("x_tok", [NMAIN + NSMP, 1024])
    valid_d = din("valid", [4, NSEQ])
    nbig_d = din("nbig", [4, NSEQ])
    cache_c = din("cache_c", [4, 30, 1024])
    sC = din("sC", [4, 4, 256, 256])
    sn = din("sn", [4, 4, 256])
    sm = din("sm", [4, 4])
    w_in = din("w_in", [1024, 8200])
    w_a = din("w_a", [1024, 1024])
    w_b = din("w_b", [1024, 1024])
    w_o = din("w_o", [1024, 1024])
    w_f1 = din("w_f1", [1024, 4096])
    w_f2 = din("w_f2", [4096, 1024])
    pv_d = din("pv", [128, NPV])
    gb_d = din("gbias", [4, 2])
    lnrows = din("lnrows", [4, 1024])
    ident_d = din("ident", [128, 128])
    cmask_d = din("cmask2", [128, 64])
    reset_d = din("resetm", [4, 512])
    oh_d = din("oh", [4, 4, 8])
    wsc = nc.dram_tensor("wsc", [NG, 128, 4096], BF16, kind="Internal").ap()
    y_d = dout("y", [NMAIN + NSMP, 1024])
    conv_p = dout("conv_p", [30, 1024])
    Cn_p = dout("Cn_p", [4, 2, 128, 257])
    m_p = dout("m_p", [4, 1])
    conv_s = dout("conv_s", [4, 30, 1024])
    Cn_s = dout("Cn_s", [4, 4, 2, 128, 257])
    m_s = dout("m_s", [4, 4])

    R_in = Res("dram_in")
    R_wsc = [Res("wsc%d" % g) for g in range(NG)]
    R_out = Res("dram_out")
    s_const = Stream(nc, "const")
    s_conv = [Stream(nc, "wcv%d" % g) for g in range(NG)]
    s_out = Stream(nc, "out")

    pb, pbr = [], []
    for i in range(7):
        pb.append(nc.alloc_psum_tensor("pb%d" % i, [128, 512], F32).ap())
        pbr.append(Res("pb%d" % i, True))
    pbh = nc.alloc_psum_tensor("pbh", [128, 1024], BF16).ap()
    pbh_r = Res("pbh", True)

    ident, ident_r = kb.sb([128, 128], F32, "ident")
    identb, identb_r = kb.sb([128, 128], BF16, "identb")
    onesb, onesb_r = kb.sb([128, 128], BF16, "onesb")
    ones4, ones4_r = kb.sb([4, 128], F32, "ones4")
    cmask, cmask_r = kb.sb([128, 64], F32, "cmask")
    resetm, resetm_r = kb.sb([4, 512], F32, "resetm")
    oh, oh_r = kb.sb([4, 4, 8], F32, "oh")
    pv, pv_r = kb.sb([128, NPV], F32, "pv")
    gbt, gbt_r = kb.sb([4, 2], F32, "gbt")
    nbf, nbf_r = kb.sb([4, 1], F32, "nbf")
    lnbc, lnbc_r = kb.sb([128, 4, 1024], F32, "lnbc")
    for (dst, src, rr) in ((ident, ident_d, ident_r), (cmask, cmask_d, cmask_r), (resetm, reset_d, resetm_r),
                           (oh, oh_d, oh_r), (pv, pv_d, pv_r), (gbt, gb_d, gbt_r)):
        kb.dma("sp", s_const, dst, src, reads=[R_in], writes=[rr])
    for i in range(4):
        kb.dma("sp", s_const, lnbc[:, i, :], lnrows[i:i + 1, :].partition_broadcast(128), reads=[R_in], writes=[lnbc_r])
    tok_all = (s_const.sem, s_const.count, "dma:const")
    for rr in (ident_r, cmask_r, resetm_r, oh_r, pv_r, gbt_r, lnbc_r):
        rr.w = tok_all
    kb.op("dve", lambda e: e.tensor_copy(out=identb, in_=ident), reads=[ident_r], writes=[identb_r])
    kb.op("dve", lambda e: e.memset(onesb, 1.0 / 1024.0), writes=[onesb_r])
    kb.op("dve", lambda e: e.memset(ones4, 1.0), writes=[ones4_r])
    kb.op("dve", lambda e: e.tensor_scalar(out=nbf, in0=gbt[:, 1:2], scalar1=-1.0, scalar2=None, op0=ALU.mult),
          reads=[gbt_r], writes=[nbf_r])

    tokE, tokE_r = kb.sb([128, 66, 2, 4], F32, "tokE")
    floorC, floorC_r = kb.sb([64, 132, 4], F32, "floorC")
    decb, decb_r = kb.sb([128, 17, 4, 8], F32, "decb")
    for r_ in (tokE_r, floorC_r, decb_r):
        r_.dj = True
    m_all, m_all_r = kb.sb([4, 129], F32, "m_all")
    msin, msin_r = kb.sb([4, 4], F32, "msin")
    mend_s, mend_s_r = kb.sb([4, 4], F32, "mend_s")
    Cst, Cst_r, Cb, Cb_r, Cst_rd = [], [], [], [], []
    for h in range(4):
        a, r = kb.sb([128, 2, 257], F32, "Cst%d" % h)
        Cst.append(a); Cst_r.append(r); Cst_rd.append([Res("Cst%d_0" % h), Res("Cst%d_1" % h)])
        a, r = kb.sb([128, 2, 257], BF16, "Cb%d" % h)
        Cb.append(a); Cb_r.append(r)
        kb.op("dve", lambda e, a=Cst[h]: e.memset(a, 0.0), writes=Cst_rd[h])
    kb.op("dve", lambda e: e.memset(m_all, 0.0), writes=[m_all_r])
    s_msin = Stream(nc, "msin")
    kb.dma("sp", s_msin, msin, sm, reads=[R_in], writes=[msin_r])
    xTh, xTh_r = kb.sb([128, 8, 128], BF16, "xTh")

    with nc.sbuf_tensor("p_wkv", [128, 8, 2048], BF16) as wkv_t, \
            nc.sbuf_tensor("p_wg", [128, 8, 8], BF16) as wg_t, \
            nc.sbuf_tensor("xTb0", [128, 8, 512], BF16) as xTb0_t, \
            nc.sbuf_tensor("xTb1", [128, 8, 512], BF16) as xTb1_t, \
            nc.sbuf_tensor("xTb2", [128, 8, 512], BF16) as xTb2_t, \
            nc.sbuf_tensor("xTb3", [128, 8, 512], BF16) as xTb3_t, \
            nc.sbuf_tensor("gt", [4, 12, 512], F32) as gt_t, \
            nc.sbuf_tensor("E3", [68, 512], F32) as E3_t, \
            nc.sbuf_tensor("gsm", [4, 8, 8], F32) as gsm_t, \
            nc.sbuf_tensor("drhs", [4, 4, 8], F32) as drhs_t, \
            nc.sbuf_tensor("ktok", [128, 2, 1024], BF16) as ktok_t, \
            nc.sbuf_tensor("wvp", [128, 2, 4, 257], BF16) as wvp_t:
        wkv, wg = wkv_t.ap(), wg_t.ap()
        xTb = [xTb0_t.ap(), xTb1_t.ap(), xTb2_t.ap(), xTb3_t.ap()]
        gt, E3, gsm, drhs, ktokp, wvp = gt_t.ap(), E3_t.ap(), gsm_t.ap(), drhs_t.ap(), ktok_t.ap(), wvp_t.ap()
        wkv_r, wg_r = Res("wkv"), Res("wg")
        xTb_r = [Res("xTb%d" % i) for i in range(4)]
        gt_r = [Res("gt%d" % i) for i in range(12)]
        E3_r, drhs_r = Res("E3"), Res("drhs")
        gsm_r = [Res("gsm%d" % i) for i in range(8)]
        ktokp_r = [Res("ktokp0"), Res("ktokp1")]
        wvp_r = [Res("wvp0"), Res("wvp1")]
        s_pw = Stream(nc, "pw")
        s_x = [Stream(nc, "xTb%d" % i) for i in range(4)]
        s_vv = [Stream(nc, "vblk0"), Stream(nc, "vblk1")]
        s_vn = [Stream(nc, "nblk0"), Stream(nc, "nblk1")]

        w_in_k = w_in.rearrange("(k p) n -> p k n", p=128)
        kb.dma("pool", s_pw, wg, w_in_k[:, :, 6144:6152], reads=[R_in], writes=[wg_r])
        kb.dma("pool", s_pw, wkv[:, :, 0:1024], w_in_k[:, :, 3072:4096], reads=[R_in], writes=[wkv_r])
        kb.dma("pool", s_pw, wkv[:, :, 1024:2048], w_in_k[:, :, 4096:5120], reads=[R_in], writes=[wkv_r])
        tok_pw = (s_pw.sem, s_pw.count, "dma:pw")
        wg_r.w = tok_pw
        wkv_r.w = tok_pw
        kb.op("dve", lambda e: e.memset(E3, 0.0), writes=[E3_r])
        for r_ in ktokp_r + wvp_r:
            r_.dj = True

        xTs = xT_seq.rearrange("(k p) t -> p k t", p=128)
        xTm = xT_smp.rearrange("(k p) t -> p k t", p=128)

        xsrcs = [(xTs[:, :, b_ * 512:(b_ + 1) * 512], 512) for b_ in range(16)] + [(xTm[:, :, 0:NSMP], NSMP)]
        xissued = [0]

        def xget(i):
            while xissued[0] < len(xsrcs) and xissued[0] <= i + 1:
                j = xissued[0]
                src_, n_ = xsrcs[j]
                kb.dma("pool", s_x[j % 4], xTb[j % 4][:, :, 0:n_], src_, reads=[R_in], writes=[xTb_r[j % 4]])
                xissued[0] += 1
            return xTb[i % 4], xTb_r[i % 4]

        def conv_dma(g, src):
            kb.dma("pool", s_conv[g], wsc[g].rearrange("p (k n) -> p k n", k=src.shape[1]), src, reads=[R_in], writes=[R_wsc[g]])

        conv_jobs = []
        for gi, c0 in enumerate(WIN_COL):
            conv_jobs.append((gi, w_in_k[:, :, c0:c0 + 512]))
        for (g0, wd) in ((G_A, w_a), (G_B, w_b), (G_OUT, w_o)):
            wk_ = wd.rearrange("(k p) n -> p k n", p=128)
            for i in range(2):
                conv_jobs.append((g0 + i, wk_[:, :, i * 512:(i + 1) * 512]))
        wk_ = w_f1.rearrange("(k p) n -> p k n", p=128)
        for i in range(8):
            conv_jobs.append((G_F1 + i, wk_[:, :, i * 512:(i + 1) * 512]))
        wk_ = w_f2.rearrange("(k p) n -> p k n", p=128)
        for i in range(8):
            conv_jobs.append((G_F2 + i, wk_[:, 4 * i:4 * i + 4, :]))

        use_order = [2, 3, 0, 1, G_Q, G_Q + 1, G_K, G_K + 1, G_V, G_V + 1, G_O, G_O + 1, G_GB, G_GB + 1,
                     G_GA, G_GA + 1, G_A, G_A + 1, G_B, G_B + 1, G_OUT, G_OUT + 1] + [G_F1 + i for i in range(4)] + \
                    [G_F2 + i for i in range(4)] + [G_F1 + 4 + i for i in range(4)] + [G_F2 + 4 + i for i in range(4)]
        conv_jobs.sort(key=lambda j_: use_order.index(j_[0]))
        def gate_block(blk, n, chunk0, tile0, sample, grouped):
            nch = n // 64
            xt, xt_r = xget(blk)
            vs, ns_ = (5, 6) if blk % 2 == 0 else (7, 8)
            for _ in range(1):
                if conv_jobs:
                    conv_dma(*conv_jobs.pop(0))
            def load_mask(b_):
                v_, n_2 = (5, 6) if b_ % 2 == 0 else (7, 8)
                kb.dma("sp", s_vv[b_ % 2], gt[:, v_, 0:512], valid_d[:, b_ * 512:b_ * 512 + 512], reads=[R_in], writes=[gt_r[v_]])
                kb.dma("sp", s_vn[b_ % 2], gt[:, n_2, 0:512], nbig_d[:, b_ * 512:b_ * 512 + 512], reads=[R_in], writes=[gt_r[n_2]])
            if blk == 0:
                load_mask(0)
            if blk + 1 < 16:
                load_mask(blk + 1)
            zi, zf = pb[4], pb[5]
            for k in range(8):
                kb.op("pe", lambda e, k=k: e.matmul(zi[0:4, 0:n], lhsT=wg[:, k, 0:4], rhs=xt[:, k, 0:n], start=(k == 0), stop=(k == 7)),
                      reads=[wg_r, xt_r], writes=[pbr[4]])
            for k in range(8):
                kb.op("pe", lambda e, k=k: e.matmul(zf[0:4, 0:n], lhsT=wg[:, k, 4:8], rhs=xt[:, k, 0:n], start=(k == 0), stop=(k == 7)),
                      reads=[wg_r, xt_r], writes=[pbr[5]])
            s1_, s2_ = (1, 2) if blk % 2 == 0 else (9, 10)
            te, nlfm, igm, bneg, g = gt[:, 0, 0:n], gt[:, s1_, 0:n], gt[:, s2_, 0:n], gt[:, 3, 0:n], gt[:, 4, 0:n]
            kb.op("act", lambda e: e.activation(out=te, in_=zf[0:4, 0:n], func=AF.Exp, scale=-1.0, bias=nbf[:, 0:1]),
                  reads=[pbr[5], nbf_r], writes=[gt_r[0]])
            kb.op("act", lambda e: e.activation(out=te, in_=te, func=AF.Ln, bias=1.0), reads=[gt_r[0]], writes=[gt_r[0]])
            if not sample:
                kb.op("dve", lambda e: e.tensor_tensor(out=nlfm, in0=te, in1=gt[:, vs, 0:n], op=ALU.mult),
                      reads=[gt_r[0], gt_r[vs]], writes=[gt_r[s1_]])
                kb.op("dve", lambda e: e.scalar_tensor_tensor(out=igm, in0=zi[0:4, 0:n], scalar=gbt[:, 0:1], in1=gt[:, vs, 0:n],
                                                              op0=ALU.add, op1=ALU.mult),
                      reads=[pbr[4], gbt_r, gt_r[vs]], writes=[gt_r[s2_]])
                kb.op("dve", lambda e: e.tensor_tensor(out=igm, in0=igm, in1=gt[:, ns_, 0:n], op=ALU.add),
                      reads=[gt_r[s2_], gt_r[ns_]], writes=[gt_r[s2_]])
            else:
                kb.op("dve", lambda e: e.tensor_copy(out=nlfm, in_=te), reads=[gt_r[0]], writes=[gt_r[s1_]])
                kb.op("dve", lambda e: e.tensor_scalar(out=igm, in0=zi[0:4, 0:n], scalar1=gbt[:, 0:1], scalar2=None, op0=ALU.add),
                      reads=[pbr[4], gbt_r], writes=[gt_r[s2_]])
            yield
            kb.op("dve", lambda e: e.tensor_tensor_scan(out=bneg, data0=resetm[:, 0:n], data1=nlfm, initial=0.0,
                                                        op0=ALU.mult, op1=ALU.add),
                  reads=[resetm_r, gt_r[s1_]], writes=[gt_r[3]])
            yield
            kb.op("dve", lambda e: e.tensor_tensor(out=g, in0=igm, in1=bneg, op=ALU.add),
                  reads=[gt_r[s2_], gt_r[3]], writes=[gt_r[4]])
            yield
            g3 = g.rearrange("p (c l) -> p c l", l=64)
            b3 = bneg.rearrange("p (c l) -> p c l", l=64)
            gmax, Mc, nd, dec = gsm[:, 0, 0:nch], gsm[:, 1, 0:nch], gsm[:, 2, 0:nch], gsm[:, 3, 0:nch]
            nbtot = b3[:, :, 63]
            kb.op("dve", lambda e: e.tensor_reduce(out=gmax, in_=g3, axis=AX.X, op=ALU.max), reads=[gt_r[4]], writes=[gsm_r[0]])
            yield
            if not sample:
                kb.op("dve", lambda e: e.tensor_tensor_scan(out=m_all[:, chunk0 + 1:chunk0 + 1 + nch], data0=gmax, data1=nbtot,
                                                            initial=m_all[:, chunk0:chunk0 + 1], op0=ALU.max, op1=ALU.subtract),
                      reads=[gsm_r[0], gt_r[3], m_all_r], writes=[m_all_r])
                yield
                m0 = m_all[:, chunk0:chunk0 + nch]
                m0_r = m_all_r
            else:
                m0 = msin[:, 0:nch]
                m0_r = msin_r
            kb.op("dve", lambda e: e.tensor_tensor(out=Mc, in0=gmax, in1=m0, op=ALU.max), reads=[gsm_r[0], m0_r], writes=[gsm_r[1]])
            yield
            if sample:
                kb.op("dve", lambda e: e.tensor_tensor(out=mend_s[:, 0:nch], in0=Mc, in1=nbtot, op=ALU.subtract),
                      reads=[gsm_r[1], gt_r[3]], writes=[mend_s_r])
                yield
            kb.op("dve", lambda e: e.tensor_tensor(out=nd, in0=m0, in1=Mc, op=ALU.subtract), reads=[gsm_r[1], m0_r], writes=[gsm_r[2]])
            yield
            if grouped:
                Mc2, ndg = gsm[:, 4, 0:nch], gsm[:, 5, 0:nch // 2]
                Mcv = Mc.rearrange("p (j t) -> p j t", t=2)
                Mc2v = Mc2.rearrange("p (j t) -> p j t", t=2)
                ndv = nd.rearrange("p (j t) -> p j t", t=2)
                kb.op("dve", lambda e: e.tensor_copy(out=Mc2, in_=Mc), reads=[gsm_r[1]], writes=[gsm_r[4]])
                yield
                kb.op("dve", lambda e: e.tensor_tensor(out=Mc2v[:, :, 0], in0=Mcv[:, :, 0], in1=ndv[:, :, 1], op=ALU.subtract),
                      reads=[gsm_r[1], gsm_r[2], gsm_r[4]], writes=[gsm_r[4]])
                yield
                kb.op("dve", lambda e: e.tensor_tensor(out=ndg, in0=ndv[:, :, 0], in1=ndv[:, :, 1], op=ALU.add), reads=[gsm_r[2]], writes=[gsm_r[5]])
                yield
                Mw, Mw_r, ndx, ndx_r, ndec = Mc2, gsm_r[4], ndg, gsm_r[5], nch // 2
            else:
                Mw, Mw_r, ndx, ndx_r, ndec = Mc, gsm_r[1], nd, gsm_r[2], nch
            e3a = E3[0:4, 0:n].rearrange("p (c l) -> p c l", l=64)
            e3b = E3[32:36, 0:n].rearrange("p (c l) -> p c l", l=64)
            e3c = E3[64:68, 0:n].rearrange("p (c l) -> p c l", l=64)
            m0b = m0.unsqueeze(2).to_broadcast([4, nch, 64])
            Mcb = Mw.unsqueeze(2).to_broadcast([4, nch, 64])
            kb.op("dve", lambda e: e.tensor_tensor(out=e3a, in0=g3, in1=m0b, op=ALU.subtract), reads=[gt_r[4], m0_r], writes=[E3_r])
            yield
            kb.op("dve", lambda e: e.tensor_tensor(out=e3b, in0=g3, in1=Mcb, op=ALU.subtract), reads=[gt_r[4], Mw_r], writes=[E3_r])
            yield
            kb.op("dve", lambda e: e.tensor_tensor(out=e3c, in0=b3, in1=m0b, op=ALU.subtract), reads=[gt_r[3], m0_r], writes=[E3_r])
            yield
            for r0 in (0, 32, 64):
                kb.op("act", lambda e, r0=r0: e.activation(out=E3[r0:r0 + 4, 0:n], in_=E3[r0:r0 + 4, 0:n], func=AF.Exp), reads=[E3_r], writes=[E3_r])
                yield
            decx = gsm[:, 3, 0:ndec]
            kb.op("act", lambda e: e.activation(out=decx, in_=ndx, func=AF.Exp), reads=[ndx_r], writes=[gsm_r[3]])
            yield
            kb.op("dve", lambda e: e.tensor_tensor(out=drhs[:, :, 0:ndec], in0=oh[:, :, 0:ndec],
                                                   in1=decx.unsqueeze(1).to_broadcast([4, 4, ndec]), op=ALU.mult),
                  reads=[oh_r, gsm_r[3]], writes=[drhs_r])
            yield
            pd = pb[6][:, 0:4 * ndec].rearrange("p (h c) -> p h c", h=4)
            kb.op("pe", lambda e: e.matmul(pd, lhsT=ones4[0:4, :], rhs=drhs[:, :, 0:ndec], start=True, stop=True),
                  reads=[ones4_r, drhs_r], writes=[pbr[6]])
            yield
            kb.op("act", lambda e: e.activation(out=decb[:, blk, :, 0:ndec], in_=pd, func=AF.Identity), reads=[pbr[6]], writes=[decb_r])
            yield
            ntt = n // 128
            ptE = pb[6][:, 64:64 + 4 * 68].rearrange("p (t x) -> p t x", x=68)
            for tt in range(ntt):
                kb.op("pe", lambda e, tt=tt: e.transpose(out=ptE[:, tt, :], in_=E3[0:68, tt * 128:(tt + 1) * 128], identity=ident[0:68, 0:68]),
                      reads=[E3_r, ident_r], writes=[pbr[6]])
                yield
            src2 = pb[6][:, 64:64 + 4 * 68].rearrange("p (t x) -> p t x", x=68)[:, 0:ntt, 0:64].rearrange("p t (q x) -> p t q x", x=32)[:, :, :, 0:4]
            kb.op("act", lambda e: e.activation(out=tokE[:, tile0:tile0 + ntt, :, :], in_=src2, func=AF.Identity), reads=[pbr[6]], writes=[tokE_r])
            yield
            pf = pb[6][0:64, 400:400 + 4 * nch].rearrange("p (c x) -> p c x", x=4)
            for cc in range(nch):
                kb.op("pe", lambda e, cc=cc: e.transpose(out=pf[:, cc, :], in_=E3[64:68, cc * 64:(cc + 1) * 64], identity=ident[64:68, 64:68]),
                      reads=[E3_r, ident_r], writes=[pbr[6]])
                yield
            kb.op("dve", lambda e: e.tensor_copy(out=floorC[:, chunk0:chunk0 + nch, :], in_=pf), reads=[pbr[6]], writes=[floorC_r])
            yield

        ubank = [0]

        ubank = [0]

        def state_update_tile(ktok_ap, ktok_res, wv_ap, wv_res, tile, banks):
            out = []
            for h in range(4):
                for dkc in range(2):
                    def f(h=h, dkc=dkc):
                        bi = banks[ubank[0] % len(banks)]
                        ubank[0] += 1
                        kb.op("pe", lambda e: e.matmul(
                            pb[bi][:, 0:257], lhsT=ktok_ap[:, h * 256 + dkc * 128:h * 256 + dkc * 128 + 128],
                            rhs=wv_ap[:, h, 0:257], start=True, stop=True),
                            reads=[ktok_res, wv_res], writes=[pbr[bi]])
                        kb.op("dve", lambda e: e.scalar_tensor_tensor(
                            out=Cst[h][:, dkc, :], in0=Cst[h][:, dkc, :], scalar=decb[:, tile // 4, h, (tile % 4):(tile % 4) + 1],
                            in1=pb[bi][:, 0:257], op0=ALU.mult, op1=ALU.add),
                            reads=[Cst_rd[h][dkc], decb_r, pbr[bi]], writes=[Cst_rd[h][dkc]])
                    out.append(f)
            return out

        deferred = []

        gstep = [None]

        def pump(n=1):
            for _ in range(n):
                if deferred:
                    deferred.pop(0)()
            if gstep[0] is not None and n == 1:
                for _ in range(2):
                    try:
                        next(gstep[0])
                    except StopIteration:
                        gstep[0] = None
                        break

        evt = [0]

        def kv_tile(xt, xt_r, tsl, wk_ap, wk_r, wv_w_ap, wv_w_r, ktok_ap, ktok_res, wv_ap, wv_res, tileg, vext_ap=None, vext_res=None):
            for hf in range(2):
                bi = hf
                for k in range(8):
                    kb.op("pe", lambda e, k=k, hf=hf, bi=bi: e.matmul(pb[bi][:, 0:512], lhsT=xt[:, k, tsl], rhs=wk_ap(hf)[:, k, :],
                                                                     start=(k == 0), stop=(k == 7)),
                          reads=[xt_r, wk_r(hf)], writes=[pbr[bi]])
                    if k % 4 == 3:
                        pump()
                kb.op("act", lambda e, hf=hf, bi=bi: e.activation(out=ktok_ap[:, hf * 512:(hf + 1) * 512], in_=pb[bi][:, 0:512],
                                                                  func=AF.Identity, scale=0.0625),
                      reads=[pbr[bi]], writes=[ktok_res])
            for hf in range(2):
                bi = 2 + hf
                for k in range(8):
                    kb.op("pe", lambda e, k=k, hf=hf, bi=bi: e.matmul(pb[bi][:, 0:512], lhsT=xt[:, k, tsl], rhs=wv_w_ap(hf)[:, k, :],
                                                                     start=(k == 0), stop=(k == 7)),
                          reads=[xt_r, wv_w_r(hf)], writes=[pbr[bi]])
                    if k % 4 == 3:
                        pump()
                if vext_ap is not None:
                    kb.op("act", lambda e, hf=hf, bi=bi: e.activation(out=vext_ap[:, 2 * hf:2 * hf + 2, 0:256],
                                                                      in_=pb[bi][:, 0:512].rearrange("p (h v) -> p h v", h=2), func=AF.Identity),
                          reads=[pbr[bi]], writes=[vext_res])
                for h2 in range(2):
                    hh_ = 2 * hf + h2
                    kb.op("act", lambda e, hf=hf, bi=bi, h2=h2, hh_=hh_: e.activation(
                        out=wv_ap[:, hh_, 0:256], in_=pb[bi][:, h2 * 256:(h2 + 1) * 256], func=AF.Identity,
                        scale=tokE[:, tileg, 1, hh_:hh_ + 1]),
                        reads=[pbr[bi], tokE_r], writes=[wv_res])
            kb.op("pool", lambda e: e.tensor_copy(out=wv_ap[:, :, 256], in_=tokE[:, tileg, 1, :]), reads=[tokE_r], writes=[wv_res])

        def p2_block(tb):
            xt, xt_r = xTb[tb % 4], xTb_r[tb % 4]
            for _ in range(2):
                if conv_jobs:
                    conv_dma(*conv_jobs.pop(0))
            for t4 in range(4):
                tile = tb * 4 + t4
                i2 = tile % 2
                kv_tile(xt, xt_r, slice(t4 * 128, (t4 + 1) * 128),
                        lambda hf: wkv[:, :, hf * 512:(hf + 1) * 512], lambda hf: wkv_r,
                        lambda hf: wkv[:, :, 1024 + hf * 512:1024 + (hf + 1) * 512], lambda hf: wkv_r,
                        ktokp[:, i2, :], ktokp_r[i2], wvp[:, i2, :, :], wvp_r[i2], tile)
                deferred.extend(state_update_tile(ktokp[:, i2, :], ktokp_r[i2], wvp[:, i2, :, :], wvp_r[i2], tile, [4, 5]))

        gens = [gate_block(blk, 512, blk * 8, blk * 4, False, blk < NPRE // 512) for blk in range(16)]
        gens.append(gate_block(16, NSMP, 128, 64, True, False))
        next(gens[0])
        for bi_ in range(17):
            if bi_ + 1 < 17:
                next(gens[bi_ + 1])
            if 0 <= bi_ - 1 < NPRE // 512:
                gstep[0] = gens[bi_]
                p2_block(bi_ - 1)
                gstep[0] = None
            for _ in gens[bi_]:
                pass
        kb.dma("sp", s_out, m_p, m_all[:, 128:129], reads=[m_all_r], writes=[R_out])
        kb.dma("sp", s_out, m_s, mend_s, reads=[mend_s_r], writes=[R_out])

        pump(1000)
        while conv_jobs:
            conv_dma(*conv_jobs.pop(0))
        s_h = Stream(nc, "xTh")
        kb.dma("pool", s_h, xTh, xTs[:, :, NPRE - 128:NPRE], reads=[R_in], writes=[xTh_r])
        bar_res = [wkv_r, wg_r, E3_r, drhs_r] + xTb_r + gt_r + gsm_r + ktokp_r + wvp_r
        for eng in ("pe", "act", "dve", "pool", "sp"):
            kb._deps(eng, [], bar_res)
            kb._deps(eng, bar_res, [])

    ring, ring_r, s_ring = [], [], []
    for i in range(NW):
        a, r = kb.sb([128, 8, 512], BF16, "ring%d" % i)
        ring.append(a); ring_r.append(r); s_ring.append(Stream(nc, "ring%d" % i))
    xTm_b, xTm_r, s_xm = [], [], []
    for i in range(2):
        a, r = kb.sb([128, 8, T], BF16, "xTm%d" % i)
        xTm_b.append(a); xTm_r.append(r); s_xm.append(Stream(nc, "xTm%d" % i))
    xtok, xtok_r = kb.sb([128, 2, 1024], F32, "xtok")
    s_xt = Stream(nc, "xtok")
    glu, glu_r = kb.sb([128, 8, 376], BF16, "glu")
    glu32, glu32_r = kb.sb([128, 8, 4, 30], F32, "glu32")
    hist, hist_r = kb.sb([128, 8, 30], BF16, "hist")
    sigzh, sigzh_r = kb.sb([128, 8, 128], BF16, "sigzh")
    sigz, sigz_r = kb.sb([128, 8, T], BF16, "sigz")
    diags = [kb.sb([128, 31, 128], BF16, "diag%d" % i) for i in range(2)]
    dw, dw_r = kb.sb([128, 8, T], F32, "dw")
    dwb, sq, t1, t2 = [], [], [], []
    for i in range(2):
        dwb.append(kb.sb([128, T], BF16, "dwb%d" % i))
        sq.append(kb.sb([128, T], BF16, "sq%d" % i))
        t1.append(kb.sb([128, T], F32, "t1_%d" % i))
        t2.append(kb.sb([128, T], F32, "t2_%d" % i))
    mean_sb, mean_r = kb.sb([128, T], F32, "mean_sb")
    rstd, rstd_r = kb.sb([128, T], F32, "rstd")
    aT, aT_r = kb.sb([128, 8, T], BF16, "aT")
    sga, sga_r = kb.sb([128, 8, T], BF16, "sga")
    mrg, mrg_r = dw, dw_r
    mrgb, mrgb_r = kb.sb([128, 8, T], BF16, "mrgb")
    qT, qT_r = kb.sb([128, 8, T], BF16, "qT")
    kT, kT_r = kb.sb([128, 8, T], BF16, "kT")
    ktok, ktok_r = kb.sb([128, 2, 1024], BF16, "ktokm")
    vext, vext_r = kb.sb([128, 2, 4, 257], BF16, "vext")
    wvm, wvm_r = kb.sb([128, 2, 4, 257], BF16, "wvm")
    sog, sog_r = kb.sb([128, 8, T], BF16, "sog")
    sgb, sgb_r = sigz, sigz_r
    pT, pT_r0 = kb.sb([128, 4, 64], BF16, "pT")
    pT_rh = [Res("pT%d" % h) for h in range(4)]
    hbuf, hbuf_r = kb.sb([64, 1024], BF16, "hbuf")
    hgT, hgT_r = kb.sb([128, 8, T], BF16, "hgT")
    sm1, sm1_r = kb.sb([64, 64], F32, "sm1")
    asb, asb_r = [], []
    for i in range(3):
        a_, r_ = kb.sb([64, 4, 257], BF16, "asb%d" % i)
        asb.append(a_); asb_r.append(r_)
    sm1s = [kb.sb([64, 32], F32, "sm1s%d" % i) for i in range(3)]
    bsts = [kb.sb([64, 4, 6], F32, "bsts%d" % i) for i in range(3)]
    hbufs = [(hbuf, hbuf_r), kb.sb([64, 1024], BF16, "hbuf2")]
    bst, bst_r = kb.sb([64, 4, 6], F32, "bst")
    x1T, x1T_r = qT, qT_r
    hff, hff_r = kb.sb([128, 16, T], BF16, "hff")
    rl = t1
    lsm, lsm_r = kb.sb([128, 16], F32, "lsm")
    lst, lst_r = kb.sb([128, 2, 2, 6], F32, "lst")
    xtok_rt = [Res("xtok_t0"), Res("xtok_t1")]
    lsm_rt = [Res("lsm0"), Res("lsm1")]
    lst_rt = [Res("lst0"), Res("lst1")]
    cch, cch_r = kb.sb([30, 1024], F32, "cch")
    cout, cout_r = kb.sb([30, 1024], F32, "cout")
    s_cch = Stream(nc, "cch")
    s_st = [Stream(nc, "state%d" % h) for h in range(4)]
    s_y = Stream(nc, "ystore")
    for r_ in (sigz_r, sga_r, qT_r, kT_r, aT_r, hff_r, dw_r, mrgb_r, ktok_r, wvm_r, hgT_r, sog_r, vext_r, glu32_r, cout_r):
        r_.dj = True
    kb.op("dve", lambda e: e.memset(vext, 1.0), writes=[vext_r])
    kb.op("dve", lambda e: e.memset(glu, 0.0), writes=[glu_r])

    blk_groups = [2, 3, 0, 1, G_Q, G_Q + 1, G_K, G_K + 1, G_V, G_V + 1, G_O, G_O + 1, G_GB, G_GB + 1,
                  G_GA, G_GA + 1, G_A, G_A + 1, G_B, G_B + 1, G_OUT, G_OUT + 1] + [G_F1 + i for i in range(4)] + [G_F2 + i for i in range(4)] + [G_F1 + 4 + i for i in range(4)] + [G_F2 + 4 + i for i in range(4)]
    NBLK = NMAIN // T + 1
    sched = blk_groups * NBLK
    nxt = [0, 0]

    def ring_use():
        i = nxt[1]
        while nxt[0] < len(sched) and nxt[0] <= i + NW - 1:
            j = nxt[0]
            s = j % NW
            kb.dma("sp", s_ring[s], ring[s], wsc[sched[j]].rearrange("p (k n) -> p k n", k=8), reads=[R_wsc[sched[j]]], writes=[ring_r[s]])
            nxt[0] += 1
        nxt[1] += 1
        return ring[i % NW], ring_r[i % NW]

    bankrr = [0]

    def nbank():
        b = bankrr[0] % 4
        bankrr[0] += 1
        return b

    def proj_fm(wt, wt_r, act, act_r, n, evac, banks=None):
        for ocl in range(4):
            b = nbank() if banks is None else banks[ocl % len(banks)]
            for k in range(8):
                kb.op("pe", lambda e, k=k, b=b, ocl=ocl: e.matmul(pb[b][:, 0:n], lhsT=wt[:, k, ocl * 128:(ocl + 1) * 128], rhs=act[:, k, 0:n],
                                                                  start=(k == 0), stop=(k == 7)),
                      reads=[wt_r, act_r], writes=[pbr[b]])
            evac(ocl, pb[b][:, 0:n], pbr[b])

    def proj_tm(wt, wt_r, act, act_r, ntile, evac):
        for tt in range(ntile):
            b = nbank()
            for k in range(8):
                kb.op("pe", lambda e, k=k, b=b, tt=tt: e.matmul(pb[b][:, 0:512], lhsT=act[:, k, tt * 128:(tt + 1) * 128], rhs=wt[:, k, :],
                                                                start=(k == 0), stop=(k == 7)),
                      reads=[wt_r, act_r], writes=[pbr[b]])
            evac(tt, pb[b][:, 0:512], pbr[b])

    xw_loaded = [0]

    def load_xm(blk):
        i = blk % 2
        if blk < NMAIN // T:
            src = xTs[:, :, NPRE + blk * T:NPRE + (blk + 1) * T]
        else:
            src = xTm[:, :, 0:T]
        kb.dma("pool", s_xm[i], xTm_b[i], src, reads=[R_in], writes=[xTm_r[i]])

    load_xm(0)
    for h in range(4):
        kb.op("act", lambda e, h=h: e.activation(out=Cb[h], in_=Cst[h], func=AF.Identity), reads=Cst_rd[h], writes=[Cb_r[h]])

    for blk in range(NBLK):
        sample = blk == NBLK - 1
        nseg, L = (4, 64) if sample else (1, T)
        xt, xt_r = xTm_b[blk % 2], xTm_r[blk % 2]
        gl = glu[:, :, 0:nseg * (30 + L)].rearrange("p c (s l) -> p c s l", s=nseg)
        tok_lo = blk * T
        last_p = blk == NBLK - 2
        kb.dma("sp", s_xt, xtok, x_tok[tok_lo:tok_lo + T, :].rearrange("(t p) f -> p t f", p=128), reads=[R_in], writes=xtok_rt)

        for gi in (2, 3, 0, 1):
            wt, wt_r = ring_use()
            if gi >= 2:
                def ev(ocl, ps, ps_r, gi=gi):
                    c = (gi - 2) * 4 + ocl
                    kb.op("act", lambda e: e.activation(out=sigz[:, c, :], in_=ps, func=AF.Sigmoid), reads=[ps_r], writes=[sigz_r])
                proj_fm(wt, wt_r, xt, xt_r, T, ev)
                if blk == 0:
                    def evh(ocl, ps, ps_r, gi=gi):
                        c = (gi - 2) * 4 + ocl
                        kb.op("act", lambda e: e.activation(out=sigzh[:, c, :], in_=ps, func=AF.Sigmoid), reads=[ps_r], writes=[sigzh_r])
                    proj_fm(wt, wt_r, xTh, xTh_r, 128, evh)
            else:
                def ev(ocl, ps, ps_r, gi=gi):
                    c = gi * 4 + ocl
                    kb.op("dve", lambda e: e.tensor_tensor(out=gl[:, c, :, 30:30 + L], in0=ps.rearrange("p (s l) -> p s l", s=nseg),
                                                           in1=sigz[:, c, :].rearrange("p (s l) -> p s l", s=nseg), op=ALU.mult),
                          reads=[ps_r, sigz_r], writes=[glu_r])
                    if sample or last_p:
                        kb.op("dve", lambda e: e.tensor_tensor(out=glu32[:, c, 0:nseg, :],
                                                               in0=ps.rearrange("p (s l) -> p s l", s=nseg)[:, :, L - 30:L],
                                                               in1=sigz[:, c, :].rearrange("p (s l) -> p s l", s=nseg)[:, :, L - 30:L], op=ALU.mult),
                              reads=[ps_r, sigz_r], writes=[glu32_r])
                if blk == 0:
                    def evh(ocl, ps, ps_r, gi=gi):
                        c = gi * 4 + ocl
                        kb.op("dve", lambda e: e.tensor_tensor(out=gl[:, c, 0, 0:30], in0=ps[:, 98:128], in1=sigzh[:, c, 98:128], op=ALU.mult),
                              reads=[ps_r, sigzh_r], writes=[glu_r])
                    proj_fm(wt, wt_r, xTh, xTh_r, 128, evh)
                proj_fm(wt, wt_r, xt, xt_r, T, ev)
        if sample:
            for s in range(4):
                pc = pb[6][:, 0:240].rearrange("p (c r) -> p c r", r=30)
                kb.dma("sp", s_cch, cch, cache_c[s], reads=[R_in], writes=[cch_r])
                for c in range(8):
                    kb.op("pe", lambda e, s=s, c=c: e.transpose(out=pc[:, c, :], in_=cch[0:30, c * 128:(c + 1) * 128], identity=ident[0:30, 0:30]),
                          reads=[cch_r, ident_r], writes=[pbr[6]])
                kb.op("act", lambda e, s=s: e.activation(out=gl[:, :, s, 0:30], in_=pc, func=AF.Identity), reads=[pbr[6]], writes=[glu_r])
        elif blk > 0:
            kb.op("pool", lambda e: e.tensor_copy(out=gl[:, :, 0, 0:30], in_=hist), reads=[hist_r], writes=[glu_r])
        if not sample:
            kb.op("pool", lambda e: e.tensor_copy(out=hist, in_=gl[:, :, 0, L:L + 30]), reads=[glu_r], writes=[hist_r])
        if blk + 1 < NBLK:
            load_xm(blk + 1)
        if sample or last_p:
            for s in range(nseg):
                pc = pb[6][0:30, 0:512]
                pc2 = pb[5][0:30, 0:512]
                for c in range(8):
                    dst = (pc if c < 4 else pc2)[:, (c % 4) * 128:(c % 4 + 1) * 128]
                    kb.op("pe", lambda e, s=s, c=c, dst=dst: e.transpose(out=dst, in_=glu32[:, c, s, :], identity=ident),
                          reads=[glu32_r, ident_r], writes=[pbr[6], pbr[5]])
                kb.op("act", lambda e: e.activation(out=cout[:, 0:512], in_=pc, func=AF.Identity), reads=[pbr[6]], writes=[cout_r])
                kb.op("act", lambda e: e.activation(out=cout[:, 512:1024], in_=pc2, func=AF.Identity), reads=[pbr[5]], writes=[cout_r])
                kb.dma("sp", s_out, conv_s[s] if sample else conv_p, cout, reads=[cout_r], writes=[R_out])

        tile_g0 = (NPRE + blk * T) // 128 if not sample else 64
        for gi in range(2):
            wt, wt_r = ring_use()
            def ev(ocl, ps, ps_r, gi=gi):
                kb.op("act", lambda e: e.activation(out=qT[:, gi * 4 + ocl, :], in_=ps, func=AF.Identity), reads=[ps_r], writes=[qT_r])
            proj_fm(wt, wt_r, xt, xt_r, T, ev)
        for gi in range(2):
            wt, wt_r = ring_use()
            def ev(ocl, ps, ps_r, gi=gi):
                kb.op("act", lambda e: e.activation(out=kT[:, gi * 4 + ocl, :], in_=ps, func=AF.Identity, scale=0.0625), reads=[ps_r], writes=[kT_r])
            proj_fm(wt, wt_r, xt, xt_r, T, ev)
            def evt_(tt, ps, ps_r, gi=gi):
                kb.op("dve", lambda e: e.tensor_scalar(out=ktok[:, tt, gi * 512:(gi + 1) * 512], in0=ps, scalar1=0.0625, scalar2=None, op0=ALU.mult),
                      reads=[ps_r], writes=[ktok_r])
            proj_tm(wt, wt_r, xt, xt_r, T // 128, evt_)
        for gi in range(2):
            wt, wt_r = ring_use()
            def evt_(tt, ps, ps_r, gi=gi):
                p3 = ps.rearrange("p (h v) -> p h v", h=2)
                kb.op("act", lambda e: e.activation(out=vext[:, tt, 2 * gi:2 * gi + 2, 0:256], in_=p3, func=AF.Identity), reads=[ps_r], writes=[vext_r])
                kb.op("dve", lambda e: e.tensor_tensor(out=wvm[:, tt, 2 * gi:2 * gi + 2, 0:256], in0=p3,
                                                       in1=tokE[:, tile_g0 + tt, 1, 2 * gi:2 * gi + 2].unsqueeze(2).to_broadcast([128, 2, 256]), op=ALU.mult),
                      reads=[ps_r, tokE_r], writes=[wvm_r])
            proj_tm(wt, wt_r, xt, xt_r, T // 128, evt_)
        for tt in range(T // 128):
            kb.op("dve", lambda e, tt=tt: e.tensor_copy(out=wvm[:, tt, :, 256], in_=tokE[:, tile_g0 + tt, 1, :]), reads=[tokE_r], writes=[wvm_r])
        for gi in range(2):
            wt, wt_r = ring_use()
            def ev(ocl, ps, ps_r, gi=gi):
                oc = gi * 4 + ocl
                kb.op("act", lambda e: e.activation(out=sog[:, oc, :], in_=ps, func=AF.Sigmoid), reads=[ps_r], writes=[sog_r])
                kb.op("dve", lambda e: e.tensor_scalar(out=sog[:, oc, :], in0=sog[:, oc, :], scalar1=pv[:, 24 + oc:25 + oc], scalar2=None, op0=ALU.mult),
                      reads=[sog_r, pv_r], writes=[sog_r])
            proj_fm(wt, wt_r, xt, xt_r, T, ev)
        for gi in range(2):
            wt, wt_r = ring_use()
            def ev(ocl, ps, ps_r, gi=gi):
                kb.op("act", lambda e: e.activation(out=sgb[:, gi * 4 + ocl, :], in_=ps, func=AF.Sigmoid), reads=[ps_r], writes=[sgb_r])
            proj_fm(wt, wt_r, xt, xt_r, T, ev)

        pend = None
        for c in range(8):
            def build_diag(c2):
                dg, dg_r = diags[c2 % 2]
                kb.op("dve" if c2 % 2 == 0 else "pool", lambda e: e.tensor_tensor(out=dg, in0=identb.unsqueeze(1).to_broadcast([128, 31, 128]),
                                                       in1=pv[:, 32 + c2 * 31:32 + (c2 + 1) * 31].unsqueeze(2).to_broadcast([128, 31, 128]),
                                                       op=ALU.mult),
                      reads=[identb_r, pv_r], writes=[dg_r])
            diag, diag_r = diags[c % 2]
            if c == 0:
                build_diag(0)
            b = nbank()
            po = pb[b][:, 0:T].rearrange("p (s l) -> p s l", s=nseg)
            for j in range(31):
                kb.op("pe", lambda e, j=j, c=c, po=po, diag=diag: e.matmul(po, lhsT=diag[:, j, :], rhs=gl[:, c, :, j:j + L], start=(j == 0), stop=(j == 30)),
                      reads=[diag_r, glu_r], writes=[pbr[b]])
            if c + 1 < 8:
                build_diag(c + 1)
            i2 = c % 2
            kb.op("dve", lambda e, c=c, b=b: e.tensor_scalar(out=dw[:, c, :], in0=pb[b][:, 0:T], scalar1=pv[:, c:c + 1], scalar2=None, op0=ALU.add),
                  reads=[pbr[b], pv_r], writes=[dw_r])
            kb.op("act", lambda e, c=c, b=b, i2=i2: e.activation(out=sq[i2][0], in_=pb[b][:, 0:T], func=AF.Square, bias=pv[:, c:c + 1]),
                  reads=[pbr[b], pv_r], writes=[sq[i2][1]])
            kb.op("act", lambda e, c=c, b=b, i2=i2: e.activation(out=dwb[i2][0], in_=pb[b][:, 0:T], func=AF.Identity, bias=pv[:, c:c + 1]),
                  reads=[pbr[b], pv_r], writes=[dwb[i2][1]])

            def stats(c=c, i2=i2):
                kb.op("pe", lambda e: e.matmul(pb[4][:, 0:T], lhsT=onesb, rhs=dwb[i2][0], start=(c == 0), stop=(c == 7)),
                      reads=[onesb_r, dwb[i2][1]], writes=[pbr[4]])
                kb.op("pe", lambda e: e.matmul(pb[5][:, 0:T], lhsT=onesb, rhs=sq[i2][0], start=(c == 0), stop=(c == 7)),
                      reads=[onesb_r, sq[i2][1]], writes=[pbr[5]])
            if pend is not None:
                pend()
            pend = stats
        pend()
        def ln_finalize_a():
            kb.op("dve", lambda e: e.tensor_copy(out=mean_sb, in_=pb[4][:, 0:T]), reads=[pbr[4]], writes=[mean_r])
            kb.op("pool", lambda e: e.tensor_tensor(out=t1[0][0], in0=mean_sb, in1=mean_sb, op=ALU.mult), reads=[mean_r], writes=[t1[0][1]])
            kb.op("dve", lambda e: e.tensor_tensor(out=rstd, in0=pb[5][:, 0:T], in1=t1[0][0], op=ALU.subtract), reads=[pbr[5], t1[0][1]], writes=[rstd_r])
        def ln_finalize_b():
            kb.op("dve", lambda e: e.tensor_scalar(out=rstd, in0=rstd, scalar1=0.0, scalar2=EPS, op0=ALU.max, op1=ALU.add), reads=[rstd_r], writes=[rstd_r])
            kb.op("act", lambda e: e.activation(out=rstd, in_=rstd, func=AF.Sqrt), reads=[rstd_r], writes=[rstd_r])
            kb.op("dve", lambda e: e.reciprocal(out=rstd, in_=rstd), reads=[rstd_r], writes=[rstd_r])
            for c in range(8):
                i2 = c % 2
                kb.op("pool", lambda e, c=c, i2=i2: e.tensor_tensor(out=t1[i2][0], in0=dw[:, c, :], in1=mean_sb, op=ALU.subtract),
                      reads=[dw_r, mean_r], writes=[t1[i2][1]])
                kb.op("dve", lambda e, i2=i2: e.tensor_tensor(out=t2[i2][0], in0=t1[i2][0], in1=rstd, op=ALU.mult),
                      reads=[t1[i2][1], rstd_r], writes=[t2[i2][1]])
                kb.op("act", lambda e, c=c, i2=i2: e.activation(out=aT[:, c, :], in_=t2[i2][0], func=AF.Silu, scale=pv[:, 8 + c:9 + c], bias=pv[:, 16 + c:17 + c]),
                      reads=[t2[i2][1], pv_r], writes=[aT_r])
        def chunk_ids(cc):
            tt, half = cc // 2, cc % 2
            cg = (NPRE + blk * T) // 64 + cc if not sample else 128 + cc
            return tt, half, half * 64, cc * 64, cg, tile_g0 + tt

        def mlstm_crit(cc):
            tt, half, p0, tok0, cg, tg = chunk_ids(cc)
            ai = asb[cc % 3]
            ai_r = asb_r[cc % 3]
            if sample:
                for h in range(4):
                    kb.dma("sp", s_st[h], Cst[h][:, :, 0:256], sC[cc, h].rearrange("(c p) v -> p c v", p=128), reads=[R_in], writes=Cst_rd[h])
                    kb.dma("sp", s_st[h], Cst[h][:, :, 256], sn[cc, h].rearrange("(c p) -> p c", p=128), reads=[R_in], writes=Cst_rd[h],
                           allow_slow_non_contiguous=True)
                    kb.op("act", lambda e, h=h: e.activation(out=Cb[h], in_=Cst[h], func=AF.Identity), reads=Cst_rd[h], writes=[Cb_r[h]])
            if half == 0:
                for h in range(4):
                    for dkc in range(2):
                        kb.op("pe", lambda e, h=h, dkc=dkc: e.matmul(pb[4][:, h * 128:(h + 1) * 128], lhsT=kT[:, 2 * h + dkc, tt * 128:(tt + 1) * 128],
                                                                     rhs=qT[:, 2 * h + dkc, tt * 128:(tt + 1) * 128], start=(dkc == 0), stop=(dkc == 1)),
                              reads=[kT_r, qT_r], writes=[pbr[4]])
            for h in range(4):
                kb.op("dve", lambda e, h=h: e.scalar_tensor_tensor(out=pT[p0:p0 + 64, h, :], in0=pb[4][p0:p0 + 64, h * 128 + p0:h * 128 + p0 + 64],
                                                                   scalar=tokE[p0:p0 + 64, tg, 0, h:h + 1], in1=cmask[p0:p0 + 64, :],
                                                                   op0=ALU.mult, op1=ALU.mult),
                      reads=[pbr[4], tokE_r, cmask_r], writes=[pT_rh[h]])
            def upd(h):
                for dkc in range(2):
                    bi = dkc
                    kb.op("pe", lambda e, dkc=dkc, bi=bi: e.matmul(
                        pb[bi][:, 0:257], lhsT=ktok[p0:p0 + 64, tt, h * 256 + dkc * 128:h * 256 + dkc * 128 + 128],
                        rhs=wvm[p0:p0 + 64, tt, h, 0:257], start=True, stop=True),
                        reads=[ktok_r, wvm_r], writes=[pbr[bi]])
                    kb.op("dve", lambda e, dkc=dkc, bi=bi: e.scalar_tensor_tensor(
                        out=Cst[h][:, dkc, :], in0=Cst[h][:, dkc, :], scalar=decb[:, cg // 8, h, (cg % 8):(cg % 8) + 1],
                        in1=pb[bi][:, 0:257], op0=ALU.mult, op1=ALU.add),
                        reads=[Cst_rd[h][dkc], decb_r, pbr[bi]], writes=[Cst_rd[h][dkc]])

            for h in range(4):
                upd(h)
                nb = (5, 6, 3, 2)[h]
                on = pb[nb][0:64, 256:512] if h == 3 else pb[nb][0:64, 0:256]
                od = pb[2][0:64, h:h + 1]
                kb.op("pe", lambda e, h=h, on=on: e.matmul(on, lhsT=pT[p0:p0 + 64, h, :], rhs=vext[p0:p0 + 64, tt, h, 0:256], start=True, stop=False),
                      reads=[pT_rh[h], vext_r], writes=[pbr[nb]])
                for dkc in range(2):
                    kb.op("pe", lambda e, h=h, dkc=dkc, on=on: e.matmul(on, lhsT=qT[:, 2 * h + dkc, tok0:tok0 + 64], rhs=Cb[h][:, dkc, 0:256],
                                                                        start=False, stop=(dkc == 1)),
                          reads=[qT_r, Cb_r[h]], writes=[pbr[nb]])
                kb.op("pe", lambda e, h=h, od=od: e.matmul(od, lhsT=pT[p0:p0 + 64, h, :], rhs=vext[p0:p0 + 64, tt, h, 256:257], start=True, stop=False),
                      reads=[pT_rh[h], vext_r], writes=[pbr[2]])
                for dkc in range(2):
                    kb.op("pe", lambda e, h=h, dkc=dkc, od=od: e.matmul(od, lhsT=qT[:, 2 * h + dkc, tok0:tok0 + 64], rhs=Cb[h][:, dkc, 256:257],
                                                                        start=False, stop=(dkc == 1)),
                          reads=[qT_r, Cb_r[h]], writes=[pbr[2]])
                kb.op("act", lambda e, h=h, on=on: e.activation(out=ai[:, h, 0:256], in_=on, func=AF.Identity), reads=[pbr[nb]], writes=[ai_r])
                if sample:
                    kb.dma("sp", s_out, Cn_s[cc, h].rearrange("c p v -> p c v"), Cst[h], reads=Cst_rd[h], writes=[R_out])
                elif last_p and cc == T // 64 - 1:
                    kb.dma("sp", s_out, Cn_p[h].rearrange("c p v -> p c v"), Cst[h], reads=Cst_rd[h], writes=[R_out])
                if not (sample or (last_p and cc == T // 64 - 1)):
                    kb.op("act", lambda e, h=h: e.activation(out=Cb[h], in_=Cst[h], func=AF.Identity), reads=Cst_rd[h], writes=[Cb_r[h]])
            kb.op("act", lambda e: e.activation(out=ai[:, :, 256], in_=pb[2][0:64, 0:4], func=AF.Identity), reads=[pbr[2]], writes=[ai_r])

        def norm_a(cc):
            tt, half, p0, tok0, cg, tg = chunk_ids(cc)
            ai, ai_r = asb[cc % 3], asb_r[cc % 3]
            sm, sm_r, bs, bs_r = sm1s[cc % 3][0], sm1s[cc % 3][1], bsts[cc % 3][0], bsts[cc % 3][1]
            d_, d2_, vv_ = sm[:, 0:4], sm[:, 4:8], sm[:, 8:12]
            mv = sm[:, 24:32].rearrange("p (h x) -> p h x", x=2)
            for h in range(4):
                kb.op("dve", lambda e, h=h: e.bn_stats(out=bs[:, h, :], in_=ai[:, h, 0:256]), reads=[ai_r], writes=[bs_r])
                kb.op("dve", lambda e, h=h: e.bn_aggr(out=mv[:, h, :], in_=bs[:, h, :]), reads=[bs_r], writes=[sm_r])
            kb.op("dve", lambda e: e.tensor_tensor(out=d2_, in0=ai[:, :, 256], in1=ai[:, :, 256], op=ALU.mult), reads=[ai_r], writes=[sm_r])
            kb.op("dve", lambda e: e.tensor_tensor(out=d_, in0=floorC[:, cg, :], in1=floorC[:, cg, :], op=ALU.mult), reads=[sm_r, floorC_r], writes=[sm_r])
            kb.op("dve", lambda e: e.tensor_tensor(out=d2_, in0=d2_, in1=d_, op=ALU.max), reads=[sm_r], writes=[sm_r])
            kb.op("dve", lambda e: e.scalar_tensor_tensor(out=vv_, in0=d2_, scalar=EPS, in1=mv[:, :, 1], op0=ALU.mult, op1=ALU.add), reads=[sm_r], writes=[sm_r])
            kb.op("act", lambda e: e.activation(out=vv_, in_=vv_, func=AF.Sqrt), reads=[sm_r], writes=[sm_r])

        def norm_b(cc):
            ai, ai_r = asb[cc % 3], asb_r[cc % 3]
            sm, sm_r = sm1s[cc % 3][0], sm1s[cc % 3][1]
            hb, hb_r = hbufs[cc % 2]
            vv_, rs_, nm_ = sm[:, 8:12], sm[:, 12:16], sm[:, 16:20]
            mv = sm[:, 24:32].rearrange("p (h x) -> p h x", x=2)
            kb.op("dve", lambda e: e.reciprocal(out=rs_, in_=vv_), reads=[sm_r], writes=[sm_r])
            kb.op("dve", lambda e: e.scalar_tensor_tensor(out=nm_, in0=mv[:, :, 0], scalar=-1.0, in1=rs_, op0=ALU.mult, op1=ALU.mult), reads=[sm_r], writes=[sm_r])
            for h in range(4):
                kb.op("act", lambda e, h=h: e.activation(out=hb[:, h * 256:(h + 1) * 256], in_=ai[:, h, 0:256],
                                                         func=AF.Identity, scale=rs_[:, h:h + 1], bias=nm_[:, h:h + 1]),
                      reads=[ai_r, sm_r], writes=[hb_r])

        def norm_c(cc):
            tt, half, p0, tok0, cg, tg = chunk_ids(cc)
            hb, hb_r = hbufs[cc % 2]
            for fc in range(8):
                kb.op("pe", lambda e, fc=fc: e.transpose(out=pbh[:, fc * 64:(fc + 1) * 64], in_=hb[:, fc * 128:(fc + 1) * 128], identity=identb[0:64, 0:64]),
                      reads=[hb_r, identb_r], writes=[pbh_r])
            kb.op("dve", lambda e: e.tensor_tensor(out=hgT[:, :, tok0:tok0 + 64], in0=pbh[:, 0:512].rearrange("p (c t) -> p c t", t=64),
                                                   in1=sog[:, :, tok0:tok0 + 64], op=ALU.mult),
                  reads=[pbh_r, sog_r], writes=[hgT_r])

        ln_finalize_a()
        NCH = T // 64
        for it in range(NCH + 3):
            if it < NCH:
                mlstm_crit(it)
            if 0 <= it - 1 < NCH:
                norm_a(it - 1)
            if 0 <= it - 2 < NCH:
                norm_b(it - 2)
            if 0 <= it - 3 < NCH:
                norm_c(it - 3)
        ln_finalize_b()
        for gi in range(2):
            wt, wt_r = ring_use()
            def ev(ocl, ps, ps_r, gi=gi):
                kb.op("act", lambda e: e.activation(out=sga[:, gi * 4 + ocl, :], in_=ps, func=AF.Sigmoid), reads=[ps_r], writes=[sga_r])
            proj_fm(wt, wt_r, xt, xt_r, T, ev)
        for gi in range(2):
            wt, wt_r = ring_use()
            def ev(ocl, ps, ps_r, gi=gi):
                oc = gi * 4 + ocl
                kb.op("dve", lambda e: e.tensor_tensor(out=mrg[:, oc, :], in0=ps, in1=sga[:, oc, :], op=ALU.mult), reads=[ps_r, sga_r], writes=[mrg_r])
            proj_fm(wt, wt_r, aT, aT_r, T, ev)

        for gi in range(2):
            wt, wt_r = ring_use()
            def ev(ocl, ps, ps_r, gi=gi):
                oc = gi * 4 + ocl
                i2 = oc % 2
                kb.op("dve", lambda e: e.tensor_tensor(out=t2[i2][0], in0=ps, in1=sgb[:, oc, :], op=ALU.mult), reads=[ps_r, sgb_r], writes=[t2[i2][1]])
                kb.op("pool", lambda e: e.tensor_tensor(out=mrgb[:, oc, :], in0=mrg[:, oc, :], in1=t2[i2][0], op=ALU.add),
                      reads=[mrg_r, t2[i2][1]], writes=[mrgb_r])
            proj_fm(wt, wt_r, hgT, hgT_r, T, ev)

        def layernorm_tok(tt, gi, bi_):
            xs = xtok[:, tt, :]
            ls = lsm[:, tt * 8:tt * 8 + 8]
            for j in range(2):
                kb.op("dve", lambda e, j=j: e.bn_stats(out=lst[:, tt, j, :], in_=xs[:, j * 512:(j + 1) * 512]), reads=[xtok_rt[tt]], writes=[lst_rt[tt]])
            kb.op("dve", lambda e: e.bn_aggr(out=ls[:, 0:2], in_=lst[:, tt, :, :].rearrange("p a b -> p (a b)")), reads=[lst_rt[tt]], writes=[lsm_rt[tt]])
            kb.op("dve", lambda e: e.tensor_scalar(out=ls[:, 2:3], in0=ls[:, 1:2], scalar1=EPS, scalar2=None, op0=ALU.add), reads=[lsm_rt[tt]], writes=[lsm_rt[tt]])
            kb.op("act", lambda e: e.activation(out=ls[:, 2:3], in_=ls[:, 2:3], func=AF.Sqrt), reads=[lsm_rt[tt]], writes=[lsm_rt[tt]])
            kb.op("dve", lambda e: e.reciprocal(out=ls[:, 3:4], in_=ls[:, 2:3]), reads=[lsm_rt[tt]], writes=[lsm_rt[tt]])
            kb.op("dve", lambda e: e.scalar_tensor_tensor(out=ls[:, 4:5], in0=ls[:, 0:1], scalar=-1.0, in1=ls[:, 3:4], op0=ALU.mult, op1=ALU.mult),
                  reads=[lsm_rt[tt]], writes=[lsm_rt[tt]])
            kb.op("act", lambda e: e.activation(out=xs, in_=xs, func=AF.Identity, scale=ls[:, 3:4], bias=ls[:, 4:5]),
                  reads=[xtok_rt[tt], lsm_rt[tt]], writes=[xtok_rt[tt]])
            kb.op("dve", lambda e: e.tensor_tensor(out=xs, in0=xs, in1=lnbc[:, gi, :], op=ALU.mult), reads=[xtok_rt[tt], lnbc_r], writes=[xtok_rt[tt]])
            kb.op("dve", lambda e: e.tensor_tensor(out=xs, in0=xs, in1=lnbc[:, bi_, :], op=ALU.add), reads=[xtok_rt[tt], lnbc_r], writes=[xtok_rt[tt]])

        for gi in range(2):
            wt, wt_r = ring_use()
            def evt_(tt, ps, ps_r, gi=gi):
                kb.op("dve", lambda e: e.scalar_tensor_tensor(out=xtok[:, tt, gi * 512:(gi + 1) * 512], in0=xtok[:, tt, gi * 512:(gi + 1) * 512],
                                                              scalar=ALPHA, in1=ps, op0=ALU.mult, op1=ALU.add),
                      reads=[ps_r, xtok_rt[tt]], writes=[xtok_rt[tt]])
            proj_tm(wt, wt_r, mrgb, mrgb_r, T // 128, evt_)
        for tt in range(T // 128):
            layernorm_tok(tt, 0, 1)
            for hb in range(2):
                bnk = 4 + hb
                for f4 in range(4):
                    fc = hb * 4 + f4
                    kb.op("pe", lambda e, fc=fc, f4=f4, bnk=bnk: e.transpose(out=pb[bnk][:, f4 * 128:(f4 + 1) * 128], in_=xtok[:, tt, fc * 128:(fc + 1) * 128],
                                                                             identity=ident),
                          reads=[xtok_rt[tt], ident_r], writes=[pbr[bnk]])
                kb.op("act", lambda e, hb=hb, bnk=bnk: e.activation(out=x1T[:, hb * 4:hb * 4 + 4, tt * 128:(tt + 1) * 128],
                                                                    in_=pb[bnk][:, 0:512].rearrange("p (c t) -> p c t", t=128), func=AF.Identity),
                      reads=[pbr[bnk]], writes=[x1T_r])
        for hh in range(2):
            for gi in range(4):
                wt, wt_r = ring_use()
                def ev(ocl, ps, ps_r, gi=gi):
                    oc = gi * 4 + ocl
                    i2 = oc % 2
                    kb.op("act", lambda e: e.activation(out=rl[i2][0], in_=ps, func=AF.Relu), reads=[ps_r], writes=[rl[i2][1]])
                    kb.op("pool", lambda e: e.tensor_tensor(out=hff[:, oc, :], in0=rl[i2][0], in1=rl[i2][0], op=ALU.mult), reads=[rl[i2][1]], writes=[hff_r])
                proj_fm(wt, wt_r, x1T, x1T_r, T, ev, banks=[4, 5, 6])
            for gi in range(4):
                wt, wt_r = ring_use()
                w4 = wt.rearrange("p a n -> p (a n)").rearrange("p (k n) -> p k n", k=4)
                for tt in range(T // 128):
                    for hc in range(2):
                        b = tt * 2 + hc
                        for k4 in range(4):
                            kl = gi * 4 + k4
                            kk = hh * 16 + kl
                            kb.op("pe", lambda e, k4=k4, kl=kl, kk=kk, b=b, tt=tt, hc=hc, w4=w4: e.matmul(
                                pb[b][:, 0:512], lhsT=hff[:, kl, tt * 128:(tt + 1) * 128], rhs=w4[:, k4, hc * 512:(hc + 1) * 512],
                                start=(kk == 0), stop=(kk == 31)),
                                reads=[hff_r, wt_r], writes=[pbr[b]])
        for tt in range(T // 128):
            for hc in range(2):
                b = tt * 2 + hc
                kb.op("dve", lambda e, b=b, tt=tt, hc=hc: e.scalar_tensor_tensor(
                    out=xtok[:, tt, hc * 512:(hc + 1) * 512], in0=xtok[:, tt, hc * 512:(hc + 1) * 512], scalar=ALPHA, in1=pb[b][:, 0:512],
                    op0=ALU.mult, op1=ALU.add), reads=[pbr[b], xtok_rt[tt]], writes=[xtok_rt[tt]])
        for tt in range(T // 128):
            layernorm_tok(tt, 2, 3)
        kb.dma("sp", s_y, y_d[tok_lo:tok_lo + T, :].rearrange("(t p) f -> p t f", p=128), xtok, reads=xtok_rt, writes=[R_out])

    nc.sync.wait_ge(s_out.sem, s_out.count)
    nc.sync.wait_ge(s_y.sem, s_y.count)
    return nc


_CACHE = {}


def kernel(**inp):
    f = lambda k: np.ascontiguousarray(np.asarray(inp[k], dtype=np.float32))
    xp, xs = f("x_prompt"), f("x_sample")
    cc_, sC_, sn_, sm_ = f("cache_conv")[0], f("state_C")[0], f("state_n")[0], f("state_m")[0]
    w_in, bg, w_dw, b_dw = f("w_in")[0], f("b_gate")[0], f("w_dw")[0], f("b_dw")[0]
    pv = np.zeros((128, NPV), np.float32)
    for i, v in enumerate((b_dw, f("ln_a_g")[0], f("ln_a_b")[0], f("hn_g")[0])):
        pv[:, 8 * i:8 * i + 8] = v.reshape(8, 128).T
    pv[:, 32:] = w_dw.reshape(31, 8, 128).transpose(2, 1, 0).reshape(128, 248)
    gbias = np.stack([bg[0:4], bg[4:8]], axis=1).astype(np.float32)
    lnrows = np.stack([f("ln1_g")[0], f("ln1_b")[0], f("ln2_g")[0], f("ln2_b")[0]]).astype(np.float32)
    ident = np.eye(128, dtype=np.float32)
    cm = (np.arange(64)[:, None] <= np.arange(64)[None, :]).astype(np.float32)
    cmask2 = np.concatenate([cm, cm], axis=0)
    resetm = np.ones((4, 512), np.float32)
    resetm[:, ::64] = 0
    oh = np.zeros((4, 4, 8), np.float32)
    for h in range(4):
        oh[h, h, :] = 1
    shared = {"w_in": w_in, "w_a": f("w_a_out")[0], "w_b": f("w_b_out")[0], "w_o": f("w_out")[0],
              "w_f1": f("w_ff1")[0], "w_f2": f("w_ff2")[0], "pv": pv, "gbias": gbias, "lnrows": lnrows,
              "ident": ident, "cmask2": cmask2, "resetm": resetm, "oh": oh}
    in_maps = []
    for c in range(8):
        b, q = c // 4, c % 4
        npad = NPRE - NMAIN * q
        xT_seq = np.zeros((1024, NSEQ), np.float32)
        xT_seq[:, npad:] = xp[b, :NMAIN * (q + 1)].T
        valid = np.zeros((4, NSEQ), np.float32)
        valid[:, npad:] = 1.0
        nbig = np.where(valid > 0, 0.0, -30000.0).astype(np.float32)
        xsm = xs[4 * c:4 * c + 4].reshape(NSMP, 1024)
        m = dict(shared)
        m.update({"xT_seq": xT_seq, "xT_smp": np.ascontiguousarray(xsm.T),
                  "x_tok": np.ascontiguousarray(np.concatenate([xp[b, NMAIN * q:NMAIN * (q + 1)], xsm], axis=0)),
                  "valid": valid, "nbig": nbig, "cache_c": np.ascontiguousarray(cc_[4 * c:4 * c + 4]),
                  "sC": np.ascontiguousarray(sC_[4 * c:4 * c + 4]), "sn": np.ascontiguousarray(sn_[4 * c:4 * c + 4]),
                  "sm": np.ascontiguousarray(sm_[4 * c:4 * c + 4].T)})
        in_maps.append(m)
    if inp.get("_only_maps"):
        return in_maps
    if "nc" not in _CACHE:
        _CACHE["nc"] = build()
    res = run_bass_kernel_spmd(_CACHE["nc"], in_maps, core_ids=list(range(8)))
    R = res.results
    yp = np.zeros((2, 8192, 1024), np.float32)
    ys = np.zeros((32, 64, 1024), np.float32)
    conv_p = np.zeros((1, 2, 30, 1024), np.float32)
    C_p = np.zeros((1, 2, 4, 256, 256), np.float32)
    n_p = np.zeros((1, 2, 4, 256), np.float32)
    m_p = np.zeros((1, 2, 4), np.float32)
    conv_s = np.zeros((1, 32, 30, 1024), np.float32)
    C_s = np.zeros((1, 32, 4, 256, 256), np.float32)
    n_s = np.zeros((1, 32, 4, 256), np.float32)
    m_s = np.zeros((1, 32, 4), np.float32)
    for c in range(8):
        b, q = c // 4, c % 4
        r = R[c]
        yp[b, NMAIN * q:NMAIN * (q + 1)] = r["y"][:NMAIN]
        ys[4 * c:4 * c + 4] = r["y"][NMAIN:].reshape(4, 64, 1024)
        if q == 3:
            conv_p[0, b] = r["conv_p"]
            cn = r["Cn_p"].reshape(4, 256, 257)
            C_p[0, b] = cn[:, :, :256]
            n_p[0, b] = cn[:, :, 256]
            m_p[0, b] = r["m_p"][:, 0]
        conv_s[0, 4 * c:4 * c + 4] = r["conv_s"]
        cn = r["Cn_s"].reshape(4, 4, 256, 257)
        C_s[0, 4 * c:4 * c + 4] = cn[:, :, :, :256]
        n_s[0, 4 * c:4 * c + 4] = cn[:, :, :, 256]
        m_s[0, 4 * c:4 * c + 4] = r["m_s"].T
    return (yp, ys, conv_p, C_p, n_p, m_p, conv_s, C_s, n_s, m_s)
```

```python
import numpy as np
import concourse.bass as bass
import concourse.mybir as mybir
from concourse.bass_utils import run_bass_kernel_spmd

F32, BF16 = mybir.dt.float32, mybir.dt.bfloat16
AF = mybir.ActivationFunctionType
ALU = mybir.AluOpType
AX = mybir.AxisListType

NSEQ, NPRE, NMAIN, NSMP = 8192, 6144, 2048, 256
T = 256
ALPHA = 2.0 ** 0.25
EPS = 1e-5
NPV = 32 + 8 * 31
NW = 4
G_CONV, G_Q, G_K, G_V, G_O, G_GA, G_GB = 0, 4, 6, 8, 10, 12, 14
G_A, G_B, G_OUT, G_F1, G_F2 = 16, 18, 20, 22, 30
NG = 38
WIN_COL = [0, 512, 1024, 1536, 2048, 2560, 3072, 3584, 4096, 4608, 5120, 5632, 6152, 6664, 7176, 7688]


class Res:
    __slots__ = ("name", "w", "r", "excl", "dj")

    def __init__(self, name, excl=False):
        self.name = name
        self.w = None
        self.r = {}
        self.excl = excl
        self.dj = False


class Stream:
    def __init__(self, nc, name):
        self.name = name
        self.sem = nc.alloc_semaphore("ds_" + name)
        self.count = 0


class KB:
    def __init__(self, nc):
        self.nc = nc
        self.eng = {"pe": nc.tensor, "act": nc.scalar, "dve": nc.vector, "pool": nc.gpsimd, "sp": nc.sync}
        self.sem = {e: nc.alloc_semaphore("sem_" + e) for e in ("pe", "act", "dve", "pool")}
        self.seq = {e: 0 for e in self.sem}
        self.waited = {}
        self.nbuf = 0

    def sb(self, shape, dt=F32, name=None):
        self.nbuf += 1
        nm = "s_" + (name or ("b%d" % self.nbuf))
        t = self.nc.alloc_sbuf_tensor(nm, list(shape), dt).ap()
        return t, Res(nm)

    def _wait(self, eng, sem, val):
        key = (eng, sem.num)
        if self.waited.get(key, 0) >= val:
            return
        self.waited[key] = val
        self.eng[eng].wait_ge(sem, val)

    def _deps(self, eng, reads, writes):
        toks = []
        for r in reads:
            if r.w is not None:
                toks.append((r.w, "raw", False))
            if r.excl:
                for t in r.r.values():
                    toks.append((t, "war", False))
        for w in writes:
            if w.w is not None:
                toks.append((w.w, "waw", w.dj))
            for t in w.r.values():
                toks.append((t, "war", w.dj))
        for (tok, kind, dj) in toks:
            sem, val, teng = tok
            if teng == eng:
                if eng == "pe":
                    continue
                if dj and kind != "raw":
                    continue
            self._wait(eng, sem, val)

    def op(self, eng, fn, reads=(), writes=()):
        self._deps(eng, reads, writes)
        ins = fn(self.eng[eng])
        self.seq[eng] += 1
        ins.then_inc(self.sem[eng], 1)
        tok = (self.sem[eng], self.seq[eng], eng)
        for r in reads:
            r.r[eng] = tok
        for w in writes:
            w.w = tok
            w.r = {}
        return tok

    def dma(self, q, stream, out, in_, reads=(), writes=(), **kw):
        self._deps(q, reads, writes)
        ins = self.eng[q].dma_start(out=out, in_=in_, **kw)
        stream.count += 16
        ins.then_inc(stream.sem, 16)
        tok = (stream.sem, stream.count, "dma:" + stream.name)
        for r in reads:
            r.r["dma:" + stream.name] = tok
        for w in writes:
            w.w = tok
            w.r = {}
        return tok


def build():
    nc = bass.Bass("TRN2", target_bir_lowering=False)
    kb = KB(nc)

    def din(name, shape):
        return nc.dram_tensor(name, list(shape), F32, kind="ExternalInput").ap()

    def dout(name, shape):
        return nc.dram_tensor(name, list(shape), F32, kind="ExternalOutput").ap()

    xT_seq = din("xT_seq", [1024, NSEQ])
    xT_smp = din("xT_smp", [1024, NSMP])
    x_tok = din("x_tok", [NMAIN + NSMP, 1024])
    valid_d = din("valid", [4, NSEQ])
    nbig_d = din("nbig", [4, NSEQ])
    cache_c = din("cache_c", [4, 30, 1024])
    sC = din("sC", [4, 4, 256, 256])
    sn = din("sn", [4, 4, 256])
    sm = din("sm", [4, 4])
    w_in = din("w_in", [1024, 8200])
    w_a = din("w_a", [1024, 1024])
    w_b = din("w_b", [1024, 1024])
    w_o = din("w_o", [1024, 1024])
    w_f1 = din("w_f1", [1024, 4096])
    w_f2 = din("w_f2", [4096, 1024])
    pv_d = din("pv", [128, NPV])
    gb_d = din("gbias", [4, 2])
    lnrows = din("lnrows", [4, 1024])
    ident_d = din("ident", [128, 128])
    cmask_d = din("cmask2", [128, 64])
    reset_d = din("resetm", [4, 512])
    oh_d = din("oh", [4, 4, 8])
    wsc = nc.dram_tensor("wsc", [NG, 128, 4096], BF16, kind="Internal").ap()
    y_d = dout("y", [NMAIN + NSMP, 1024])
    conv_p = dout("conv_p", [30, 1024])
    Cn_p = dout("Cn_p", [4, 2, 128, 257])
    m_p = dout("m_p", [4, 1])
    conv_s = dout("conv_s", [4, 30, 1024])
    Cn_s = dout("Cn_s", [4, 4, 2, 128, 257])
    m_s = dout("m_s", [4, 4])

    R_in = Res("dram_in")
    R_wsc = [Res("wsc%d" % g) for g in range(NG)]
    R_out = Res("dram_out")
    s_const = Stream(nc, "const")
    s_conv = [Stream(nc, "wcv%d" % g) for g in range(NG)]
    s_out = Stream(nc, "out")

    pb, pbr = [], []
    for i in range(7):
        pb.append(nc.alloc_psum_tensor("pb%d" % i, [128, 512], F32).ap())
        pbr.append(Res("pb%d" % i, True))
    pbh = nc.alloc_psum_tensor("pbh", [128, 1024], BF16).ap()
    pbh_r = Res("pbh", True)

    ident, ident_r = kb.sb([128, 128], F32, "ident")
    identb, identb_r = kb.sb([128, 128], BF16, "identb")
    onesb, onesb_r = kb.sb([128, 128], BF16, "onesb")
    ones4, ones4_r = kb.sb([4, 128], F32, "ones4")
    cmask, cmask_r = kb.sb([128, 64], F32, "cmask")
    resetm, resetm_r = kb.sb([4, 512], F32, "resetm")
    oh, oh_r = kb.sb([4, 4, 8], F32, "oh")
    pv, pv_r = kb.sb([128, NPV], F32, "pv")
    gbt, gbt_r = kb.sb([4, 2], F32, "gbt")
    nbf, nbf_r = kb.sb([4, 1], F32, "nbf")
    lnbc, lnbc_r = kb.sb([128, 4, 1024], F32, "lnbc")
    for (dst, src, rr) in ((ident, ident_d, ident_r), (cmask, cmask_d, cmask_r), (resetm, reset_d, resetm_r),
                           (oh, oh_d, oh_r), (pv, pv_d, pv_r), (gbt, gb_d, gbt_r)):
        kb.dma("sp", s_const, dst, src, reads=[R_in], writes=[rr])
    for i in range(4):
        kb.dma("sp", s_const, lnbc[:, i, :], lnrows[i:i + 1, :].partition_broadcast(128), reads=[R_in], writes=[lnbc_r])
    tok_all = (s_const.sem, s_const.count, "dma:const")
    for rr in (ident_r, cmask_r, resetm_r, oh_r, pv_r, gbt_r, lnbc_r):
        rr.w = tok_all
    kb.op("dve", lambda e: e.tensor_copy(out=identb, in_=ident), reads=[ident_r], writes=[identb_r])
    kb.op("dve", lambda e: e.memset(onesb, 1.0 / 1024.0), writes=[onesb_r])
    kb.op("dve", lambda e: e.memset(ones4, 1.0), writes=[ones4_r])
    kb.op("dve", lambda e: e.tensor_scalar(out=nbf, in0=gbt[:, 1:2], scalar1=-1.0, scalar2=None, op0=ALU.mult),
          reads=[gbt_r], writes=[nbf_r])

    tokE, tokE_r = kb.sb([128, 66, 2, 4], F32, "tokE")
    floorC, floorC_r = kb.sb([64, 132, 4], F32, "floorC")
    decb, decb_r = kb.sb([128, 17, 4, 8], F32, "decb")
    for r_ in (tokE_r, floorC_r, decb_r):
        r_.dj = True
    m_all, m_all_r = kb.sb([4, 129], F32, "m_all")
    msin, msin_r = kb.sb([4, 4], F32, "msin")
    mend_s, mend_s_r = kb.sb([4, 4], F32, "mend_s")
    Cst, Cst_r, Cb, Cb_r, Cst_rd = [], [], [], [], []
    for h in range(4):
        a, r = kb.sb([128, 2, 257], F32, "Cst%d" % h)
        Cst.append(a); Cst_r.append(r); Cst_rd.append([Res("Cst%d_0" % h), Res("Cst%d_1" % h)])
        a, r = kb.sb([128, 2, 257], BF16, "Cb%d" % h)
        Cb.append(a); Cb_r.append(r)
        kb.op("dve", lambda e, a=Cst[h]: e.memset(a, 0.0), writes=Cst_rd[h])
    kb.op("dve", lambda e: e.memset(m_all, 0.0), writes=[m_all_r])
    s_msin = Stream(nc, "msin")
    kb.dma("sp", s_msin, msin, sm, reads=[R_in], writes=[msin_r])
    xTh, xTh_r = kb.sb([128, 8, 128], BF16, "xTh")

    with nc.sbuf_tensor("p_wkv", [128, 8, 2048], BF16) as wkv_t, \
            nc.sbuf_tensor("p_wg", [128, 8, 8], BF16) as wg_t, \
            nc.sbuf_tensor("xTb0", [128, 8, 512], BF16) as xTb0_t, \
            nc.sbuf_tensor("xTb1", [128, 8, 512], BF16) as xTb1_t, \
            nc.sbuf_tensor("xTb2", [128, 8, 512], BF16) as xTb2_t, \
            nc.sbuf_tensor("xTb3", [128, 8, 512], BF16) as xTb3_t, \
            nc.sbuf_tensor("gt", [4, 12, 512], F32) as gt_t, \
            nc.sbuf_tensor("E3", [68, 512], F32) as E3_t, \
            nc.sbuf_tensor("gsm", [4, 8, 8], F32) as gsm_t, \
            nc.sbuf_tensor("drhs", [4, 4, 8], F32) as drhs_t, \
            nc.sbuf_tensor("ktok", [128, 2, 1024], BF16) as ktok_t, \
            nc.sbuf_tensor("wvp", [128, 2, 4, 257], BF16) as wvp_t:
        wkv, wg = wkv_t.ap(), wg_t.ap()
        xTb = [xTb0_t.ap(), xTb1_t.ap(), xTb2_t.ap(), xTb3_t.ap()]
        gt, E3, gsm, drhs, ktokp, wvp = gt_t.ap(), E3_t.ap(), gsm_t.ap(), drhs_t.ap(), ktok_t.ap(), wvp_t.ap()
        wkv_r, wg_r = Res("wkv"), Res("wg")
        xTb_r = [Res("xTb%d" % i) for i in range(4)]
        gt_r = [Res("gt%d" % i) for i in range(12)]
        E3_r, drhs_r = Res("E3"), Res("drhs")
        gsm_r = [Res("gsm%d" % i) for i in range(8)]
        ktokp_r = [Res("ktokp0"), Res("ktokp1")]
        wvp_r = [Res("wvp0"), Res("wvp1")]
        s_pw = Stream(nc, "pw")
        s_x = [Stream(nc, "xTb%d" % i) for i in range(4)]
        s_vv = [Stream(nc, "vblk0"), Stream(nc, "vblk1")]
        s_vn = [Stream(nc, "nblk0"), Stream(nc, "nblk1")]

        w_in_k = w_in.rearrange("(k p) n -> p k n", p=128)
        kb.dma("pool", s_pw, wg, w_in_k[:, :, 6144:6152], reads=[R_in], writes=[wg_r])
        kb.dma("pool", s_pw, wkv[:, :, 0:1024], w_in_k[:, :, 3072:4096], reads=[R_in], writes=[wkv_r])
        kb.dma("pool", s_pw, wkv[:, :, 1024:2048], w_in_k[:, :, 4096:5120], reads=[R_in], writes=[wkv_r])
        tok_pw = (s_pw.sem, s_pw.count, "dma:pw")
        wg_r.w = tok_pw
        wkv_r.w = tok_pw
        kb.op("dve", lambda e: e.memset(E3, 0.0), writes=[E3_r])
        for r_ in ktokp_r + wvp_r:
            r_.dj = True

        xTs = xT_seq.rearrange("(k p) t -> p k t", p=128)
        xTm = xT_smp.rearrange("(k p) t -> p k t", p=128)

        xsrcs = [(xTs[:, :, b_ * 512:(b_ + 1) * 512], 512) for b_ in range(16)] + [(xTm[:, :, 0:NSMP], NSMP)]
        xissued = [0]

        def xget(i):
            while xissued[0] < len(xsrcs) and xissued[0] <= i + 1:
                j = xissued[0]
                src_, n_ = xsrcs[j]
                kb.dma("pool", s_x[j % 4], xTb[j % 4][:, :, 0:n_], src_, reads=[R_in], writes=[xTb_r[j % 4]])
                xissued[0] += 1
            return xTb[i % 4], xTb_r[i % 4]

        def conv_dma(g, src):
            kb.dma("pool", s_conv[g], wsc[g].rearrange("p (k n) -> p k n", k=src.shape[1]), src, reads=[R_in], writes=[R_wsc[g]])

        conv_jobs = []
        for gi, c0 in enumerate(WIN_COL):
            conv_jobs.append((gi, w_in_k[:, :, c0:c0 + 512]))
        for (g0, wd) in ((G_A, w_a), (G_B, w_b), (G_OUT, w_o)):
            wk_ = wd.rearrange("(k p) n -> p k n", p=128)
            for i in range(2):
                conv_jobs.append((g0 + i, wk_[:, :, i * 512:(i + 1) * 512]))
        wk_ = w_f1.rearrange("(k p) n -> p k n", p=128)
        for i in range(8):
            conv_jobs.append((G_F1 + i, wk_[:, :, i * 512:(i + 1) * 512]))
        wk_ = w_f2.rearrange("(k p) n -> p k n", p=128)
        for i in range(8):
            conv_jobs.append((G_F2 + i, wk_[:, 4 * i:4 * i + 4, :]))

        use_order = [2, 3, 0, 1, G_Q, G_Q + 1, G_K, G_K + 1, G_V, G_V + 1, G_O, G_O + 1, G_GB, G_GB + 1,
                     G_GA, G_GA + 1, G_A, G_A + 1, G_B, G_B + 1, G_OUT, G_OUT + 1] + [G_F1 + i for i in range(4)] + \
                    [G_F2 + i for i in range(4)] + [G_F1 + 4 + i for i in range(4)] + [G_F2 + 4 + i for i in range(4)]
        conv_jobs.sort(key=lambda j_: use_order.index(j_[0]))
        def gate_block(blk, n, chunk0, tile0, sample, grouped):
            nch = n // 64
            xt, xt_r = xget(blk)
            vs, ns_ = (5, 6) if blk % 2 == 0 else (7, 8)
            for _ in range(1):
                if conv_jobs:
                    conv_dma(*conv_jobs.pop(0))
            def load_mask(b_):
                v_, n_2 = (5, 6) if b_ % 2 == 0 else (7, 8)
                kb.dma("sp", s_vv[b_ % 2], gt[:, v_, 0:512], valid_d[:, b_ * 512:b_ * 512 + 512], reads=[R_in], writes=[gt_r[v_]])
                kb.dma("sp", s_vn[b_ % 2], gt[:, n_2, 0:512], nbig_d[:, b_ * 512:b_ * 512 + 512], reads=[R_in], writes=[gt_r[n_2]])
            if blk == 0:
                load_mask(0)
            if blk + 1 < 16:
                load_mask(blk + 1)
            zi, zf = pb[4], pb[5]
            for k in range(8):
                kb.op("pe", lambda e, k=k: e.matmul(zi[0:4, 0:n], lhsT=wg[:, k, 0:4], rhs=xt[:, k, 0:n], start=(k == 0), stop=(k == 7)),
                      reads=[wg_r, xt_r], writes=[pbr[4]])
            for k in range(8):
                kb.op("pe", lambda e, k=k: e.matmul(zf[0:4, 0:n], lhsT=wg[:, k, 4:8], rhs=xt[:, k, 0:n], start=(k == 0), stop=(k == 7)),
                      reads=[wg_r, xt_r], writes=[pbr[5]])
            s1_, s2_ = (1, 2) if blk % 2 == 0 else (9, 10)
            te, nlfm, igm, bneg, g = gt[:, 0, 0:n], gt[:, s1_, 0:n], gt[:, s2_, 0:n], gt[:, 3, 0:n], gt[:, 4, 0:n]
            kb.op("act", lambda e: e.activation(out=te, in_=zf[0:4, 0:n], func=AF.Exp, scale=-1.0, bias=nbf[:, 0:1]),
                  reads=[pbr[5], nbf_r], writes=[gt_r[0]])
            kb.op("act", lambda e: e.activation(out=te, in_=te, func=AF.Ln, bias=1.0), reads=[gt_r[0]], writes=[gt_r[0]])
            if not sample:
                kb.op("dve", lambda e: e.tensor_tensor(out=nlfm, in0=te, in1=gt[:, vs, 0:n], op=ALU.mult),
                      reads=[gt_r[0], gt_r[vs]], writes=[gt_r[s1_]])
                kb.op("dve", lambda e: e.scalar_tensor_tensor(out=igm, in0=zi[0:4, 0:n], scalar=gbt[:, 0:1], in1=gt[:, vs, 0:n],
                                                              op0=ALU.add, op1=ALU.mult),
                      reads=[pbr[4], gbt_r, gt_r[vs]], writes=[gt_r[s2_]])
                kb.op("dve", lambda e: e.tensor_tensor(out=igm, in0=igm, in1=gt[:, ns_, 0:n], op=ALU.add),
                      reads=[gt_r[s2_], gt_r[ns_]], writes=[gt_r[s2_]])
            else:
                kb.op("dve", lambda e: e.tensor_copy(out=nlfm, in_=te), reads=[gt_r[0]], writes=[gt_r[s1_]])
                kb.op("dve", lambda e: e.tensor_scalar(out=igm, in0=zi[0:4, 0:n], scalar1=gbt[:, 0:1], scalar2=None, op0=ALU.add),
                      reads=[pbr[4], gbt_r], writes=[gt_r[s2_]])
            yield
            kb.op("dve", lambda e: e.tensor_tensor_scan(out=bneg, data0=resetm[:, 0:n], data1=nlfm, initial=0.0,
                                                        op0=ALU.mult, op1=ALU.add),
                  reads=[resetm_r, gt_r[s1_]], writes=[gt_r[3]])
            yield
            kb.op("dve", lambda e: e.tensor_tensor(out=g, in0=igm, in1=bneg, op=ALU.add),
                  reads=[gt_r[s2_], gt_r[3]], writes=[gt_r[4]])
            yield
            g3 = g.rearrange("p (c l) -> p c l", l=64)
            b3 = bneg.rearrange("p (c l) -> p c l", l=64)
            gmax, Mc, nd, dec = gsm[:, 0, 0:nch], gsm[:, 1, 0:nch], gsm[:, 2, 0:nch], gsm[:, 3, 0:nch]
            nbtot = b3[:, :, 63]
            kb.op("dve", lambda e: e.tensor_reduce(out=gmax, in_=g3, axis=AX.X, op=ALU.max), reads=[gt_r[4]], writes=[gsm_r[0]])
            yield
            if not sample:
                kb.op("dve", lambda e: e.tensor_tensor_scan(out=m_all[:, chunk0 + 1:chunk0 + 1 + nch], data0=gmax, data1=nbtot,
                                                            initial=m_all[:, chunk0:chunk0 + 1], op0=ALU.max, op1=ALU.subtract),
                      reads=[gsm_r[0], gt_r[3], m_all_r], writes=[m_all_r])
                yield
                m0 = m_all[:, chunk0:chunk0 + nch]
                m0_r = m_all_r
            else:
                m0 = msin[:, 0:nch]
                m0_r = msin_r
            kb.op("dve", lambda e: e.tensor_tensor(out=Mc, in0=gmax, in1=m0, op=ALU.max), reads=[gsm_r[0], m0_r], writes=[gsm_r[1]])
            yield
            if sample:
                kb.op("dve", lambda e: e.tensor_tensor(out=mend_s[:, 0:nch], in0=Mc, in1=nbtot, op=ALU.subtract),
                      reads=[gsm_r[1], gt_r[3]], writes=[mend_s_r])
                yield
            kb.op("dve", lambda e: e.tensor_tensor(out=nd, in0=m0, in1=Mc, op=ALU.subtract), reads=[gsm_r[1], m0_r], writes=[gsm_r[2]])
            yield
            if grouped:
                Mc2, ndg = gsm[:, 4, 0:nch], gsm[:, 5, 0:nch // 2]
                Mcv = Mc.rearrange("p (j t) -> p j t", t=2)
                Mc2v = Mc2.rearrange("p (j t) -> p j t", t=2)
                ndv = nd.rearrange("p (j t) -> p j t", t=2)
                kb.op("dve", lambda e: e.tensor_copy(out=Mc2, in_=Mc), reads=[gsm_r[1]], writes=[gsm_r[4]])
                yield
                kb.op("dve", lambda e: e.tensor_tensor(out=Mc2v[:, :, 0], in0=Mcv[:, :, 0], in1=ndv[:, :, 1], op=ALU.subtract),
                      reads=[gsm_r[1], gsm_r[2], gsm_r[4]], writes=[gsm_r[4]])
                yield
                kb.op("dve", lambda e: e.tensor_tensor(out=ndg, in0=ndv[:, :, 0], in1=ndv[:, :, 1], op=ALU.add), reads=[gsm_r[2]], writes=[gsm_r[5]])
                yield
                Mw, Mw_r, ndx, ndx_r, ndec = Mc2, gsm_r[4], ndg, gsm_r[5], nch // 2
            else:
                Mw, Mw_r, ndx, ndx_r, ndec = Mc, gsm_r[1], nd, gsm_r[2], nch
            e3a = E3[0:4, 0:n].rearrange("p (c l) -> p c l", l=64)
            e3b = E3[32:36, 0:n].rearrange("p (c l) -> p c l", l=64)
            e3c = E3[64:68, 0:n].rearrange("p (c l) -> p c l", l=64)
            m0b = m0.unsqueeze(2).to_broadcast([4, nch, 64])
            Mcb = Mw.unsqueeze(2).to_broadcast([4, nch, 64])
            kb.op("dve", lambda e: e.tensor_tensor(out=e3a, in0=g3, in1=m0b, op=ALU.subtract), reads=[gt_r[4], m0_r], writes=[E3_r])
            yield
            kb.op("dve", lambda e: e.tensor_tensor(out=e3b, in0=g3, in1=Mcb, op=ALU.subtract), reads=[gt_r[4], Mw_r], writes=[E3_r])
            yield
            kb.op("dve", lambda e: e.tensor_tensor(out=e3c, in0=b3, in1=m0b, op=ALU.subtract), reads=[gt_r[3], m0_r], writes=[E3_r])
            yield
            for r0 in (0, 32, 64):
                kb.op("act", lambda e, r0=r0: e.activation(out=E3[r0:r0 + 4, 0:n], in_=E3[r0:r0 + 4, 0:n], func=AF.Exp), reads=[E3_r], writes=[E3_r])
                yield
            decx = gsm[:, 3, 0:ndec]
            kb.op("act", lambda e: e.activation(out=decx, in_=ndx, func=AF.Exp), reads=[ndx_r], writes=[gsm_r[3]])
            yield
            kb.op("dve", lambda e: e.tensor_tensor(out=drhs[:, :, 0:ndec], in0=oh[:, :, 0:ndec],
                                                   in1=decx.unsqueeze(1).to_broadcast([4, 4, ndec]), op=ALU.mult),
                  reads=[oh_r, gsm_r[3]], writes=[drhs_r])
            yield
            pd = pb[6][:, 0:4 * ndec].rearrange("p (h c) -> p h c", h=4)
            kb.op("pe", lambda e: e.matmul(pd, lhsT=ones4[0:4, :], rhs=drhs[:, :, 0:ndec], start=True, stop=True),
                  reads=[ones4_r, drhs_r], writes=[pbr[6]])
            yield
            kb.op("act", lambda e: e.activation(out=decb[:, blk, :, 0:ndec], in_=pd, func=AF.Identity), reads=[pbr[6]], writes=[decb_r])
            yield
            ntt = n // 128
            ptE = pb[6][:, 64:64 + 4 * 68].rearrange("p (t x) -> p t x", x=68)
            for tt in range(ntt):
                kb.op("pe", lambda e, tt=tt: e.transpose(out=ptE[:, tt, :], in_=E3[0:68, tt * 128:(tt + 1) * 128], identity=ident[0:68, 0:68]),
                      reads=[E3_r, ident_r], writes=[pbr[6]])
                yield
            src2 = pb[6][:, 64:64 + 4 * 68].rearrange("p (t x) -> p t x", x=68)[:, 0:ntt, 0:64].rearrange("p t (q x) -> p t q x", x=32)[:, :, :, 0:4]
            kb.op("act", lambda e: e.activation(out=tokE[:, tile0:tile0 + ntt, :, :], in_=src2, func=AF.Identity), reads=[pbr[6]], writes=[tokE_r])
            yield
            pf = pb[6][0:64, 400:400 + 4 * nch].rearrange("p (c x) -> p c x", x=4)
            for cc in range(nch):
                kb.op("pe", lambda e, cc=cc: e.transpose(out=pf[:, cc, :], in_=E3[64:68, cc * 64:(cc + 1) * 64], identity=ident[64:68, 64:68]),
                      reads=[E3_r, ident_r], writes=[pbr[6]])
                yield
            kb.op("dve", lambda e: e.tensor_copy(out=floorC[:, chunk0:chunk0 + nch, :], in_=pf), reads=[pbr[6]], writes=[floorC_r])
            yield

        ubank = [0]

        ubank = [0]

        def state_update_tile(ktok_ap, ktok_res, wv_ap, wv_res, tile, banks):
            out = []
            for h in range(4):
                for dkc in range(2):
                    def f(h=h, dkc=dkc):
                        bi = banks[ubank[0] % len(banks)]
                        ubank[0] += 1
                        kb.op("pe", lambda e: e.matmul(
                            pb[bi][:, 0:257], lhsT=ktok_ap[:, h * 256 + dkc * 128:h * 256 + dkc * 128 + 128],
                            rhs=wv_ap[:, h, 0:257], start=True, stop=True),
                            reads=[ktok_res, wv_res], writes=[pbr[bi]])
                        kb.op("dve", lambda e: e.scalar_tensor_tensor(
                            out=Cst[h][:, dkc, :], in0=Cst[h][:, dkc, :], scalar=decb[:, tile // 4, h, (tile % 4):(tile % 4) + 1],
                            in1=pb[bi][:, 0:257], op0=ALU.mult, op1=ALU.add),
                            reads=[Cst_rd[h][dkc], decb_r, pbr[bi]], writes=[Cst_rd[h][dkc]])
                    out.append(f)
            return out

        deferred = []

        gstep = [None]

        def pump(n=1):
            for _ in range(n):
                if deferred:
                    deferred.pop(0)()
            if gstep[0] is not None and n == 1:
                for _ in range(2):
                    try:
                        next(gstep[0])
                    except StopIteration:
                        gstep[0] = None
                        break

        evt = [0]

        def kv_tile(xt, xt_r, tsl, wk_ap, wk_r, wv_w_ap, wv_w_r, ktok_ap, ktok_res, wv_ap, wv_res, tileg, vext_ap=None, vext_res=None):
            for hf in range(2):
                bi = hf
                for k in range(8):
                    kb.op("pe", lambda e, k=k, hf=hf, bi=bi: e.matmul(pb[bi][:, 0:512], lhsT=xt[:, k, tsl], rhs=wk_ap(hf)[:, k, :],
                                                                     start=(k == 0), stop=(k == 7)),
                          reads=[xt_r, wk_r(hf)], writes=[pbr[bi]])
                    if k % 4 == 3:
                        pump()
                kb.op("act", lambda e, hf=hf, bi=bi: e.activation(out=ktok_ap[:, hf * 512:(hf + 1) * 512], in_=pb[bi][:, 0:512],
                                                                  func=AF.Identity, scale=0.0625),
                      reads=[pbr[bi]], writes=[ktok_res])
            for hf in range(2):
                bi = 2 + hf
                for k in range(8):
                    kb.op("pe", lambda e, k=k, hf=hf, bi=bi: e.matmul(pb[bi][:, 0:512], lhsT=xt[:, k, tsl], rhs=wv_w_ap(hf)[:, k, :],
                                                                     start=(k == 0), stop=(k == 7)),
                          reads=[xt_r, wv_w_r(hf)], writes=[pbr[bi]])
                    if k % 4 == 3:
                        pump()
                if vext_ap is not None:
                    kb.op("act", lambda e, hf=hf, bi=bi: e.activation(out=vext_ap[:, 2 * hf:2 * hf + 2, 0:256],
                                                                      in_=pb[bi][:, 0:512].rearrange("p (h v) -> p h v", h=2), func=AF.Identity),
                          reads=[pbr[bi]], writes=[vext_res])
                for h2 in range(2):
                    hh_ = 2 * hf + h2
                    kb.op("act", lambda e, hf=hf, bi=bi, h2=h2, hh_=hh_: e.activation(
                        out=wv_ap[:, hh_, 0:256], in_=pb[bi][:, h2 * 256:(h2 + 1) * 256], func=AF.Identity,
                        scale=tokE[:, tileg, 1, hh_:hh_ + 1]),
                        reads=[pbr[bi], tokE_r], writes=[wv_res])
            kb.op("pool", lambda e: e.tensor_copy(out=wv_ap[:, :, 256], in_=tokE[:, tileg, 1, :]), reads=[tokE_r], writes=[wv_res])

        def p2_block(tb):
            xt, xt_r = xTb[tb % 4], xTb_r[tb % 4]
            for _ in range(2):
                if conv_jobs:
                    conv_dma(*conv_jobs.pop(0))
            for t4 in range(4):
                tile = tb * 4 + t4
                i2 = tile % 2
                kv_tile(xt, xt_r, slice(t4 * 128, (t4 + 1) * 128),
                        lambda hf: wkv[:, :, hf * 512:(hf + 1) * 512], lambda hf: wkv_r,
                        lambda hf: wkv[:, :, 1024 + hf * 512:1024 + (hf + 1) * 512], lambda hf: wkv_r,
                        ktokp[:, i2, :], ktokp_r[i2], wvp[:, i2, :, :], wvp_r[i2], tile)
                deferred.extend(state_update_tile(ktokp[:, i2, :], ktokp_r[i2], wvp[:, i2, :, :], wvp_r[i2], tile, [4, 5]))

        gens = [gate_block(blk, 512, blk * 8, blk * 4, False, blk < NPRE // 512) for blk in range(16)]
        gens.append(gate_block(16, NSMP, 128, 64, True, False))
        next(gens[0])
        for bi_ in range(17):
            if bi_ + 1 < 17:
                next(gens[bi_ + 1])
            if 0 <= bi_ - 1 < NPRE // 512:
                gstep[0] = gens[bi_]
                p2_block(bi_ - 1)
                gstep[0] = None
            for _ in gens[bi_]:
                pass
        kb.dma("sp", s_out, m_p, m_all[:, 128:129], reads=[m_all_r], writes=[R_out])
        kb.dma("sp", s_out, m_s, mend_s, reads=[mend_s_r], writes=[R_out])

        pump(1000)
        while conv_jobs:
            conv_dma(*conv_jobs.pop(0))
        s_h = Stream(nc, "xTh")
        kb.dma("pool", s_h, xTh, xTs[:, :, NPRE - 128:NPRE], reads=[R_in], writes=[xTh_r])
        bar_res = [wkv_r, wg_r, E3_r, drhs_r] + xTb_r + gt_r + gsm_r + ktokp_r + wvp_r
        for eng in ("pe", "act", "dve", "pool", "sp"):
            kb._deps(eng, [], bar_res)
            kb._deps(eng, bar_res, [])

    ring, ring_r, s_ring = [], [], []
    for i in range(NW):
        a, r = kb.sb([128, 8, 512], BF16, "ring%d" % i)
        ring.append(a); ring_r.append(r); s_ring.append(Stream(nc, "ring%d" % i))
    xTm_b, xTm_r, s_xm = [], [], []
    for i in range(2):
        a, r = kb.sb([128, 8, T], BF16, "xTm%d" % i)
        xTm_b.append(a); xTm_r.append(r); s_xm.append(Stream(nc, "xTm%d" % i))
    xtok, xtok_r = kb.sb([128, 2, 1024], F32, "xtok")
    s_xt = Stream(nc, "xtok")
    glu, glu_r = kb.sb([128, 8, 376], BF16, "glu")
    glu32, glu32_r = kb.sb([128, 8, 4, 30], F32, "glu32")
    hist, hist_r = kb.sb([128, 8, 30], BF16, "hist")
    sigzh, sigzh_r = kb.sb([128, 8, 128], BF16, "sigzh")
    sigz, sigz_r = kb.sb([128, 8, T], BF16, "sigz")
    diags = [kb.sb([128, 31, 128], BF16, "diag%d" % i) for i in range(2)]
    dw, dw_r = kb.sb([128, 8, T], F32, "dw")
    dwb, sq, t1, t2 = [], [], [], []
    for i in range(2):
        dwb.append(kb.sb([128, T], BF16, "dwb%d" % i))
        sq.append(kb.sb([128, T], BF16, "sq%d" % i))
        t1.append(kb.sb([128, T], F32, "t1_%d" % i))
        t2.append(kb.sb([128, T], F32, "t2_%d" % i))
    mean_sb, mean_r = kb.sb([128, T], F32, "mean_sb")
    rstd, rstd_r = kb.sb([128, T], F32, "rstd")
    aT, aT_r = kb.sb([128, 8, T], BF16, "aT")
    sga, sga_r = kb.sb([128, 8, T], BF16, "sga")
    mrg, mrg_r = dw, dw_r
    mrgb, mrgb_r = kb.sb([128, 8, T], BF16, "mrgb")
    qT, qT_r = kb.sb([128, 8, T], BF16, "qT")
    kT, kT_r = kb.sb([128, 8, T], BF16, "kT")
    ktok, ktok_r = kb.sb([128, 2, 1024], BF16, "ktokm")
    vext, vext_r = kb.sb([128, 2, 4, 257], BF16, "vext")
    wvm, wvm_r = kb.sb([128, 2, 4, 257], BF16, "wvm")
    sog, sog_r = kb.sb([128, 8, T], BF16, "sog")
    sgb, sgb_r = sigz, sigz_r
    pT, pT_r0 = kb.sb([128, 4, 64], BF16, "pT")
    pT_rh = [Res("pT%d" % h) for h in range(4)]
    hbuf, hbuf_r = kb.sb([64, 1024], BF16, "hbuf")
    hgT, hgT_r = kb.sb([128, 8, T], BF16, "hgT")
    sm1, sm1_r = kb.sb([64, 64], F32, "sm1")
    asb, asb_r = [], []
    for i in range(3):
        a_, r_ = kb.sb([64, 4, 257], BF16, "asb%d" % i)
        asb.append(a_); asb_r.append(r_)
    sm1s = [kb.sb([64, 32], F32, "sm1s%d" % i) for i in range(3)]
    bsts = [kb.sb([64, 4, 6], F32, "bsts%d" % i) for i in range(3)]
    hbufs = [(hbuf, hbuf_r), kb.sb([64, 1024], BF16, "hbuf2")]
    bst, bst_r = kb.sb([64, 4, 6], F32, "bst")
    x1T, x1T_r = qT, qT_r
    hff, hff_r = kb.sb([128, 16, T], BF16, "hff")
    rl = t1
    lsm, lsm_r = kb.sb([128, 16], F32, "lsm")
    lst, lst_r = kb.sb([128, 2, 2, 6], F32, "lst")
    xtok_rt = [Res("xtok_t0"), Res("xtok_t1")]
    lsm_rt = [Res("lsm0"), Res("lsm1")]
    lst_rt = [Res("lst0"), Res("lst1")]
    cch, cch_r = kb.sb([30, 1024], F32, "cch")
    cout, cout_r = kb.sb([30, 1024], F32, "cout")
    s_cch = Stream(nc, "cch")
    s_st = [Stream(nc, "state%d" % h) for h in range(4)]
    s_y = Stream(nc, "ystore")
    for r_ in (sigz_r, sga_r, qT_r, kT_r, aT_r, hff_r, dw_r, mrgb_r, ktok_r, wvm_r, hgT_r, sog_r, vext_r, glu32_r, cout_r):
        r_.dj = True
    kb.op("dve", lambda e: e.memset(vext, 1.0), writes=[vext_r])
    kb.op("dve", lambda e: e.memset(glu, 0.0), writes=[glu_r])

    blk_groups = [2, 3, 0, 1, G_Q, G_Q + 1, G_K, G_K + 1, G_V, G_V + 1, G_O, G_O + 1, G_GB, G_GB + 1,
                  G_GA, G_GA + 1, G_A, G_A + 1, G_B, G_B + 1, G_OUT, G_OUT + 1] + [G_F1 + i for i in range(4)] + [G_F2 + i for i in range(4)] + [G_F1 + 4 + i for i in range(4)] + [G_F2 + 4 + i for i in range(4)]
    NBLK = NMAIN // T + 1
    sched = blk_groups * NBLK
    nxt = [0, 0]

    def ring_use():
        i = nxt[1]
        while nxt[0] < len(sched) and nxt[0] <= i + NW - 1:
            j = nxt[0]
            s = j % NW
            kb.dma("sp", s_ring[s], ring[s], wsc[sched[j]].rearrange("p (k n) -> p k n", k=8), reads=[R_wsc[sched[j]]], writes=[ring_r[s]])
            nxt[0] += 1
        nxt[1] += 1
        return ring[i % NW], ring_r[i % NW]

    bankrr = [0]

    def nbank():
        b = bankrr[0] % 4
        bankrr[0] += 1
        return b

    def proj_fm(wt, wt_r, act, act_r, n, evac, banks=None):
        for ocl in range(4):
            b = nbank() if banks is None else banks[ocl % len(banks)]
            for k in range(8):
                kb.op("pe", lambda e, k=k, b=b, ocl=ocl: e.matmul(pb[b][:, 0:n], lhsT=wt[:, k, ocl * 128:(ocl + 1) * 128], rhs=act[:, k, 0:n],
                                                                  start=(k == 0), stop=(k == 7)),
                      reads=[wt_r, act_r], writes=[pbr[b]])
            evac(ocl, pb[b][:, 0:n], pbr[b])

    def proj_tm(wt, wt_r, act, act_r, ntile, evac):
        for tt in range(ntile):
            b = nbank()
            for k in range(8):
                kb.op("pe", lambda e, k=k, b=b, tt=tt: e.matmul(pb[b][:, 0:512], lhsT=act[:, k, tt * 128:(tt + 1) * 128], rhs=wt[:, k, :],
                                                                start=(k == 0), stop=(k == 7)),
                      reads=[wt_r, act_r], writes=[pbr[b]])
            evac(tt, pb[b][:, 0:512], pbr[b])

    xw_loaded = [0]

    def load_xm(blk):
        i = blk % 2
        if blk < NMAIN // T:
            src = xTs[:, :, NPRE + blk * T:NPRE + (blk + 1) * T]
        else:
            src = xTm[:, :, 0:T]
        kb.dma("pool", s_xm[i], xTm_b[i], src, reads=[R_in], writes=[xTm_r[i]])

    load_xm(0)
    for h in range(4):
        kb.op("act", lambda e, h=h: e.activation(out=Cb[h], in_=Cst[h], func=AF.Identity), reads=Cst_rd[h], writes=[Cb_r[h]])

    for blk in range(NBLK):
        sample = blk == NBLK - 1
        nseg, L = (4, 64) if sample else (1, T)
        xt, xt_r = xTm_b[blk % 2], xTm_r[blk % 2]
        gl = glu[:, :, 0:nseg * (30 + L)].rearrange("p c (s l) -> p c s l", s=nseg)
        tok_lo = blk * T
        last_p = blk == NBLK - 2
        kb.dma("sp", s_xt, xtok, x_tok[tok_lo:tok_lo + T, :].rearrange("(t p) f -> p t f", p=128), reads=[R_in], writes=xtok_rt)

        for gi in (2, 3, 0, 1):
            wt, wt_r = ring_use()
            if gi >= 2:
                def ev(ocl, ps, ps_r, gi=gi):
                    c = (gi - 2) * 4 + ocl
                    kb.op("act", lambda e: e.activation(out=sigz[:, c, :], in_=ps, func=AF.Sigmoid), reads=[ps_r], writes=[sigz_r])
                proj_fm(wt, wt_r, xt, xt_r, T, ev)
                if blk == 0:
                    def evh(ocl, ps, ps_r, gi=gi):
                        c = (gi - 2) * 4 + ocl
                        kb.op("act", lambda e: e.activation(out=sigzh[:, c, :], in_=ps, func=AF.Sigmoid), reads=[ps_r], writes=[sigzh_r])
                    proj_fm(wt, wt_r, xTh, xTh_r, 128, evh)
            else:
                def ev(ocl, ps, ps_r, gi=gi):
                    c = gi * 4 + ocl
                    kb.op("dve", lambda e: e.tensor_tensor(out=gl[:, c, :, 30:30 + L], in0=ps.rearrange("p (s l) -> p s l", s=nseg),
                                                           in1=sigz[:, c, :].rearrange("p (s l) -> p s l", s=nseg), op=ALU.mult),
                          reads=[ps_r, sigz_r], writes=[glu_r])
                    if sample or last_p:
                        kb.op("dve", lambda e: e.tensor_tensor(out=glu32[:, c, 0:nseg, :],
                                                               in0=ps.rearrange("p (s l) -> p s l", s=nseg)[:, :, L - 30:L],
                                                               in1=sigz[:, c, :].rearrange("p (s l) -> p s l", s=nseg)[:, :, L - 30:L], op=ALU.mult),
                              reads=[ps_r, sigz_r], writes=[glu32_r])
                if blk == 0:
                    def evh(ocl, ps, ps_r, gi=gi):
                        c = gi * 4 + ocl
                        kb.op("dve", lambda e: e.tensor_tensor(out=gl[:, c, 0, 0:30], in0=ps[:, 98:128], in1=sigzh[:, c, 98:128], op=ALU.mult),
                              reads=[ps_r, sigzh_r], writes=[glu_r])
                    proj_fm(wt, wt_r, xTh, xTh_r, 128, evh)
                proj_fm(wt, wt_r, xt, xt_r, T, ev)
        if sample:
            for s in range(4):
                pc = pb[6][:, 0:240].rearrange("p (c r) -> p c r", r=30)
                kb.dma("sp", s_cch, cch, cache_c[s], reads=[R_in], writes=[cch_r])
                for c in range(8):
                    kb.op("pe", lambda e, s=s, c=c: e.transpose(out=pc[:, c, :], in_=cch[0:30, c * 128:(c + 1) * 128], identity=ident[0:30, 0:30]),
                          reads=[cch_r, ident_r], writes=[pbr[6]])
                kb.op("act", lambda e, s=s: e.activation(out=gl[:, :, s, 0:30], in_=pc, func=AF.Identity), reads=[pbr[6]], writes=[glu_r])
        elif blk > 0:
            kb.op("pool", lambda e: e.tensor_copy(out=gl[:, :, 0, 0:30], in_=hist), reads=[hist_r], writes=[glu_r])
        if not sample:
            kb.op("pool", lambda e: e.tensor_copy(out=hist, in_=gl[:, :, 0, L:L + 30]), reads=[glu_r], writes=[hist_r])
        if blk + 1 < NBLK:
            load_xm(blk + 1)
        if sample or last_p:
            for s in range(nseg):
                pc = pb[6][0:30, 0:512]
                pc2 = pb[5][0:30, 0:512]
                for c in range(8):
                    dst = (pc if c < 4 else pc2)[:, (c % 4) * 128:(c % 4 + 1) * 128]
                    kb.op("pe", lambda e, s=s, c=c, dst=dst: e.transpose(out=dst, in_=glu32[:, c, s, :], identity=ident),
                          reads=[glu32_r, ident_r], writes=[pbr[6], pbr[5]])
                kb.op("act", lambda e: e.activation(out=cout[:, 0:512], in_=pc, func=AF.Identity), reads=[pbr[6]], writes=[cout_r])
                kb.op("act", lambda e: e.activation(out=cout[:, 512:1024], in_=pc2, func=AF.Identity), reads=[pbr[5]], writes=[cout_r])
                kb.dma("sp", s_out, conv_s[s] if sample else conv_p, cout, reads=[cout_r], writes=[R_out])

        tile_g0 = (NPRE + blk * T) // 128 if not sample else 64
        for gi in range(2):
            wt, wt_r = ring_use()
            def ev(ocl, ps, ps_r, gi=gi):
                kb.op("act", lambda e: e.activation(out=qT[:, gi * 4 + ocl, :], in_=ps, func=AF.Identity), reads=[ps_r], writes=[qT_r])
            proj_fm(wt, wt_r, xt, xt_r, T, ev)
        for gi in range(2):
            wt, wt_r = ring_use()
            def ev(ocl, ps, ps_r, gi=gi):
                kb.op("act", lambda e: e.activation(out=kT[:, gi * 4 + ocl, :], in_=ps, func=AF.Identity, scale=0.0625), reads=[ps_r], writes=[kT_r])
            proj_fm(wt, wt_r, xt, xt_r, T, ev)
            def evt_(tt, ps, ps_r, gi=gi):
                kb.op("dve", lambda e: e.tensor_scalar(out=ktok[:, tt, gi * 512:(gi + 1) * 512], in0=ps, scalar1=0.0625, scalar2=None, op0=ALU.mult),
                      reads=[ps_r], writes=[ktok_r])
            proj_tm(wt, wt_r, xt, xt_r, T // 128, evt_)
        for gi in range(2):
            wt, wt_r = ring_use()
            def evt_(tt, ps, ps_r, gi=gi):
                p3 = ps.rearrange("p (h v) -> p h v", h=2)
                kb.op("act", lambda e: e.activation(out=vext[:, tt, 2 * gi:2 * gi + 2, 0:256], in_=p3, func=AF.Identity), reads=[ps_r], writes=[vext_r])
                kb.op("dve", lambda e: e.tensor_tensor(out=wvm[:, tt, 2 * gi:2 * gi + 2, 0:256], in0=p3,
                                                       in1=tokE[:, tile_g0 + tt, 1, 2 * gi:2 * gi + 2].unsqueeze(2).to_broadcast([128, 2, 256]), op=ALU.mult),
                      reads=[ps_r, tokE_r], writes=[wvm_r])
            proj_tm(wt, wt_r, xt, xt_r, T // 128, evt_)
        for tt in range(T // 128):
            kb.op("dve", lambda e, tt=tt: e.tensor_copy(out=wvm[:, tt, :, 256], in_=tokE[:, tile_g0 + tt, 1, :]), reads=[tokE_r], writes=[wvm_r])
        for gi in range(2):
            wt, wt_r = ring_use()
            def ev(ocl, ps, ps_r, gi=gi):
                oc = gi * 4 + ocl
                kb.op("act", lambda e: e.activation(out=sog[:, oc, :], in_=ps, func=AF.Sigmoid), reads=[ps_r], writes=[sog_r])
                kb.op("dve", lambda e: e.tensor_scalar(out=sog[:, oc, :], in0=sog[:, oc, :], scalar1=pv[:, 24 + oc:25 + oc], scalar2=None, op0=ALU.mult),
                      reads=[sog_r, pv_r], writes=[sog_r])
            proj_fm(wt, wt_r, xt, xt_r, T, ev)
        for gi in range(2):
            wt, wt_r = ring_use()
            def ev(ocl, ps, ps_r, gi=gi):
                kb.op("act", lambda e: e.activation(out=sgb[:, gi * 4 + ocl, :], in_=ps, func=AF.Sigmoid), reads=[ps_r], writes=[sgb_r])
            proj_fm(wt, wt_r, xt, xt_r, T, ev)

        pend = None
        for c in range(8):
            def build_diag(c2):
                dg, dg_r = diags[c2 % 2]
                kb.op("dve" if c2 % 2 == 0 else "pool", lambda e: e.tensor_tensor(out=dg, in0=identb.unsqueeze(1).to_broadcast([128, 31, 128]),
                                                       in1=pv[:, 32 + c2 * 31:32 + (c2 + 1) * 31].unsqueeze(2).to_broadcast([128, 31, 128]),
                                                       op=ALU.mult),
                      reads=[identb_r, pv_r], writes=[dg_r])
            diag, diag_r = diags[c % 2]
            if c == 0:
                build_diag(0)
            b = nbank()
            po = pb[b][:, 0:T].rearrange("p (s l) -> p s l", s=nseg)
            for j in range(31):
                kb.op("pe", lambda e, j=j, c=c, po=po, diag=diag: e.matmul(po, lhsT=diag[:, j, :], rhs=gl[:, c, :, j:j + L], start=(j == 0), stop=(j == 30)),
                      reads=[diag_r, glu_r], writes=[pbr[b]])
            if c + 1 < 8:
                build_diag(c + 1)
            i2 = c % 2
            kb.op("dve", lambda e, c=c, b=b: e.tensor_scalar(out=dw[:, c, :], in0=pb[b][:, 0:T], scalar1=pv[:, c:c + 1], scalar2=None, op0=ALU.add),
                  reads=[pbr[b], pv_r], writes=[dw_r])
            kb.op("act", lambda e, c=c, b=b, i2=i2: e.activation(out=sq[i2][0], in_=pb[b][:, 0:T], func=AF.Square, bias=pv[:, c:c + 1]),
                  reads=[pbr[b], pv_r], writes=[sq[i2][1]])
            kb.op("act", lambda e, c=c, b=b, i2=i2: e.activation(out=dwb[i2][0], in_=pb[b][:, 0:T], func=AF.Identity, bias=pv[:, c:c + 1]),
                  reads=[pbr[b], pv_r], writes=[dwb[i2][1]])

            def stats(c=c, i2=i2):
                kb.op("pe", lambda e: e.matmul(pb[4][:, 0:T], lhsT=onesb, rhs=dwb[i2][0], start=(c == 0), stop=(c == 7)),
                      reads=[onesb_r, dwb[i2][1]], writes=[pbr[4]])
                kb.op("pe", lambda e: e.matmul(pb[5][:, 0:T], lhsT=onesb, rhs=sq[i2][0], start=(c == 0), stop=(c == 7)),
                      reads=[onesb_r, sq[i2][1]], writes=[pbr[5]])
            if pend is not None:
                pend()
            pend = stats
        pend()
        def ln_finalize_a():
            kb.op("dve", lambda e: e.tensor_copy(out=mean_sb, in_=pb[4][:, 0:T]), reads=[pbr[4]], writes=[mean_r])
            kb.op("pool", lambda e: e.tensor_tensor(out=t1[0][0], in0=mean_sb, in1=mean_sb, op=ALU.mult), reads=[mean_r], writes=[t1[0][1]])
            kb.op("dve", lambda e: e.tensor_tensor(out=rstd, in0=pb[5][:, 0:T], in1=t1[0][0], op=ALU.subtract), reads=[pbr[5], t1[0][1]], writes=[rstd_r])
        def ln_finalize_b():
            kb.op("dve", lambda e: e.tensor_scalar(out=rstd, in0=rstd, scalar1=0.0, scalar2=EPS, op0=ALU.max, op1=ALU.add), reads=[rstd_r], writes=[rstd_r])
            kb.op("act", lambda e: e.activation(out=rstd, in_=rstd, func=AF.Sqrt), reads=[rstd_r], writes=[rstd_r])
            kb.op("dve", lambda e: e.reciprocal(out=rstd, in_=rstd), reads=[rstd_r], writes=[rstd_r])
            for c in range(8):
                i2 = c % 2
                kb.op("pool", lambda e, c=c, i2=i2: e.tensor_tensor(out=t1[i2][0], in0=dw[:, c, :], in1=mean_sb, op=ALU.subtract),
                      reads=[dw_r, mean_r], writes=[t1[i2][1]])
                kb.op("dve", lambda e, i2=i2: e.tensor_tensor(out=t2[i2][0], in0=t1[i2][0], in1=rstd, op=ALU.mult),
                      reads=[t1[i2][1], rstd_r], writes=[t2[i2][1]])
                kb.op("act", lambda e, c=c, i2=i2: e.activation(out=aT[:, c, :], in_=t2[i2][0], func=AF.Silu, scale=pv[:, 8 + c:9 + c], bias=pv[:, 16 + c:17 + c]),
                      reads=[t2[i2][1], pv_r], writes=[aT_r])
        def chunk_ids(cc):
            tt, half = cc // 2, cc % 2
            cg = (NPRE + blk * T) // 64 + cc if not sample else 128 + cc
            return tt, half, half * 64, cc * 64, cg, tile_g0 + tt

        def mlstm_crit(cc):
            tt, half, p0, tok0, cg, tg = chunk_ids(cc)
            ai = asb[cc % 3]
            ai_r = asb_r[cc % 3]
            if sample:
                for h in range(4):
                    kb.dma("sp", s_st[h], Cst[h][:, :, 0:256], sC[cc, h].rearrange("(c p) v -> p c v", p=128), reads=[R_in], writes=Cst_rd[h])
                    kb.dma("sp", s_st[h], Cst[h][:, :, 256], sn[cc, h].rearrange("(c p) -> p c", p=128), reads=[R_in], writes=Cst_rd[h],
                           allow_slow_non_contiguous=True)
                    kb.op("act", lambda e, h=h: e.activation(out=Cb[h], in_=Cst[h], func=AF.Identity), reads=Cst_rd[h], writes=[Cb_r[h]])
            if half == 0:
                for h in range(4):
                    for dkc in range(2):
                        kb.op("pe", lambda e, h=h, dkc=dkc: e.matmul(pb[4][:, h * 128:(h + 1) * 128], lhsT=kT[:, 2 * h + dkc, tt * 128:(tt + 1) * 128],
                                                                     rhs=qT[:, 2 * h + dkc, tt * 128:(tt + 1) * 128], start=(dkc == 0), stop=(dkc == 1)),
                              reads=[kT_r, qT_r], writes=[pbr[4]])
            for h in range(4):
                kb.op("dve", lambda e, h=h: e.scalar_tensor_tensor(out=pT[p0:p0 + 64, h, :], in0=pb[4][p0:p0 + 64, h * 128 + p0:h * 128 + p0 + 64],
                                                                   scalar=tokE[p0:p0 + 64, tg, 0, h:h + 1], in1=cmask[p0:p0 + 64, :],
                                                                   op0=ALU.mult, op1=ALU.mult),
                      reads=[pbr[4], tokE_r, cmask_r], writes=[pT_rh[h]])
            def upd(h):
                for dkc in range(2):
                    bi = dkc
                    kb.op("pe", lambda e, dkc=dkc, bi=bi: e.matmul(
                        pb[bi][:, 0:257], lhsT=ktok[p0:p0 + 64, tt, h * 256 + dkc * 128:h * 256 + dkc * 128 + 128],
                        rhs=wvm[p0:p0 + 64, tt, h, 0:257], start=True, stop=True),
                        reads=[ktok_r, wvm_r], writes=[pbr[bi]])
                    kb.op("dve", lambda e, dkc=dkc, bi=bi: e.scalar_tensor_tensor(
                        out=Cst[h][:, dkc, :], in0=Cst[h][:, dkc, :], scalar=decb[:, cg // 8, h, (cg % 8):(cg % 8) + 1],
                        in1=pb[bi][:, 0:257], op0=ALU.mult, op1=ALU.add),
                        reads=[Cst_rd[h][dkc], decb_r, pbr[bi]], writes=[Cst_rd[h][dkc]])

            for h in range(4):
                upd(h)
                nb = (5, 6, 3, 2)[h]
                on = pb[nb][0:64, 256:512] if h == 3 else pb[nb][0:64, 0:256]
                od = pb[2][0:64, h:h + 1]
                kb.op("pe", lambda e, h=h, on=on: e.matmul(on, lhsT=pT[p0:p0 + 64, h, :], rhs=vext[p0:p0 + 64, tt, h, 0:256], start=True, stop=False),
                      reads=[pT_rh[h], vext_r], writes=[pbr[nb]])
                for dkc in range(2):
                    kb.op("pe", lambda e, h=h, dkc=dkc, on=on: e.matmul(on, lhsT=qT[:, 2 * h + dkc, tok0:tok0 + 64], rhs=Cb[h][:, dkc, 0:256],
                                                                        start=False, stop=(dkc == 1)),
                          reads=[qT_r, Cb_r[h]], writes=[pbr[nb]])
                kb.op("pe", lambda e, h=h, od=od: e.matmul(od, lhsT=pT[p0:p0 + 64, h, :], rhs=vext[p0:p0 + 64, tt, h, 256:257], start=True, stop=False),
                      reads=[pT_rh[h], vext_r], writes=[pbr[2]])
                for dkc in range(2):
                    kb.op("pe", lambda e, h=h, dkc=dkc, od=od: e.matmul(od, lhsT=qT[:, 2 * h + dkc, tok0:tok0 + 64], rhs=Cb[h][:, dkc, 256:257],
                                                                        start=False, stop=(dkc == 1)),
                          reads=[qT_r, Cb_r[h]], writes=[pbr[2]])
                kb.op("act", lambda e, h=h, on=on: e.activation(out=ai[:, h, 0:256], in_=on, func=AF.Identity), reads=[pbr[nb]], writes=[ai_r])
                if sample:
                    kb.dma("sp", s_out, Cn_s[cc, h].rearrange("c p v -> p c v"), Cst[h], reads=Cst_rd[h], writes=[R_out])
                elif last_p and cc == T // 64 - 1:
                    kb.dma("sp", s_out, Cn_p[h].rearrange("c p v -> p c v"), Cst[h], reads=Cst_rd[h], writes=[R_out])
                if not (sample or (last_p and cc == T // 64 - 1)):
                    kb.op("act", lambda e, h=h: e.activation(out=Cb[h], in_=Cst[h], func=AF.Identity), reads=Cst_rd[h], writes=[Cb_r[h]])
            kb.op("act", lambda e: e.activation(out=ai[:, :, 256], in_=pb[2][0:64, 0:4], func=AF.Identity), reads=[pbr[2]], writes=[ai_r])

        def norm_a(cc):
            tt, half, p0, tok0, cg, tg = chunk_ids(cc)
            ai, ai_r = asb[cc % 3], asb_r[cc % 3]
            sm, sm_r, bs, bs_r = sm1s[cc % 3][0], sm1s[cc % 3][1], bsts[cc % 3][0], bsts[cc % 3][1]
            d_, d2_, vv_ = sm[:, 0:4], sm[:, 4:8], sm[:, 8:12]
            mv = sm[:, 24:32].rearrange("p (h x) -> p h x", x=2)
            for h in range(4):
                kb.op("dve", lambda e, h=h: e.bn_stats(out=bs[:, h, :], in_=ai[:, h, 0:256]), reads=[ai_r], writes=[bs_r])
                kb.op("dve", lambda e, h=h: e.bn_aggr(out=mv[:, h, :], in_=bs[:, h, :]), reads=[bs_r], writes=[sm_r])
            kb.op("dve", lambda e: e.tensor_tensor(out=d2_, in0=ai[:, :, 256], in1=ai[:, :, 256], op=ALU.mult), reads=[ai_r], writes=[sm_r])
            kb.op("dve", lambda e: e.tensor_tensor(out=d_, in0=floorC[:, cg, :], in1=floorC[:, cg, :], op=ALU.mult), reads=[sm_r, floorC_r], writes=[sm_r])
            kb.op("dve", lambda e: e.tensor_tensor(out=d2_, in0=d2_, in1=d_, op=ALU.max), reads=[sm_r], writes=[sm_r])
            kb.op("dve", lambda e: e.scalar_tensor_tensor(out=vv_, in0=d2_, scalar=EPS, in1=mv[:, :, 1], op0=ALU.mult, op1=ALU.add), reads=[sm_r], writes=[sm_r])
            kb.op("act", lambda e: e.activation(out=vv_, in_=vv_, func=AF.Sqrt), reads=[sm_r], writes=[sm_r])

        def norm_b(cc):
            ai, ai_r = asb[cc % 3], asb_r[cc % 3]
            sm, sm_r = sm1s[cc % 3][0], sm1s[cc % 3][1]
            hb, hb_r = hbufs[cc % 2]
            vv_, rs_, nm_ = sm[:, 8:12], sm[:, 12:16], sm[:, 16:20]
            mv = sm[:, 24:32].rearrange("p (h x) -> p h x", x=2)
            kb.op("dve", lambda e: e.reciprocal(out=rs_, in_=vv_), reads=[sm_r], writes=[sm_r])
            kb.op("dve", lambda e: e.scalar_tensor_tensor(out=nm_, in0=mv[:, :, 0], scalar=-1.0, in1=rs_, op0=ALU.mult, op1=ALU.mult), reads=[sm_r], writes=[sm_r])
            for h in range(4):
                kb.op("act", lambda e, h=h: e.activation(out=hb[:, h * 256:(h + 1) * 256], in_=ai[:, h, 0:256],
                                                         func=AF.Identity, scale=rs_[:, h:h + 1], bias=nm_[:, h:h + 1]),
                      reads=[ai_r, sm_r], writes=[hb_r])

        def norm_c(cc):
            tt, half, p0, tok0, cg, tg = chunk_ids(cc)
            hb, hb_r = hbufs[cc % 2]
            for fc in range(8):
                kb.op("pe", lambda e, fc=fc: e.transpose(out=pbh[:, fc * 64:(fc + 1) * 64], in_=hb[:, fc * 128:(fc + 1) * 128], identity=identb[0:64, 0:64]),
                      reads=[hb_r, identb_r], writes=[pbh_r])
            kb.op("dve", lambda e: e.tensor_tensor(out=hgT[:, :, tok0:tok0 + 64], in0=pbh[:, 0:512].rearrange("p (c t) -> p c t", t=64),
                                                   in1=sog[:, :, tok0:tok0 + 64], op=ALU.mult),
                  reads=[pbh_r, sog_r], writes=[hgT_r])

        ln_finalize_a()
        NCH = T // 64
        for it in range(NCH + 3):
            if it < NCH:
                mlstm_crit(it)
            if 0 <= it - 1 < NCH:
                norm_a(it - 1)
            if 0 <= it - 2 < NCH:
                norm_b(it - 2)
            if 0 <= it - 3 < NCH:
                norm_c(it - 3)
        ln_finalize_b()
        for gi in range(2):
            wt, wt_r = ring_use()
            def ev(ocl, ps, ps_r, gi=gi):
                kb.op("act", lambda e: e.activation(out=sga[:, gi * 4 + ocl, :], in_=ps, func=AF.Sigmoid), reads=[ps_r], writes=[sga_r])
            proj_fm(wt, wt_r, xt, xt_r, T, ev)
        for gi in range(2):
            wt, wt_r = ring_use()
            def ev(ocl, ps, ps_r, gi=gi):
                oc = gi * 4 + ocl
                kb.op("dve", lambda e: e.tensor_tensor(out=mrg[:, oc, :], in0=ps, in1=sga[:, oc, :], op=ALU.mult), reads=[ps_r, sga_r], writes=[mrg_r])
            proj_fm(wt, wt_r, aT, aT_r, T, ev)

        for gi in range(2):
            wt, wt_r = ring_use()
            def ev(ocl, ps, ps_r, gi=gi):
                oc = gi * 4 + ocl
                i2 = oc % 2
                kb.op("dve", lambda e: e.tensor_tensor(out=t2[i2][0], in0=ps, in1=sgb[:, oc, :], op=ALU.mult), reads=[ps_r, sgb_r], writes=[t2[i2][1]])
                kb.op("pool", lambda e: e.tensor_tensor(out=mrgb[:, oc, :], in0=mrg[:, oc, :], in1=t2[i2][0], op=ALU.add),
                      reads=[mrg_r, t2[i2][1]], writes=[mrgb_r])
            proj_fm(wt, wt_r, hgT, hgT_r, T, ev)

        def layernorm_tok(tt, gi, bi_):
            xs = xtok[:, tt, :]
            ls = lsm[:, tt * 8:tt * 8 + 8]
            for j in range(2):
                kb.op("dve", lambda e, j=j: e.bn_stats(out=lst[:, tt, j, :], in_=xs[:, j * 512:(j + 1) * 512]), reads=[xtok_rt[tt]], writes=[lst_rt[tt]])
            kb.op("dve", lambda e: e.bn_aggr(out=ls[:, 0:2], in_=lst[:, tt, :, :].rearrange("p a b -> p (a b)")), reads=[lst_rt[tt]], writes=[lsm_rt[tt]])
            kb.op("dve", lambda e: e.tensor_scalar(out=ls[:, 2:3], in0=ls[:, 1:2], scalar1=EPS, scalar2=None, op0=ALU.add), reads=[lsm_rt[tt]], writes=[lsm_rt[tt]])
            kb.op("act", lambda e: e.activation(out=ls[:, 2:3], in_=ls[:, 2:3], func=AF.Sqrt), reads=[lsm_rt[tt]], writes=[lsm_rt[tt]])
            yield
            kb.op("dve", lambda e: e.reciprocal(out=ls[:, 3:4], in_=ls[:, 2:3]), reads=[lsm_rt[tt]], writes=[lsm_rt[tt]])
            kb.op("dve", lambda e: e.scalar_tensor_tensor(out=ls[:, 4:5], in0=ls[:, 0:1], scalar=-1.0, in1=ls[:, 3:4], op0=ALU.mult, op1=ALU.mult),
                  reads=[lsm_rt[tt]], writes=[lsm_rt[tt]])
            kb.op("act", lambda e: e.activation(out=xs, in_=xs, func=AF.Identity, scale=ls[:, 3:4], bias=ls[:, 4:5]),
                  reads=[xtok_rt[tt], lsm_rt[tt]], writes=[xtok_rt[tt]])
            yield
            kb.op("dve", lambda e: e.tensor_tensor(out=xs, in0=xs, in1=lnbc[:, gi, :], op=ALU.mult), reads=[xtok_rt[tt], lnbc_r], writes=[xtok_rt[tt]])
            kb.op("dve", lambda e: e.tensor_tensor(out=xs, in0=xs, in1=lnbc[:, bi_, :], op=ALU.add), reads=[xtok_rt[tt], lnbc_r], writes=[xtok_rt[tt]])

        for gi in range(2):
            wt, wt_r = ring_use()
            def evt_(tt, ps, ps_r, gi=gi):
                kb.op("dve", lambda e: e.scalar_tensor_tensor(out=xtok[:, tt, gi * 512:(gi + 1) * 512], in0=xtok[:, tt, gi * 512:(gi + 1) * 512],
                                                              scalar=ALPHA, in1=ps, op0=ALU.mult, op1=ALU.add),
                      reads=[ps_r, xtok_rt[tt]], writes=[xtok_rt[tt]])
            proj_tm(wt, wt_r, mrgb, mrgb_r, T // 128, evt_)
        lng = [layernorm_tok(tt, 0, 1) for tt in range(T // 128)]
        for _st in range(2):
            for g_ in lng:
                next(g_, None)
        for tt in range(T // 128):
            next(lng[tt], None)
            for hb in range(2):
                bnk = 4 + hb
                for f4 in range(4):
                    fc = hb * 4 + f4
                    kb.op("pe", lambda e, fc=fc, f4=f4, bnk=bnk: e.transpose(out=pb[bnk][:, f4 * 128:(f4 + 1) * 128], in_=xtok[:, tt, fc * 128:(fc + 1) * 128],
                                                                             identity=ident),
                          reads=[xtok_rt[tt], ident_r], writes=[pbr[bnk]])
                kb.op("act", lambda e, hb=hb, bnk=bnk: e.activation(out=x1T[:, hb * 4:hb * 4 + 4, tt * 128:(tt + 1) * 128],
                                                                    in_=pb[bnk][:, 0:512].rearrange("p (c t) -> p c t", t=128), func=AF.Identity),
                      reads=[pbr[bnk]], writes=[x1T_r])
        for hh in range(2):
            for gi in range(4):
                wt, wt_r = ring_use()
                def ev(ocl, ps, ps_r, gi=gi):
                    oc = gi * 4 + ocl
                    i2 = oc % 2
                    kb.op("act", lambda e: e.activation(out=rl[i2][0], in_=ps, func=AF.Relu), reads=[ps_r], writes=[rl[i2][1]])
                    kb.op("pool", lambda e: e.tensor_tensor(out=hff[:, oc, :], in0=rl[i2][0], in1=rl[i2][0], op=ALU.mult), reads=[rl[i2][1]], writes=[hff_r])
                proj_fm(wt, wt_r, x1T, x1T_r, T, ev, banks=[4, 5, 6])
            for gi in range(4):
                wt, wt_r = ring_use()
                w4 = wt.rearrange("p a n -> p (a n)").rearrange("p (k n) -> p k n", k=4)
                for tt in range(T // 128):
                    for hc in range(2):
                        b = tt * 2 + hc
                        for k4 in range(4):
                            kl = gi * 4 + k4
                            kk = hh * 16 + kl
                            kb.op("pe", lambda e, k4=k4, kl=kl, kk=kk, b=b, tt=tt, hc=hc, w4=w4: e.matmul(
                                pb[b][:, 0:512], lhsT=hff[:, kl, tt * 128:(tt + 1) * 128], rhs=w4[:, k4, hc * 512:(hc + 1) * 512],
                                start=(kk == 0), stop=(kk == 31)),
                                reads=[hff_r, wt_r], writes=[pbr[b]])
        for tt in range(T // 128):
            for hc in range(2):
                b = tt * 2 + hc
                kb.op("dve", lambda e, b=b, tt=tt, hc=hc: e.scalar_tensor_tensor(
                    out=xtok[:, tt, hc * 512:(hc + 1) * 512], in0=xtok[:, tt, hc * 512:(hc + 1) * 512], scalar=ALPHA, in1=pb[b][:, 0:512],
                    op0=ALU.mult, op1=ALU.add), reads=[pbr[b], xtok_rt[tt]], writes=[xtok_rt[tt]])
        lng = [layernorm_tok(tt, 2, 3) for tt in range(T // 128)]
        for _st in range(3):
            for g_ in lng:
                next(g_, None)
        kb.dma("sp", s_y, y_d[tok_lo:tok_lo + T, :].rearrange("(t p) f -> p t f", p=128), xtok, reads=xtok_rt, writes=[R_out])

    nc.sync.wait_ge(s_out.sem, s_out.count)
    nc.sync.wait_ge(s_y.sem, s_y.count)
    return nc


_CACHE = {}


def kernel(**inp):
    f = lambda k: np.ascontiguousarray(np.asarray(inp[k], dtype=np.float32))
    xp, xs = f("x_prompt"), f("x_sample")
    cc_, sC_, sn_, sm_ = f("cache_conv")[0], f("state_C")[0], f("state_n")[0], f("state_m")[0]
    w_in, bg, w_dw, b_dw = f("w_in")[0], f("b_gate")[0], f("w_dw")[0], f("b_dw")[0]
    pv = np.zeros((128, NPV), np.float32)
    for i, v in enumerate((b_dw, f("ln_a_g")[0], f("ln_a_b")[0], f("hn_g")[0])):
        pv[:, 8 * i:8 * i + 8] = v.reshape(8, 128).T
    pv[:, 32:] = w_dw.reshape(31, 8, 128).transpose(2, 1, 0).reshape(128, 248)
    gbias = np.stack([bg[0:4], bg[4:8]], axis=1).astype(np.float32)
    lnrows = np.stack([f("ln1_g")[0], f("ln1_b")[0], f("ln2_g")[0], f("ln2_b")[0]]).astype(np.float32)
    ident = np.eye(128, dtype=np.float32)
    cm = (np.arange(64)[:, None] <= np.arange(64)[None, :]).astype(np.float32)
    cmask2 = np.concatenate([cm, cm], axis=0)
    resetm = np.ones((4, 512), np.float32)
    resetm[:, ::64] = 0
    oh = np.zeros((4, 4, 8), np.float32)
    for h in range(4):
        oh[h, h, :] = 1
    shared = {"w_in": w_in, "w_a": f("w_a_out")[0], "w_b": f("w_b_out")[0], "w_o": f("w_out")[0],
              "w_f1": f("w_ff1")[0], "w_f2": f("w_ff2")[0], "pv": pv, "gbias": gbias, "lnrows": lnrows,
              "ident": ident, "cmask2": cmask2, "resetm": resetm, "oh": oh}
    in_maps = []
    for c in range(8):
        b, q = c // 4, c % 4
        npad = NPRE - NMAIN * q
        xT_seq = np.zeros((1024, NSEQ), np.float32)
        xT_seq[:, npad:] = xp[b, :NMAIN * (q + 1)].T
        valid = np.zeros((4, NSEQ), np.float32)
        valid[:, npad:] = 1.0
        nbig = np.where(valid > 0, 0.0, -30000.0).astype(np.float32)
        xsm = xs[4 * c:4 * c + 4].reshape(NSMP, 1024)
        m = dict(shared)
        m.update({"xT_seq": xT_seq, "xT_smp": np.ascontiguousarray(xsm.T),
                  "x_tok": np.ascontiguousarray(np.concatenate([xp[b, NMAIN * q:NMAIN * (q + 1)], xsm], axis=0)),
                  "valid": valid, "nbig": nbig, "cache_c": np.ascontiguousarray(cc_[4 * c:4 * c + 4]),
                  "sC": np.ascontiguousarray(sC_[4 * c:4 * c + 4]), "sn": np.ascontiguousarray(sn_[4 * c:4 * c + 4]),
                  "sm": np.ascontiguousarray(sm_[4 * c:4 * c + 4].T)})
        in_maps.append(m)
    if inp.get("_only_maps"):
        return in_maps
    if "nc" not in _CACHE:
        _CACHE["nc"] = build()
    res = run_bass_kernel_spmd(_CACHE["nc"], in_maps, core_ids=list(range(8)))
    R = res.results
    yp = np.zeros((2, 8192, 1024), np.float32)
    ys = np.zeros((32, 64, 1024), np.float32)
    conv_p = np.zeros((1, 2, 30, 1024), np.float32)
    C_p = np.zeros((1, 2, 4, 256, 256), np.float32)
    n_p = np.zeros((1, 2, 4, 256), np.float32)
    m_p = np.zeros((1, 2, 4), np.float32)
    conv_s = np.zeros((1, 32, 30, 1024), np.float32)
    C_s = np.zeros((1, 32, 4, 256, 256), np.float32)
    n_s = np.zeros((1, 32, 4, 256), np.float32)
    m_s = np.zeros((1, 32, 4), np.float32)
    for c in range(8):
        b, q = c // 4, c % 4
        r = R[c]
        yp[b, NMAIN * q:NMAIN * (q + 1)] = r["y"][:NMAIN]
        ys[4 * c:4 * c + 4] = r["y"][NMAIN:].reshape(4, 64, 1024)
        if q == 3:
            conv_p[0, b] = r["conv_p"]
            cn = r["Cn_p"].reshape(4, 256, 257)
            C_p[0, b] = cn[:, :, :256]
            n_p[0, b] = cn[:, :, 256]
            m_p[0, b] = r["m_p"][:, 0]
        conv_s[0, 4 * c:4 * c + 4] = r["conv_s"]
        cn = r["Cn_s"].reshape(4, 4, 256, 257)
        C_s[0, 4 * c:4 * c + 4] = cn[:, :, :, :256]
        n_s[0, 4 * c:4 * c + 4] = cn[:, :, :, 256]
        m_s[0, 4 * c:4 * c + 4] = r["m_s"].T
    return (yp, ys, conv_p, C_p, n_p, m_p, conv_s, C_s, n_s, m_s)
```
